# Optimizing a Trainium2 kernel written in Bass

```python
import math
import jax
import jax.numpy as jnp
from jax import lax
import numpy as np

D_MODEL = 1024
BATCH = 8
SEQ = 2048
DEPTH = 2
DEC_BATCH = 128
DEC_SEQ = 4
PAST_LEN = 16384
PAGE_SIZE = 128

HEAD_DIM = 128
DN_HEADS = 4
DN_DK = HEAD_DIM
DN_DV = HEAD_DIM
DN_QKV = 2 * DN_HEADS * DN_DK + DN_HEADS * DN_DV
CONV_W = 4
RET_HEADS = 4
RET_DK = HEAD_DIM
RET_DV = HEAD_DIM
RET_WIDTH = RET_HEADS * RET_DV
ROPE_BASE = 10000.0
ML_HEADS = 4
ML_DK = HEAD_DIM
ML_DV = HEAD_DIM
ML_WIDTH = ML_HEADS * ML_DV
D_FF = 4 * D_MODEL
CHUNK = 64
DEEPNORM_ALPHA = (2 * DEPTH) ** 0.25
DEEPNORM_BETA = (8 * DEPTH) ** -0.25
LN_EPS = 1e-5
RMS_EPS = 1e-6
PROJ_SIZES = (DN_QKV, DN_HEADS * DN_DV, DN_HEADS, DN_HEADS,
              RET_HEADS * RET_DK, RET_HEADS * RET_DK, RET_WIDTH, RET_WIDTH,
              ML_HEADS * ML_DK, ML_HEADS * ML_DK, ML_WIDTH, ML_WIDTH, ML_HEADS, ML_HEADS,
              3 * D_MODEL)
PROJ_WIDTH = sum(PROJ_SIZES)

kernel_name = 'hybrid_deltanet_retention_mlstm_step'


def _blocks(a, n, c):
    return a.reshape(a.shape[0], n, c, *a.shape[2:]).swapaxes(2, 3)


def _unblocks(a):
    b, n, h, c, d = a.shape
    return a.swapaxes(2, 3).reshape(b, n * c, h, d)


def _chunk_axis_first(*arrs):
    return tuple(a.swapaxes(0, 1) for a in arrs)


def _rms(x):
    return x * lax.rsqrt(jnp.mean(jnp.square(x), axis=-1, keepdims=True) + RMS_EPS)


def _l2norm(x):
    return x * lax.rsqrt(jnp.sum(jnp.square(x), axis=-1, keepdims=True) + RMS_EPS)


def _layer_norm(x, g, b):
    xf = x.astype(jnp.float32)
    mu = jnp.mean(xf, axis=-1, keepdims=True)
    var = jnp.mean(jnp.square(xf - mu), axis=-1, keepdims=True)
    return ((xf - mu) * lax.rsqrt(var + LN_EPS) * g + b).astype(x.dtype)


def _rotary(x, pos0):
    T, half = x.shape[1], x.shape[-1] // 2
    inv_freq = 1.0 / (ROPE_BASE ** jnp.linspace(0.0, 1.0, half, dtype=jnp.float32))
    pos = (pos0 + jnp.arange(T)).astype(jnp.float32)
    ang = pos[:, None] * inv_freq[None, :]
    cos, sin = jnp.cos(ang)[None, :, None, :], jnp.sin(ang)[None, :, None, :]
    x1, x2 = x[..., :half], x[..., half:]
    return jnp.concatenate([x1 * cos - x2 * sin, x1 * sin + x2 * cos], axis=-1)


def gated_delta_chunked(q, k, v, beta, g, S0):
    T = q.shape[1]
    dv = v.shape[-1]
    C = math.gcd(T, CHUNK)
    N = T // C
    q, k, v = _blocks(q, N, C), _blocks(k, N, C), _blocks(v, N, C)
    beta, g = _blocks(beta, N, C), _blocks(g, N, C)
    gc = jnp.cumsum(g, axis=-1)
    idx = jnp.arange(C)
    causal = idx[:, None] >= idx[None, :]
    strict = idx[:, None] > idx[None, :]
    decay = jnp.exp(jnp.where(causal, gc[..., :, None] - gc[..., None, :], -jnp.inf))
    kb = k * beta[..., None]
    M = jnp.where(strict, jnp.einsum('bnhtk,bnhsk->bnhts', kb, k) * decay, 0.0)
    A = M + jnp.eye(C, dtype=M.dtype)
    rhs = jnp.concatenate([v * beta[..., None], kb * jnp.exp(gc)[..., None]], axis=-1)
    sol = lax.linalg.triangular_solve(A, rhs, left_side=True, lower=True, unit_diagonal=True)
    u, w = sol[..., :dv], sol[..., dv:]
    qk = jnp.einsum('bnhtk,bnhsk->bnhts', q, k) * decay

    def step(S, xs):
        q_c, k_c, u_c, w_c, qk_c, gc_c = xs
        v_new = u_c - jnp.einsum('bhtk,bhkv->bhtv', w_c, S)
        o = (jnp.einsum('bhtk,bhkv->bhtv', q_c * jnp.exp(gc_c)[..., None], S)
             + jnp.einsum('bhts,bhsv->bhtv', qk_c, v_new))
        g_last = gc_c[..., -1]
        S = (S * jnp.exp(g_last)[..., None, None]
             + jnp.einsum('bhsk,bhsv->bhkv', k_c * jnp.exp(g_last[..., None] - gc_c)[..., None], v_new))
        return S, o

    S, o = lax.scan(step, S0, _chunk_axis_first(q, k, u, w, qk, gc))
    return _unblocks(o.swapaxes(0, 1)), S


def retention_chunked(q, k, v, R0):
    T, H = q.shape[1], q.shape[2]
    C = math.gcd(T, CHUNK)
    N = T // C
    q, k, v = _blocks(q, N, C), _blocks(k, N, C), _blocks(v, N, C)
    lg = jnp.log(1.0 - 2.0 ** (-5.0 - jnp.arange(H, dtype=jnp.float32)))
    pos = jnp.arange(C, dtype=jnp.float32)
    causal = pos[:, None] >= pos[None, :]
    dmask = jnp.exp(jnp.where(causal, (pos[:, None] - pos[None, :]) * lg[:, None, None], -jnp.inf))
    o_intra = jnp.einsum('bnhts,bnhsv->bnhtv', jnp.einsum('bnhtk,bnhsk->bnhts', q, k) * dmask, v)
    q_dec = jnp.exp((pos + 1.0) * lg[:, None])
    k_dec = jnp.exp((C - 1.0 - pos) * lg[:, None])
    c_dec = jnp.exp(C * lg)

    def step(R, xs):
        q_c, k_c, v_c = xs
        o_x = jnp.einsum('bhtk,bhkv->bhtv', q_c * q_dec[..., None], R)
        R = R * c_dec[:, None, None] + jnp.einsum('bhsk,bhsv->bhkv', k_c * k_dec[..., None], v_c)
        return R, o_x

    R, o_cross = lax.scan(step, R0, _chunk_axis_first(q, k, v))
    return _unblocks(o_intra + o_cross.swapaxes(0, 1)), R


def mlstm_chunked(q, k, v, i_pre, f_pre, C0, n0, m0):
    T = q.shape[1]
    C = math.gcd(T, CHUNK)
    N = T // C
    q, k, v = _blocks(q, N, C), _blocks(k, N, C), _blocks(v, N, C)
    i_pre, f_pre = _blocks(i_pre, N, C), _blocks(f_pre, N, C)
    F = jnp.cumsum(jax.nn.log_sigmoid(f_pre), axis=-1)
    idx = jnp.arange(C)
    causal = idx[:, None] >= idx[None, :]
    logw = jnp.where(causal, F[..., :, None] - F[..., None, :] + i_pre[..., None, :], -jnp.inf)
    m_intra = F + lax.cummax(i_pre - F, axis=3)
    qk = jnp.einsum('bnhtk,bnhsk->bnhts', q, k)

    def step(carry, xs):
        Cm, nm, mm = carry
        q_c, k_c, v_c, i_c, F_c, logw_c, mi_c, qk_c = xs
        m_t = jnp.maximum(mm[..., None] + F_c, mi_c)
        w_state = jnp.exp(mm[..., None] + F_c - m_t)
        Dm = jnp.exp(logw_c - m_t[..., None]) * qk_c
        num = (w_state[..., None] * jnp.einsum('bhtk,bhkv->bhtv', q_c, Cm)
               + jnp.einsum('bhts,bhsv->bhtv', Dm, v_c))
        den = w_state * jnp.einsum('bhtk,bhk->bht', q_c, nm) + Dm.sum(-1)
        h = num / jnp.maximum(jnp.abs(den), jnp.exp(-m_t))[..., None]
        m_new = m_t[..., -1]
        F_last = F_c[..., -1]
        s_dec = jnp.exp(mm + F_last - m_new)
        kw = k_c * jnp.exp(i_c + F_last[..., None] - F_c - m_new[..., None])[..., None]
        Cm = s_dec[..., None, None] * Cm + jnp.einsum('bhsk,bhsv->bhkv', kw, v_c)
        nm = s_dec[..., None] * nm + kw.sum(-2)
        return (Cm, nm, m_new), h

    (Cm, nm, mm), h = lax.scan(step, (C0, n0, m0),
                               _chunk_axis_first(q, k, v, i_pre, F, logw, m_intra, qk))
    return _unblocks(h.swapaxes(0, 1)), Cm, nm, mm


def trunk_layer(x, pos0, conv_buf, S_dn, R_ret, C_ml, n_ml, m_ml,
                w_in, dn_conv_w, dn_A_log, dn_dt_bias, dn_norm_w, ml_i_bias, ml_f_bias, ml_norm_w,
                w_br_a, w_br_b, w_br_c, w_out, ln1_g, ln1_b, w_ff1, w_ff2, ln2_g, ln2_b):
    bsz, T, _ = x.shape
    f32 = jnp.float32

    def heads(a, h):
        return a.reshape(bsz, T, h, -1).astype(f32)

    proj = jnp.einsum('btd,de->bte', x, w_in)
    split_at = np.cumsum(PROJ_SIZES)[:-1].tolist()
    (a_qkv, a_z, a_beta, a_decay, r_q, r_k, r_v, r_g,
     m_q, m_k, m_v, m_o, m_i, m_f, gates) = jnp.split(proj, split_at, axis=-1)

    xp = jnp.concatenate([conv_buf.astype(a_qkv.dtype), a_qkv], axis=1)
    new_conv = xp[:, T:]
    conv = jax.nn.silu(sum(xp[:, j:j + T] * dn_conv_w[j] for j in range(CONV_W)).astype(f32))
    d_q, d_k, d_v = jnp.split(conv, [DN_HEADS * DN_DK, 2 * DN_HEADS * DN_DK], axis=-1)
    d_q = _l2norm(heads(d_q, DN_HEADS)) * DN_DK ** -0.5
    d_k = _l2norm(heads(d_k, DN_HEADS))
    d_v = heads(d_v, DN_HEADS)
    beta = jax.nn.sigmoid(a_beta.astype(f32))
    g = -jnp.exp(dn_A_log.astype(f32)) * jax.nn.softplus(a_decay.astype(f32) + dn_dt_bias)
    o_a, S_new = gated_delta_chunked(d_q, d_k, d_v, beta, g, S_dn.astype(f32))
    o_a = _rms(o_a) * dn_norm_w * jax.nn.silu(heads(a_z, DN_HEADS))
    o_a = o_a.reshape(bsz, T, DN_HEADS * DN_DV).astype(x.dtype)

    r_qh = _rotary(heads(r_q, RET_HEADS), pos0)
    r_kh = _rotary(heads(r_k, RET_HEADS), pos0) * RET_DK ** -0.5
    o_b, R_new = retention_chunked(r_qh, r_kh, heads(r_v, RET_HEADS), R_ret.astype(f32))
    o_b = (_rms(o_b) * jax.nn.silu(heads(r_g, RET_HEADS))).reshape(bsz, T, RET_WIDTH).astype(x.dtype)

    i_pre = m_i.astype(f32) + ml_i_bias
    f_pre = m_f.astype(f32) + ml_f_bias
    h_c, C_new, n_new, m_new = mlstm_chunked(
        heads(m_q, ML_HEADS), heads(m_k, ML_HEADS) * ML_DK ** -0.5, heads(m_v, ML_HEADS),
        i_pre, f_pre, C_ml.astype(f32), n_ml.astype(f32), m_ml.astype(f32))
    o_c = _rms(h_c).reshape(bsz, T, ML_WIDTH) * ml_norm_w * jax.nn.sigmoid(m_o.astype(f32))
    o_c = o_c.astype(x.dtype)

    g_a, g_b, g_c = jnp.split(jax.nn.sigmoid(gates.astype(f32)).astype(x.dtype), 3, axis=-1)
    merged = g_a * (o_a @ w_br_a) + g_b * (o_b @ w_br_b) + g_c * (o_c @ w_br_c)
    h = _layer_norm(DEEPNORM_ALPHA * x + merged @ w_out, ln1_g, ln1_b)
    ff = jnp.square(jax.nn.relu(h @ w_ff1)) @ w_ff2
    out = _layer_norm(DEEPNORM_ALPHA * h + ff, ln2_g, ln2_b)
    new_states = (new_conv.astype(conv_buf.dtype), S_new.astype(S_dn.dtype), R_new.astype(R_ret.dtype),
                  C_new.astype(C_ml.dtype), n_new.astype(n_ml.dtype), m_new.astype(m_ml.dtype))
    return out, new_states


def setup_inputs(seed: int = 0) -> dict:
    key = jax.random.key(seed)
    ks = iter(jax.random.split(key, 40))
    f32 = jnp.float32
    L = DEPTH
    B = DEEPNORM_BETA

    def nrm(shape, s):
        return s * jax.random.normal(next(ks), shape, f32)

    x_prompt = nrm((BATCH, SEQ, D_MODEL), 1.0)
    x_sample = nrm((DEC_BATCH, DEC_SEQ, D_MODEL), 1.0)
    state_dn_conv = nrm((L, DEC_BATCH, CONV_W - 1, DN_QKV), 1.0)
    state_dn_S = nrm((L, DEC_BATCH, DN_HEADS, DN_DK, DN_DV), 0.1)
    state_ret_R = nrm((L, DEC_BATCH, RET_HEADS, RET_DK, RET_DV), 1.0)
    state_ml_C = nrm((L, DEC_BATCH, ML_HEADS, ML_DK, ML_DV), 0.3)
    state_ml_n = nrm((L, DEC_BATCH, ML_HEADS, ML_DK), 0.3)
    state_ml_m = jax.random.uniform(next(ks), (L, DEC_BATCH, ML_HEADS), f32, 0.0, 4.0)

    segs = [(2 * DN_HEADS * DN_DK, 1.0), (DN_HEADS * DN_DV, B), (DN_HEADS * DN_DV + 2 * DN_HEADS, 1.0),
            (2 * RET_HEADS * RET_DK, 1.0), (RET_WIDTH, B), (RET_WIDTH, 1.0),
            (2 * ML_HEADS * ML_DK, 1.0), (ML_WIDTH, B), (ML_WIDTH + 2 * ML_HEADS + 3 * D_MODEL, 1.0)]
    col_scale = jnp.concatenate([jnp.full((n,), s, f32) for n, s in segs])
    w_in = nrm((L, D_MODEL, PROJ_WIDTH), D_MODEL ** -0.5) * col_scale
    dn_conv_w = nrm((L, CONV_W, DN_QKV), CONV_W ** -0.5)
    dn_A_log = jnp.log(jax.random.uniform(next(ks), (L, DN_HEADS), f32, 1.0, 16.0))
    dt = jnp.exp(jax.random.uniform(next(ks), (L, DN_HEADS), f32, math.log(1e-3), math.log(1e-1)))
    dn_dt_bias = dt + jnp.log(-jnp.expm1(-dt))
    dn_norm_w = 1.0 + nrm((L, DN_DV), 0.02)
    ml_i_bias = nrm((L, ML_HEADS), 0.1)
    ml_f_bias = jnp.linspace(3.0, 6.0, ML_HEADS, dtype=f32) + nrm((L, ML_HEADS), 0.1)
    ml_norm_w = 1.0 + nrm((L, ML_WIDTH), 0.02)
    w_br_a = nrm((L, DN_HEADS * DN_DV, D_MODEL), (DN_HEADS * DN_DV) ** -0.5 * B)
    w_br_b = nrm((L, RET_WIDTH, D_MODEL), RET_WIDTH ** -0.5 * B)
    w_br_c = nrm((L, ML_WIDTH, D_MODEL), ML_WIDTH ** -0.5 * B)
    w_out = nrm((L, D_MODEL, D_MODEL), D_MODEL ** -0.5 * B)
    ln1_g = 1.0 + nrm((L, D_MODEL), 0.02)
    ln1_b = nrm((L, D_MODEL), 0.02)
    w_ff1 = nrm((L, D_MODEL, D_FF), D_MODEL ** -0.5 * B)
    w_ff2 = nrm((L, D_FF, D_MODEL), D_FF ** -0.5 * B)
    ln2_g = 1.0 + nrm((L, D_MODEL), 0.02)
    ln2_b = nrm((L, D_MODEL), 0.02)
    return {'x_prompt': x_prompt, 'x_sample': x_sample,
            'state_dn_conv': state_dn_conv, 'state_dn_S': state_dn_S, 'state_ret_R': state_ret_R,
            'state_ml_C': state_ml_C, 'state_ml_n': state_ml_n, 'state_ml_m': state_ml_m,
            'w_in': w_in, 'dn_conv_w': dn_conv_w, 'dn_A_log': dn_A_log, 'dn_dt_bias': dn_dt_bias,
            'dn_norm_w': dn_norm_w, 'ml_i_bias': ml_i_bias, 'ml_f_bias': ml_f_bias, 'ml_norm_w': ml_norm_w,
            'w_br_a': w_br_a, 'w_br_b': w_br_b, 'w_br_c': w_br_c, 'w_out': w_out,
            'ln1_g': ln1_g, 'ln1_b': ln1_b, 'w_ff1': w_ff1, 'w_ff2': w_ff2, 'ln2_g': ln2_g, 'ln2_b': ln2_b}


def reference(x_prompt, x_sample, state_dn_conv, state_dn_S, state_ret_R, state_ml_C, state_ml_n, state_ml_m,
              w_in, dn_conv_w, dn_A_log, dn_dt_bias, dn_norm_w, ml_i_bias, ml_f_bias, ml_norm_w,
              w_br_a, w_br_b, w_br_c, w_out, ln1_g, ln1_b, w_ff1, w_ff2, ln2_g, ln2_b):
    dt = x_prompt.dtype
    hp, hs = x_prompt, x_sample
    p_states, s_states = [], []
    for l in range(DEPTH):
        weights = (w_in[l], dn_conv_w[l], dn_A_log[l], dn_dt_bias[l], dn_norm_w[l], ml_i_bias[l],
                   ml_f_bias[l], ml_norm_w[l], w_br_a[l], w_br_b[l], w_br_c[l], w_out[l],
                   ln1_g[l], ln1_b[l], w_ff1[l], w_ff2[l], ln2_g[l], ln2_b[l])
        hp, sp = trunk_layer(hp, 0,
                             jnp.zeros((BATCH, CONV_W - 1, DN_QKV), dt),
                             jnp.zeros((BATCH, DN_HEADS, DN_DK, DN_DV), dt),
                             jnp.zeros((BATCH, RET_HEADS, RET_DK, RET_DV), dt),
                             jnp.zeros((BATCH, ML_HEADS, ML_DK, ML_DV), dt),
                             jnp.zeros((BATCH, ML_HEADS, ML_DK), dt),
                             jnp.zeros((BATCH, ML_HEADS), dt),
                             *weights)
        hs, ss = trunk_layer(hs, PAST_LEN, state_dn_conv[l], state_dn_S[l], state_ret_R[l],
                             state_ml_C[l], state_ml_n[l], state_ml_m[l], *weights)
        p_states.append(sp)
        s_states.append(ss)

    def stack(states, i):
        return jnp.stack([s[i] for s in states])

    return (hp, hs,
            stack(p_states, 0), stack(p_states, 1), stack(p_states, 2),
            stack(p_states, 3), stack(p_states, 4), stack(p_states, 5),
            stack(s_states, 0), stack(s_states, 1), stack(s_states, 2),
            stack(s_states, 3), stack(s_states, 4), stack(s_states, 5))
```

```python
import os
import numpy as np
import concourse.bass as bass
import concourse.mybir as mybir
from concourse.bass_utils import run_bass_kernel_spmd

F32 = mybir.dt.float32
BF16 = mybir.dt.bfloat16
AF = mybir.ActivationFunctionType
ALU = mybir.AluOpType
AX = mybir.AxisListType

NDMASEM = 48
NCORES = 8
D = 1024
H = 4
DEPTH = 2
SEQ = 2048
PAST = 16384
NSB = 16
ST = 4
ALPHA = (2 * DEPTH) ** 0.25
LN_EPS = 1e-5
RMS_EPS = 1e-6
PW = 9232
OFF = dict(a_qkv=0, a_z=1536, a_beta=2048, a_decay=2052, r_q=2056, r_k=2568, r_v=3080, r_g=3592,
           m_q=4104, m_k=4616, m_v=5128, m_o=5640, m_i=6152, m_f=6156, gates=6160)
NTP = 256
NPASS = SEQ // NTP
BIG = 30000.0


class Dep:
    __slots__ = ("w", "r", "excl")

    def __init__(self):
        self.w = None
        self.r = []
        self.excl = False


class Op:
    __slots__ = ("eng", "fn", "deps", "sig", "cnt", "dma", "dsem", "dval", "reuse")

    def __init__(self, eng, fn, dma):
        self.eng = eng
        self.fn = fn
        self.dma = dma
        self.deps = ()
        self.sig = dma
        self.cnt = 0
        self.dsem = None
        self.dval = 0
        self.reuse = None


def _flat(xs):
    out = []
    for x in xs:
        if isinstance(x, Dep):
            out.append(x)
        elif hasattr(x, "ds"):
            out.extend(x.ds)
        else:
            out.append(x.d)
    return out


class Vw:
    def __init__(self, ap, deps):
        self.t = ap
        self.ds = deps

    def __getitem__(self, k):
        return self.t[k]


class Prog:
    ENG = ("pe", "dve", "act", "pool", "sp")

    def __init__(self, nc):
        self.nc = nc
        self.ops = []
        self.final = []
        self.tag = ""
        self.tags = []

    def op(self, eng, fn, reads=(), writes=(), dma=False):
        reads = _flat(reads)
        writes = _flat(writes)
        for d in reads:
            if d.excl and d not in writes:
                writes.append(d)
        i = len(self.ops)
        o = Op(eng, fn, dma)
        self.tags.append((eng, self.tag))
        deps = set()
        hard = set()
        for d in reads:
            if d.w is not None:
                deps.add(d.w)
                hard.add(d.w)
        for d in writes:
            if d.w is not None:
                deps.add(d.w)
                hard.add(d.w)
            deps.update(d.r)
        keep = []
        latest = {}
        for p in deps:
            po = self.ops[p]
            if po.dma:
                keep.append(p)
                continue
            if (not dma) and po.eng == eng:
                if eng == "pe" or p not in hard:
                    continue
            if latest.get(po.eng, -1) < p:
                latest[po.eng] = p
        keep.extend(latest.values())
        for p in keep:
            self.ops[p].sig = True
        o.deps = tuple(sorted(keep))
        for d in reads:
            d.r.append(i)
        for d in writes:
            d.w = i
            d.r = []
        self.ops.append(o)
        return i

    def dma(self, q, out, in_, reads=(), writes=(), final=False, **kw):
        def fn(e):
            return e.dma_start(out=out, in_=in_, **kw)
        i = self.op(q, fn, reads, writes, dma=True)
        if final:
            self.final.append(i)
        return i

    def build(self):
        nc = self.nc
        esem = {e: nc.alloc_semaphore("sem_" + e) for e in self.ENG}
        dsems = [nc.alloc_semaphore("dsem%d" % i) for i in range(NDMASEM)]
        cnt = {e: 0 for e in self.ENG}
        slot_used = [False] * NDMASEM
        slot_val = [0] * NDMASEM
        NSW = 20
        kq = {"sw": 0, "hw": 0}
        for o in self.ops:
            if o.dma:
                if o.eng == "pool":
                    s = kq["sw"] % NSW
                    kq["sw"] += 1
                else:
                    s = NSW + kq["hw"] % (NDMASEM - NSW)
                    kq["hw"] += 1
                o.dsem = dsems[s]
                if slot_used[s]:
                    o.reuse = (dsems[s], slot_val[s])
                slot_used[s] = True
                slot_val[s] += 16
                o.dval = slot_val[s]
            elif o.sig:
                cnt[o.eng] += 1
                o.cnt = cnt[o.eng]
        ops = self.ops
        finals = [(ops[i].dsem, ops[i].dval) for i in self.final]
        if os.environ.get("KDBG_DUMP"):
            import json
            json.dump([(o.eng, o.cnt, self.tags[i][1]) for i, o in enumerate(ops) if o.sig and not o.dma],
                      open(os.environ["KDBG_DUMP"] + ".cnt", "w"))

        def emit(ename):
            def body(e):
                waited = {}

                def wait(sem, val):
                    key = id(sem)
                    if waited.get(key, 0) >= val:
                        return
                    waited[key] = val
                    e.wait_ge(sem, val)

                for o in ops:
                    if o.eng != ename:
                        continue
                    for p in o.deps:
                        po = ops[p]
                        if po.dma:
                            wait(po.dsem, po.dval)
                        else:
                            wait(esem[po.eng], po.cnt)
                    if o.reuse is not None:
                        wait(*o.reuse)
                    inst = o.fn(e)
                    if o.dma:
                        inst.then_inc(o.dsem, 16)
                    elif o.sig:
                        inst.then_inc(esem[ename], 1)
                if ename == "pool":
                    for (s, v) in finals:
                        wait(s, v)
            return body

        with nc.Block() as block:
            block.tensor(emit("pe"))
            block.vector(emit("dve"))
            block.scalar(emit("act"))
            block.gpsimd(emit("pool"))
            block.sync(emit("sp"))


class Tl:
    def __init__(self, nc, name, shape, dt=F32, psum=False):
        if psum:
            self.t = nc.alloc_psum_tensor(name, list(shape), dt)
        else:
            self.t = nc.alloc_sbuf_tensor(name, list(shape), dt)
        self.d = Dep()
        self.d.excl = psum

    def __getitem__(self, k):
        return self.t[k]


class DT_:
    def __init__(self, ap):
        self.ap = ap
        self.d = Dep()


def _host_consts():
    c = {}
    f = np.float32
    idx = np.arange(128)
    c["identf"] = np.eye(128, dtype=f)
    c["onesf"] = np.ones((128, 128), f)
    for m, C, blk in (("p", 128, 128), ("s", 64, 4)):
        i = np.arange(C)
        b = i // blk
        same = (b[:, None] == b[None, :])
        c["U_" + m] = (same & (i[:, None] <= i[None, :])).astype(f)
        c["L_" + m] = (same & (i[:, None] > i[None, :])).astype(f)
        last = (b + 1) * blk - 1
        c["lastsel_" + m] = (i[:, None] == last[None, :]).astype(f)
        nb = C // blk
        c["lastind_" + m] = (i[:, None] == (np.arange(nb)[None, :] + 1) * blk - 1).astype(f)
        c["blkind_" + m] = (b[:, None] == np.arange(nb)[None, :]).astype(f)
        c["strictT_" + m] = (same & (i[:, None] < i[None, :])).astype(f)
        c["cmT_" + m] = (same & (i[:, None] <= i[None, :])).astype(f)
        c["maskb_" + m] = np.where(same & (i[None, :] <= i[:, None]), 0.0, -BIG).astype(f)
        c["maskbT_" + m] = np.where(same & (i[:, None] <= i[None, :]), 0.0, -BIG).astype(f)
    cm = np.zeros((128, NSB, 64), f)
    for bb in range(NSB):
        cm[:, bb, bb * ST:(bb + 1) * ST] = 1.0
    c["colmask"] = cm.reshape(128, NSB * 64)
    half = 64
    inv_freq = (1.0 / (np.float32(10000.0) ** np.linspace(0.0, 1.0, half, dtype=f))).astype(f)
    lg = np.log(1.0 - 2.0 ** (-5.0 - np.arange(H, dtype=np.float64)))
    tab = np.zeros((17, 128, 4, H, half), f)
    for ci in range(17):
        if ci < 16:
            pos = (ci * 128 + idx).astype(f)
            tb = idx.astype(np.float64)
        else:
            pos = np.zeros(128, f)
            tb = np.zeros(128)
            pos[:64] = (PAST + (np.arange(64) % ST)).astype(f)
            tb[:64] = (np.arange(64) % ST)
        ang = (pos[:, None] * inv_freq[None, :]).astype(f)
        cs, sn = np.cos(ang.astype(np.float64)), np.sin(ang.astype(np.float64))
        for h in range(H):
            qs = np.exp((tb + 1.0) * lg[h])[:, None]
            ks = np.exp(-(tb + 1.0) * lg[h])[:, None] * (128.0 ** -0.5)
            tab[ci, :, 0, h] = cs * qs
            tab[ci, :, 1, h] = sn * qs
            tab[ci, :, 2, h] = cs * ks
            tab[ci, :, 3, h] = sn * ks
    c["rot"] = tab.reshape(17, 128, 4 * H * half)
    return c


_F32C = ["identf", "onesf", "U_p", "L_p", "lastsel_p", "lastind_p", "blkind_p", "strictT_p", "cmT_p",
         "U_s", "L_s", "lastsel_s", "lastind_s", "blkind_s", "strictT_s", "cmT_s"]
_BF16C = ["maskb_p", "maskbT_p", "maskb_s", "maskbT_s", "colmask", "identf", "onesf", "blkind_s"]


def build_program(consts):
    nc = bass.Bass("TRN2", target_bir_lowering=False)
    P = Prog(nc)
    uid = [0]

    def nm(s):
        uid[0] += 1
        return "%s_%d" % (s, uid[0])

    def din(name, shape, dt=F32):
        return nc.dram_tensor(name, list(shape), dt, kind="ExternalInput").ap()

    def dout(name, shape):
        return nc.dram_tensor(name, list(shape), F32, kind="ExternalOutput").ap()

    def dscr(name, shape, dt):
        return nc.dram_tensor(name, list(shape), dt, kind="Internal").ap()

    def T(name, shape, dt=F32):
        n_ = nm(name)
        _REG.setdefault(name, []).append(n_)
        return Tl(nc, n_, shape, dt)

    xp = din("xp", [SEQ, D])
    xs = din("xs", [NSB * ST, D])
    cconv = din("cconv", [DEPTH, NSB * 3, 1536])
    sS = din("sS", [DEPTH, NSB, H, 128, 128])
    sR = din("sR", [DEPTH, NSB, H, 128, 128])
    sC = din("sC", [DEPTH, NSB, H, 128, 128])
    sN = din("sN", [DEPTH, NSB, H, 128])
    sM = din("sM", [DEPTH, NSB, H])
    w_in = din("w_in", [DEPTH, D, PW])
    conv_w = din("conv_w", [DEPTH, 4, 1536])
    A_log = din("A_log", [DEPTH, H])
    dt_bias = din("dt_bias", [DEPTH, H])
    dn_nw = din("dn_nw", [DEPTH, 128])
    ibias = din("ibias", [DEPTH, H])
    fbias = din("fbias", [DEPTH, H])
    ml_nw = din("ml_nw", [DEPTH, 512])
    w_br = [din("w_br_" + s, [DEPTH, 512, D]) for s in "abc"]
    w_out = din("w_out", [DEPTH, D, D])
    ln1g = din("ln1g", [DEPTH, D]); ln1b = din("ln1b", [DEPTH, D])
    ln2g = din("ln2g", [DEPTH, D]); ln2b = din("ln2b", [DEPTH, D])
    w_ff1 = din("w_ff1", [DEPTH, D, 4 * D])
    w_ff2 = din("w_ff2", [DEPTH, 4 * D, D])
    cin = {k: din("c_" + k, list(consts[k].shape)) for k in consts}

    y_p = dout("y_p", [SEQ, D])
    y_s = dout("y_s", [NSB * ST, D])
    o_pconv = dout("o_pconv", [DEPTH, 3, 1536])
    o_pS = dout("o_pS", [DEPTH, H, 128, 128])
    o_pR = dout("o_pR", [DEPTH, H, 128, 128])
    o_pC = dout("o_pC", [DEPTH, H, 128, 128])
    o_pN = dout("o_pN", [DEPTH, H, 128])
    o_pM = dout("o_pM", [DEPTH, H])
    o_sconv = dout("o_sconv", [DEPTH, NSB * 3, 1536])
    o_sS = dout("o_sS", [DEPTH, NSB, H, 128, 128])
    o_sR = dout("o_sR", [DEPTH, NSB, H, 128, 128])
    o_sC = dout("o_sC", [DEPTH, NSB, H, 128, 128])
    o_sN = dout("o_sN", [DEPTH, NSB * H, 128])
    o_sM = dout("o_sM", [DEPTH, NSB * H])

    xmid = DT_(dscr("xmid", [SEQ + NSB * ST, D], F32))
    xmid_d = [Dep() for _ in range(17)]

    wsrc, wscr = {}, {}
    for l in range(DEPTH):
        for key, src in ((("in", l), w_in[l]), (("br0", l), w_br[0][l]), (("br1", l), w_br[1][l]),
                         (("br2", l), w_br[2][l]), (("out", l), w_out[l]), (("ff1", l), w_ff1[l]), (("ff2", l), w_ff2[l])):
            wsrc[key] = src

    def block_plan(l):
        bl = []
        for blk in range(3):
            bl.append((("in", l), 0, 8, OFF["a_qkv"] + blk * 512, 512))
        for nm_, w_ in (("m_q", 512), ("m_k", 512), ("a_z", 512), ("a_beta", 8), ("r_q", 512), ("r_k", 512), ("r_v", 512),
                        ("r_g", 512), ("m_v", 512), ("m_o", 512), ("m_i", 8)):
            bl.append((("in", l), 0, 8, OFF[nm_], w_))
        for br in range(3):
            bl.append((("br%d" % br, l), 0, 4, 0, D))
            for half in range(2):
                bl.append((("in", l), 0, 8, OFF["gates"] + br * D + half * 512, 512))
        for half in range(2):
            bl.append((("out", l), 0, 8, half * 512, 512))
        for hb in range(8):
            bl.append((("ff1", l), 0, 8, hb * 512, 512))
        for half in range(2):
            for q in range(4):
                bl.append((("ff2", l), q * 8, 8, half * 512, 512))
        return bl

    plans = [block_plan(l) for l in range(DEPTH)]
    plan_idx = [{sp: i for i, sp in enumerate(plans[l])} for l in range(DEPTH)]
    cast_dep = {}
    cast_cur = [0] * DEPTH
    LOOKAHEAD = 6

    def emit_cast(l):
        if cast_cur[l] >= len(plans[l]):
            return False
        sp = plans[l][cast_cur[l]]
        cast_cur[l] += 1
        key, k0, kc, c0, ncols = sp
        d = Dep()
        scr = dscr("wsc_%d_%d" % (l, cast_cur[l]), [128, kc * ncols], BF16)
        wscr[sp] = scr
        P.dma("pool", scr.rearrange("p (k e) -> p k e", k=kc),
              wsrc[key].rearrange("(k p) e -> p k e", p=128)[:, k0:k0 + kc, c0:c0 + ncols], writes=[d])
        cast_dep[sp] = d
        return True

    K = {}
    for k in _F32C:
        sh = consts[k].shape
        K[k] = T("k_" + k, list(sh))
        P.dma("sp", K[k][:], cin[k], writes=[K[k]])
    KB = {}
    for k in _BF16C:
        sh = consts[k].shape
        KB[k] = T("kb_" + k, list(sh), BF16)
        P.dma("pool", KB[k][:], cin[k], writes=[KB[k]])
    identf, onesf = K["identf"], K["onesf"]
    identb, onesb = KB["identf"], KB["onesf"]

    PS = [Tl(nc, "psb%d" % i, [128, 512], F32, psum=True) for i in range(8)]
    rr = {"d": 0, "m": 0}

    def pd():
        rr["d"] = (rr["d"] + 1) % 4
        return PS[rr["d"]]

    def pm():
        rr["m"] = (rr["m"] + 1) % 4
        return PS[4 + rr["m"]]

    PACC = [PS[2], PS[3]]

    def mm(out, lhsT, rhs, R, W, start=True, stop=True):
        P.op("pe", lambda e: e.matmul(out, lhsT=lhsT, rhs=rhs, start=start, stop=stop, skip_group_check=True),
             reads=R, writes=W)

    def tr(out, in_, ident, R, W):
        P.op("pe", lambda e: e.transpose(out, in_, ident), reads=R, writes=W)

    def act(out, in_, func, R, W, **kw):
        P.op("act", lambda e: e.activation(out=out, in_=in_, func=func, **kw), reads=R, writes=W)

    def cp(eng, out, in_, R, W):
        if eng == "act":
            P.op("act", lambda e: e.copy(out=out, in_=in_), reads=R, writes=W)
        else:
            P.op(eng, lambda e: e.tensor_copy(out=out, in_=in_), reads=R, writes=W)

    def tt(eng, out, in0, in1, op, R, W):
        P.op(eng, lambda e: e.tensor_tensor(out=out, in0=in0, in1=in1, op=op), reads=R, writes=W)

    def ts(eng, out, in0, s1, op0, R, W, s2=None, op1=None):
        if op1 is None:
            P.op(eng, lambda e: e.tensor_scalar(out=out, in0=in0, scalar1=s1, scalar2=None, op0=op0), reads=R, writes=W)
        else:
            P.op(eng, lambda e: e.tensor_scalar(out=out, in0=in0, scalar1=s1, scalar2=s2, op0=op0, op1=op1),
                 reads=R, writes=W)

    def stt(out, in0, scalar, in1, op0, op1, R, W, accum=None):
        if accum is None:
            P.op("dve", lambda e: e.scalar_tensor_tensor(out=out, in0=in0, scalar=scalar, in1=in1, op0=op0, op1=op1),
                 reads=R, writes=W)
        else:
            P.op("dve", lambda e: e.scalar_tensor_tensor(out=out, in0=in0, scalar=scalar, in1=in1, op0=op0, op1=op1,
                                                         accum_out=accum), reads=R, writes=W)

    def memset(eng, ap, val, W):
        P.op(eng, lambda e: e.memset(ap, val), writes=W)

    NWB = 4
    wring = [T("wring", [128, 4096], BF16) for _ in range(NWB)]
    wri = [0]

    def getw_rows(key, k0, kc, c0, ncols):
        sp = (key, k0, kc, c0, ncols)
        l = key[1]
        want = plan_idx[l][sp] + 1 + LOOKAHEAD
        while cast_cur[l] < min(want, len(plans[l])):
            emit_cast(l)
        if l + 1 < DEPTH and cast_cur[l] >= len(plans[l]):
            emit_cast(l + 1)
        buf = wring[wri[0] % NWB]
        wri[0] += 1
        view = buf.t[:, 0:kc * ncols].rearrange("p (k e) -> p k e", k=kc)
        P.dma("sp", buf.t[:, 0:kc * ncols], wscr[sp], reads=[cast_dep[sp]], writes=[buf])
        return buf, view

    def getw(key, kc, c0, ncols):
        return getw_rows(key, 0, kc, c0, ncols)

    xt = [T("xt", [128, D]) for _ in range(2)]
    xT = T("xT", [128, 8, NTP], BF16)
    big1 = T("big1", [128, 4096])
    dbig = [big1.d, Dep()]
    pre_c = [Dep() for _ in range(12)]
    pre = Vw(big1.t[:, 0:12 * (NTP + 3)].rearrange("p (c n) -> p c n", c=12), dbig + pre_c)
    actT = Vw(big1.t[:, :].bitcast(BF16).rearrange("p (k n) -> p k n", k=32), dbig + pre_c)
    cvt = [T("cvt", [128, NTP]) for _ in range(2)]
    slt = [T("slt", [128, NTP]) for _ in range(2)]
    sqt = [T("sqt", [128, NTP], BF16) for _ in range(2)]
    rnt = [T("rnt", [128, NTP]) for _ in range(2)]
    dqT = T("dqT", [128, H, NTP], BF16)
    dkT = T("dkT", [128, H, NTP], BF16)
    dvT = T("dvT", [128, H, NTP], BF16)
    mqT = T("mqT", [128, H, NTP], BF16)
    mkT = T("mkT", [128, H, NTP], BF16)
    zs = [T("zs", [128, 512], BF16) for _ in range(2)]
    gsl = [T("gsl", [128, 512], BF16) for _ in range(2)]
    osg = [T("osg", [128, 512], BF16) for _ in range(2)]
    rq_tm = [T("rq_tm", [128, H, 128], BF16) for _ in range(2)]
    rk_tm = [T("rk_tm", [128, H, 128], BF16) for _ in range(2)]
    rv_tm = [T("rv_tm", [128, H, 128], BF16) for _ in range(2)]
    vaug = [T("vaug", [128, H, 129], BF16) for _ in range(2)]
    smt = [T("smt", [128, 16]) for _ in range(2)]
    rott = [T("rott", [128, 4, H, 64]) for _ in range(2)]
    oT = [T("oT%d" % i, [128, H, NTP], BF16) for i in range(3)]
    mergedT = T("mergedT", [128, 8, NTP], BF16)
    hp = nc.alloc_sbuf_tensor("hp", [128, 2064], F32)
    hTt = nc.alloc_sbuf_tensor("hTt", [128, 2064], BF16)
    dhp = [Dep(), Dep()]
    dhT = Dep()
    hpre = [Vw(hp[:, jj * D:(jj + 1) * D], [dhp[jj]]) for jj in range(2)]
    macc = Vw(hp[:, 0:8 * NTP].rearrange("p (d n) -> p d n", d=8), dhp)
    htl = hpre
    hT = Vw(hTt[:, 0:8 * NTP].rearrange("p (k n) -> p k n", k=8), [dhT])
    lnrow = [T("lnrow", [128, D]) for _ in range(2)]
    cw = T("cw", [128, 12, 4])
    halo = T("halo", [128, 12, 3])
    rowc = T("rowc", [128, 16])
    alog = T("alog", [128, 4])
    dnnw = T("dnnw", [128, 128])
    mlnw = T("mlnw", [128, 512])
    Sdn = T("Sdn", [128, H, 128]); Sdnb = T("Sdnb", [128, H, 128], BF16)
    Rrt = T("Rrt", [128, H, 128]); Rrtb = T("Rrtb", [128, H, 128], BF16)
    Cml = T("Cml", [128, H, 129]); Cmlb = T("Cmlb", [128, H, 129], BF16)
    mcar = T("mcar", [128, H])
    S0 = [Vw(hp[:, 0:NSB * 129].rearrange("p (b v) -> p b v", b=NSB), dhp)] * 2
    S1 = [Vw(big1.t[:, jj * 2048:(jj + 1) * 2048].rearrange("p (b v) -> p b v", b=NSB),
             [dbig[jj]] + (pre_c[0:8] if jj == 0 else pre_c[7:12])) for jj in range(2)]
    S0b = [Vw(hTt[:, 0:NSB * 129].rearrange("p (b v) -> p b v", b=NSB), [dhT])] * 2
    msp = T("msp", [64, H])

    for v in vaug:
        memset("pool", v[:, :, 128:129], 1.0, [v])

    scr_i = [0]
    scr_pool = {}

    def tmp(shape, dt=F32, n=4, tag=""):
        key = (tuple(shape), str(dt), tag)
        if key not in scr_pool:
            scr_pool[key] = [[T("tmp", list(shape), dt) for _ in range(n)], 0]
        ent = scr_pool[key]
        ent[1] = (ent[1] + 1) % len(ent[0])
        return ent[0][ent[1]]

    def psum_bf(pt):
        return pt.t[:].bitcast(BF16)

    def layer_norm(src, dst, C, grow, brow):
        st = tmp([128, 2, 6], tag="bnst")
        for hh in range(2):
            P.op("dve", (lambda hh: lambda e: e.bn_stats(out=st[:C, hh, :], in_=src[:C, hh * 512:(hh + 1) * 512]))(hh),
                 reads=[src], writes=[st])
        mv = tmp([128, 4], tag="bnmv")
        P.op("dve", lambda e: e.bn_aggr(out=mv[:C, 0:2], in_=st[:C].rearrange("p a b -> p (a b)")), reads=[st], writes=[mv])
        ts("dve", mv[:C, 2:3], mv[:C, 1:2], LN_EPS, ALU.add, [mv], [mv])
        act(mv[:C, 2:3], mv[:C, 2:3], AF.Ln, [mv], [mv])
        act(mv[:C, 2:3], mv[:C, 2:3], AF.Exp, [mv], [mv], scale=-0.5)
        stt(mv[:C, 3:4], mv[:C, 0:1], -1.0, mv[:C, 2:3], ALU.mult, ALU.mult, [mv], [mv])
        act(dst[:C, :], src[:C, :], AF.Identity, [src, mv], [dst], scale=mv[:C, 2:3], bias=mv[:C, 3:4])
        tt("dve", dst[:C, :], dst[:C, :], grow[:C, :], ALU.mult, [dst, grow], [dst])
        tt("dve", dst[:C, :], dst[:C, :], brow[:C, :], ALU.add, [dst, brow], [dst])

    def branch_out(o_src, o_is_psum, C, gate_tile, oTdst, col0, R_extra):
        ss = tmp([128, 8], tag="rms")
        junk = tmp([128, 128], tag="junk", n=2)
        for h in range(H):
            P.op("act", (lambda h: lambda e: e.activation(out=junk[:C, :], in_=o_src[:, h, :], func=AF.Square,
                                                           accum_out=ss[:C, h:h + 1]))(h),
                 reads=R_extra, writes=[ss, junk])
        ts("dve", ss[:C, 4:8], ss[:C, 0:4], 1.0 / 128.0, ALU.mult, [ss], [ss], s2=RMS_EPS, op1=ALU.add)
        act(ss[:C, 4:8], ss[:C, 4:8], AF.Ln, [ss], [ss])
        act(ss[:C, 4:8], ss[:C, 4:8], AF.Exp, [ss], [ss], scale=-0.5)
        ob = tmp([128, H, 128], BF16, tag="ob", n=2)
        for h in range(H):
            stt(ob[:C, h, :], o_src[:, h, :], ss[:C, 4 + h:5 + h], gate_tile[:C, h * 128:(h + 1) * 128],
                ALU.mult, ALU.mult, R_extra + [ss, gate_tile], [ob])
        pt = pm()
        pv = psum_bf(pt)
        for h in range(H):
            tr(pv[:, h * C:(h + 1) * C], ob[:C, h, :], identb[:C, :C], [ob, identb], [pt])
        cp("act", oTdst[:, :, col0:col0 + C], pv[:, 0:H * C].rearrange("p (h c) -> p h c", h=H), [pt], [oTdst])

    DBG_NP = int(os.environ.get("KDBG_PASSES", "999"))
    DBG_ST = int(os.environ.get("KDBG_STAGE", "999"))
    DBG_SAMPLE = int(os.environ.get("KDBG_SAMPLE", "0"))
    DBG_MIX = int(os.environ.get("KDBG_MIX", "999"))
    DBG_ML = int(os.environ.get("KDBG_ML", "999"))
    npass_done = [0]

    def stop(n):
        return npass_done[0] == DBG_NP - 1 and DBG_ST <= n

    def layer_pass(l, sc):
        sample = (sc == NPASS)
        m = "s" if sample else "p"
        NT = NSB * ST if sample else NTP
        C = 64 if sample else 128
        nch = 1 if sample else 2
        nblk = NSB if sample else 1
        first = (sc == 0)
        lastp = (sc == NPASS - 1)
        Uc, Lc = K["U_" + m], K["L_" + m]
        lastsel, lastind, blkind = K["lastsel_" + m], K["lastind_" + m], K["blkind_" + m]
        strictT, cmT = K["strictT_" + m], K["cmT_" + m]
        maskb, maskbT = KB["maskb_" + m], KB["maskbT_" + m]
        colmask = KB["colmask"]
        blkindb = KB["blkind_s"]
        tok0 = SEQ if sample else sc * NTP
        P.tag = "L%d.P%d.xT" % (l, sc)

        if first:
            for c12 in range(12):
                P.dma("sp", cw[:, c12, :], conv_w[l, :, c12 * 128:(c12 + 1) * 128].rearrange("j p -> p j"),
                      writes=[cw], allow_slow_non_contiguous=True)
            P.dma("sp", rowc[:, 0:4], dt_bias[l:l + 1, :].broadcast_to([128, H]), writes=[rowc])
            P.dma("sp", alog[:], A_log[l:l + 1, :].broadcast_to([128, H]), writes=[alog])
            P.dma("sp", rowc[:, 8:12], ibias[l:l + 1, :].broadcast_to([128, H]), writes=[rowc])
            P.dma("sp", rowc[:, 12:16], fbias[l:l + 1, :].broadcast_to([128, H]), writes=[rowc])
            P.dma("sp", dnnw[:], dn_nw[l:l + 1, :].broadcast_to([128, 128]), writes=[dnnw])
            P.dma("sp", mlnw[:], ml_nw[l:l + 1, :].broadcast_to([128, 512]), writes=[mlnw])
            act(alog[:], alog[:], AF.Exp, [alog], [alog])
            ts("dve", rowc[:, 4:8], alog[:], -1.0, ALU.mult, [alog, rowc], [rowc])
            memset("pool", Sdn[:], 0.0, [Sdn]); memset("pool", Sdnb[:], 0.0, [Sdnb])
            memset("pool", Rrt[:], 0.0, [Rrt]); memset("pool", Rrtb[:], 0.0, [Rrtb])
            memset("pool", Cml[:], 0.0, [Cml]); memset("pool", Cmlb[:], 0.0, [Cmlb])
            memset("pool", mcar[:], 0.0, [mcar])

        for j in range(nch):
            r0 = j * 128
            if l == 0:
                src = xs[:, :] if sample else xp[sc * NTP + r0: sc * NTP + r0 + 128, :]
                P.dma("sp", xt[j][:C, :], src, writes=[xt[j]])
            else:
                ci = 16 if sample else sc * 2 + j
                P.dma("sp", xt[j][:C, :], xmid.ap[tok0 + r0: tok0 + r0 + C, :], reads=[xmid_d[ci]], writes=[xt[j]])
            for g in range(2):
                pt = pm()
                for i in range(4):
                    k = g * 4 + i
                    tr(pt[:, i * C:(i + 1) * C], xt[j][:C, k * 128:(k + 1) * 128], identf[:C, :C], [xt[j], identf], [pt])
                cp("act" if g == 0 else "dve", xT[:, g * 4:(g + 1) * 4, r0:r0 + C],
                   pt[:, 0:4 * C].rearrange("p (i c) -> p i c", i=4), [pt], [xT])

        if stop(1):
            return
        if sample:
            pre_v = pre.t[:, :, 0:NSB * 7].rearrange("p c (b j) -> p c b j", j=7)
            for g in range(3):
                cst = tmp([48, 512], tag="cst", n=1)
                P.dma("sp", cst[:, :], cconv[l, :, g * 512:(g + 1) * 512], writes=[cst])
                pt = pm()
                for i in range(4):
                    tr(pt[:, i * 48:(i + 1) * 48], cst[:48, i * 128:(i + 1) * 128], identf[:48, :48], [cst, identf], [pt])
                cp("dve", pre_v[:, g * 4:(g + 1) * 4, :, 0:3],
                   pt[:, 0:4 * 48].rearrange("p (i b j) -> p i b j", i=4, j=3), [pt], [pre])
        else:
            if first:
                memset("pool", pre[:, :, 0:3], 0.0, [pre])
            else:
                cp("pool", pre[:, :, 0:3], halo[:, :, :], [halo], [pre])

        P.tag = "L%d.P%d.fm" % (l, sc)

        def dn_gen(c12, pt):
            i2 = c12 % 2
            pc = pre_c[c12]
            if sample:
                cp("act", pre_v[:, c12, :, 3:7], pt[:, 0:NT].rearrange("p (b t) -> p b t", t=ST), [pt], [pc])
                src = lambda jj: pre_v[:, c12, :, jj:jj + ST]
                cv = cvt[i2].t[:, 0:NT].rearrange("p (b t) -> p b t", t=ST)
            else:
                cp("act", pre[:, c12, 3:3 + NT], pt[:, 0:NT], [pt], [pc])
                src = lambda jj: pre[:, c12, jj:jj + NT]
                cv = cvt[i2][:, 0:NT]
            yield
            act(cv, src(0), AF.Copy, [pc, cw], [cvt[i2]], scale=cw[:, c12, 0:1])
            yield
            for jj in range(1, 4):
                stt(cv, src(jj), cw[:, c12, jj:jj + 1], cv, ALU.mult, ALU.add, [pc, cw, cvt[i2]], [cvt[i2]])
            yield
            h = c12 % 4
            act(slt[i2][:, 0:NT], cvt[i2][:, 0:NT], AF.Exp, [cvt[i2]], [slt[i2]], scale=-1.0)
            act(slt[i2][:, 0:NT], slt[i2][:, 0:NT], AF.Ln, [slt[i2]], [slt[i2]], bias=1.0)
            act(slt[i2][:, 0:NT], slt[i2][:, 0:NT], AF.Exp, [slt[i2]], [slt[i2]], scale=-1.0)
            yield
            if c12 >= 8:
                tt("dve", dvT[:, h, 0:NT], cvt[i2][:, 0:NT], slt[i2][:, 0:NT], ALU.mult, [cvt[i2], slt[i2]], [dvT])
                return
            tt("dve", slt[i2][:, 0:NT], cvt[i2][:, 0:NT], slt[i2][:, 0:NT], ALU.mult, [cvt[i2], slt[i2]], [slt[i2]])
            yield
            act(sqt[i2][:, 0:NT], slt[i2][:, 0:NT], AF.Square, [slt[i2]], [sqt[i2]])
            yield
            p2 = pm()
            mm(p2[:, 0:NT], onesb[:, :], sqt[i2][:, 0:NT], [onesb, sqt[i2]], [p2])
            if c12 < 4:
                act(rnt[i2][:, 0:NT], p2[:, 0:NT], AF.Ln, [p2], [rnt[i2]], scale=128.0, bias=128.0 * RMS_EPS)
            else:
                act(rnt[i2][:, 0:NT], p2[:, 0:NT], AF.Ln, [p2], [rnt[i2]], scale=1.0, bias=RMS_EPS)
            act(rnt[i2][:, 0:NT], rnt[i2][:, 0:NT], AF.Exp, [rnt[i2]], [rnt[i2]], scale=-0.5)
            yield
            dst = dqT if c12 < 4 else dkT
            tt("dve", dst[:, h, 0:NT], slt[i2][:, 0:NT], rnt[i2][:, 0:NT], ALU.mult, [slt[i2], rnt[i2]], [dst])

        def mlq_gen(cc, pt):
            cp("act", mqT[:, cc, 0:NT], pt[:, 0:NT], [pt], [mqT])
            return
            yield

        def mlk_gen(cc, pt):
            act(mkT[:, cc, 0:NT], pt[:, 0:NT], AF.Copy, [pt], [mkT], scale=128.0 ** -0.5)
            return
            yield

        if stop(2):
            return
        active = []

        def pump(limit):
            while len(active) > limit:
                for g in list(active):
                    try:
                        next(g)
                    except StopIteration:
                        active.remove(g)

        fm_list = [(OFF["a_qkv"] + blk * 512, "dn", blk) for blk in range(3)] + [(OFF["m_q"], "mq", 0), (OFF["m_k"], "mk", 0)]
        for (c0, kind, blk) in fm_list:
            buf, wv = getw(("in", l), 8, c0, 512)
            for cc in range(4):
                pt = pd()
                for k in range(8):
                    mm(pt[:, 0:NT], wv[:, k, cc * 128:(cc + 1) * 128], xT[:, k, 0:NT], [buf, xT], [pt],
                       start=(k == 0), stop=(k == 7))
                if kind == "dn":
                    active.append(dn_gen(blk * 4 + cc, pt))
                elif kind == "mq":
                    active.append(mlq_gen(cc, pt))
                else:
                    active.append(mlk_gen(cc, pt))
                pump(1)
        pump(0)
        if not sample:
            cp("pool", halo[:, :, :], pre[:, :, NTP:NTP + 3], [pre], [halo])

        if stop(3):
            return
        if sample:
            for g in range(3):
                cso = tmp([48, 512], tag="cst", n=1)
                cso_src = tmp([128, 4, NSB, 3], tag="csos", n=1)
                cp("pool", cso_src[:], pre_v[:, g * 4:(g + 1) * 4, :, 4:7], [pre], [cso_src])
                pt = pm()
                for i in range(4):
                    tr(pt[:48, i * 128:(i + 1) * 128], cso_src[:, i].rearrange("p b j -> p (b j)"), identf[:, :],
                       [cso_src, identf], [pt])
                cp("dve", cso[:48, :], pt[:48, 0:512], [pt], [cso])
                P.dma("pool", o_sconv[l, :, g * 512:(g + 1) * 512], cso[:48, :], reads=[cso], final=True)
        elif lastp:
            for g in range(3):
                cso = tmp([48, 512], tag="cst", n=1)
                cso_src = tmp([128, 4, NSB, 3], tag="csos", n=1)
                cp("pool", cso_src[:, :, 0, :], pre[:, g * 4:(g + 1) * 4, NTP:NTP + 3], [pre], [cso_src])
                pt = pm()
                for i in range(4):
                    tr(pt[:3, i * 128:(i + 1) * 128], cso_src[:, i, 0, :], identf[:, :], [cso_src, identf], [pt])
                cp("dve", cso[:3, :], pt[:3, 0:512], [pt], [cso])
                P.dma("pool", o_pconv[l, :, g * 512:(g + 1) * 512], cso[:3, :], reads=[cso], final=True)

        if stop(4):
            return
        P.tag = "L%d.P%d.tm" % (l, sc)

        def tm_block(c0, ncols, handler):
            buf, wv = getw(("in", l), 8, c0, ncols)
            for j in range(nch):
                pt = pd()
                for k in range(8):
                    mm(pt[:C, 0:ncols], xT[:, k, j * 128:j * 128 + C], wv[:, k, :], [buf, xT], [pt],
                       start=(k == 0), stop=(k == 7))
                handler(j, pt)

        for j in range(nch):
            ci = 16 if sample else sc * 2 + j
            P.dma("sp", rott[j][:].rearrange("p a h f -> p (a h f)"), cin["rot"][ci], writes=[rott[j]])

        def z_h(j, pt):
            act(zs[j][:C, :], pt[:C, 0:512], AF.Silu, [pt], [zs[j]])
            tt("pool", zs[j][:C, :].rearrange("p (h v) -> p h v", h=H), zs[j][:C, :].rearrange("p (h v) -> p h v", h=H),
               dnnw[:C, :].unsqueeze(1).broadcast_to([C, H, 128]), ALU.mult, [zs[j], dnnw], [zs[j]])

        def g_h(j, pt):
            act(gsl[j][:C, :], pt[:C, 0:512], AF.Silu, [pt], [gsl[j]])

        def o_h(j, pt):
            act(osg[j][:C, :], pt[:C, 0:512], AF.Sigmoid, [pt], [osg[j]])
            tt("pool", osg[j][:C, :], osg[j][:C, :], mlnw[:C, :], ALU.mult, [osg[j], mlnw], [osg[j]])

        def rot_h(dst_list, ta, tb_):
            def hnd(j, pt):
                x = pt[:C, 0:512].rearrange("p (h two f) -> p h two f", h=H, two=2)
                x1, x2 = x[:, :, 0, :], x[:, :, 1, :]
                cq, sq = rott[j][:C, ta], rott[j][:C, tb_]
                dst = dst_list[j]
                dv_ = dst[:C].rearrange("p h (two f) -> p h two f", two=2)
                t1 = tmp([128, H, 64], tag="rot", n=4); t2 = tmp([128, H, 64], tag="rot", n=4)
                t3 = tmp([128, H, 64], tag="rot", n=4); t4 = tmp([128, H, 64], tag="rot", n=4)
                tt("dve", t1[:C], x1, cq, ALU.mult, [pt, rott[j]], [t1])
                tt("dve", t2[:C], x2, sq, ALU.mult, [pt, rott[j]], [t2])
                tt("dve", t3[:C], x1, sq, ALU.mult, [pt, rott[j]], [t3])
                tt("dve", t4[:C], x2, cq, ALU.mult, [pt, rott[j]], [t4])
                tt("pool", dv_[:, :, 0, :], t1[:C], t2[:C], ALU.subtract, [t1, t2], [dst])
                tt("pool", dv_[:, :, 1, :], t3[:C], t4[:C], ALU.add, [t3, t4], [dst])
            return hnd

        def rv_h(j, pt):
            cp("act", rv_tm[j][:C].rearrange("p h v -> p (h v)"), pt[:C, 0:512], [pt], [rv_tm[j]])

        def mv_h(j, pt):
            cp("act", vaug[j][:C, :, 0:128], pt[:C, 0:512].rearrange("p (h v) -> p h v", h=H), [pt], [vaug[j]])

        def sm_h(col):
            def hnd(j, pt):
                cp("dve", smt[j][:C, col:col + 8], pt[:C, 0:8], [pt], [smt[j]])
            return hnd

        tm_block(OFF["a_z"], 512, z_h)
        tm_block(OFF["a_beta"], 8, sm_h(0))
        tm_block(OFF["r_q"], 512, rot_h(rq_tm, 0, 1))
        tm_block(OFF["r_k"], 512, rot_h(rk_tm, 2, 3))
        tm_block(OFF["r_v"], 512, rv_h)
        tm_block(OFF["r_g"], 512, g_h)
        tm_block(OFF["m_v"], 512, mv_h)
        tm_block(OFF["m_o"], 512, o_h)
        tm_block(OFF["m_i"], 8, sm_h(8))

        if stop(5):
            return
        for j in range(nch):
            P.tag = "L%d.P%d.mix%d" % (l, sc, j)
            cs = slice(j * 128, j * 128 + C)
            last_chunk = lastp and j == nch - 1
            mixers(l, j, cs, C, nblk, sample, last_chunk, Uc, Lc, lastsel, lastind, blkind, blkindb, strictT, cmT,
                   maskb, maskbT, colmask)

        if stop(6):
            return
        P.tag = "L%d.P%d.merge" % (l, sc)
        for br in range(3):
            bbuf, bwv = getw(("br%d" % br, l), 4, 0, D)
            for half in range(2):
                gbuf, gwv = getw(("in", l), 8, OFF["gates"] + br * D + half * 512, 512)
                for cc in range(4):
                    d = half * 4 + cc
                    pg = pd()
                    for k in range(8):
                        mm(pg[:, 0:NT], gwv[:, k, cc * 128:(cc + 1) * 128], xT[:, k, 0:NT], [gbuf, xT], [pg],
                           start=(k == 0), stop=(k == 7))
                    gs = tmp([128, NTP], tag="gs", n=1)
                    act(gs[:, 0:NT], pg[:, 0:NT], AF.Sigmoid, [pg], [gs])
                    pb = pd()
                    for k in range(4):
                        mm(pb[:, 0:NT], bwv[:, k, d * 128:(d + 1) * 128], oT[br][:, k, 0:NT], [bbuf, oT[br]], [pb],
                           start=(k == 0), stop=(k == 3))
                    if br == 0:
                        tt("dve", macc[:, d, 0:NT], pb[:, 0:NT], gs[:, 0:NT], ALU.mult, [pb, gs], [macc])
                    else:
                        t_ = tmp([128, NTP], tag="mt", n=1)
                        tt("dve", t_[:, 0:NT], pb[:, 0:NT], gs[:, 0:NT], ALU.mult, [pb, gs], [t_])
                        if br == 1:
                            tt("pool", macc[:, d, 0:NT], macc[:, d, 0:NT], t_[:, 0:NT], ALU.add, [macc, t_], [macc])
                        else:
                            tt("pool", mergedT[:, d, 0:NT], macc[:, d, 0:NT], t_[:, 0:NT], ALU.add, [macc, t_], [mergedT])

        if stop(7):
            return
        P.tag = "L%d.P%d.wout" % (l, sc)
        P.dma("sp", lnrow[0][:], ln1g[l:l + 1, :].broadcast_to([128, D]), writes=[lnrow[0]])
        P.dma("sp", lnrow[1][:], ln1b[l:l + 1, :].broadcast_to([128, D]), writes=[lnrow[1]])
        wo = [getw(("out", l), 8, half * 512, 512) for half in range(2)]
        for j in range(nch):
            for half in range(2):
                wbuf, wv = wo[half]
                pt = pd()
                for k in range(8):
                    mm(pt[:C, 0:512], mergedT[:, k, j * 128:j * 128 + C], wv[:, k, :], [wbuf, mergedT], [pt],
                       start=(k == 0), stop=(k == 7))
                stt(hpre[j][:C, half * 512:(half + 1) * 512], xt[j][:C, half * 512:(half + 1) * 512], ALPHA,
                    pt[:C, 0:512], ALU.mult, ALU.add, [xt[j], pt], [hpre[j]])
            if j == 0:
                layer_norm(hpre[0], htl[0], C, lnrow[0], lnrow[1])
        for j in range(nch):
            if j > 0:
                layer_norm(hpre[j], htl[j], C, lnrow[0], lnrow[1])
            for g in range(2):
                pt = pm()
                for i in range(4):
                    k = g * 4 + i
                    tr(pt[:, i * C:(i + 1) * C], htl[j][:C, k * 128:(k + 1) * 128], identf[:C, :C], [htl[j], identf], [pt])
                cp("act" if g == 0 else "dve", hT[:, g * 4:(g + 1) * 4, j * 128:j * 128 + C],
                   pt[:, 0:4 * C].rearrange("p (i c) -> p i c", i=4), [pt], [hT])

        if stop(8):
            return
        P.tag = "L%d.P%d.ff1" % (l, sc)
        for hb in range(8):
            fbuf, fv = getw(("ff1", l), 8, hb * 512, 512)
            for cc in range(4):
                pt = pd()
                for k in range(8):
                    mm(pt[:, 0:NT], fv[:, k, cc * 128:(cc + 1) * 128], hT[:, k, 0:NT], [fbuf, hT], [pt],
                       start=(k == 0), stop=(k == 7))
                r_ = tmp([128, NTP], tag="relu", n=1)
                act(r_[:, 0:NT], pt[:, 0:NT], AF.Relu, [pt], [r_])
                tt("pool" if cc % 2 else "dve", actT[:, hb * 4 + cc, 0:NT], r_[:, 0:NT], r_[:, 0:NT], ALU.mult, [r_], [actT])
        P.tag = "L%d.P%d.ff2" % (l, sc)
        for half in range(2):
            for q in range(4):
                fbuf, fv = getw_rows(("ff2", l), q * 8, 8, half * 512, 512)
                for j in range(nch):
                    for k in range(8):
                        mm(PACC[j][:C, 0:512], actT[:, q * 8 + k, j * 128:j * 128 + C], fv[:, k, :], [fbuf, actT], [PACC[j]],
                           start=(q == 0 and k == 0), stop=(q == 3 and k == 7))
            for j in range(nch):
                stt(hpre[j][:C, half * 512:(half + 1) * 512], htl[j][:C, half * 512:(half + 1) * 512], ALPHA,
                    PACC[j][:C, 0:512], ALU.mult, ALU.add, [htl[j], PACC[j]], [hpre[j]])
        P.dma("sp", lnrow[0][:], ln2g[l:l + 1, :].broadcast_to([128, D]), writes=[lnrow[0]])
        P.dma("sp", lnrow[1][:], ln2b[l:l + 1, :].broadcast_to([128, D]), writes=[lnrow[1]])
        for j in range(nch):
            yt = hpre[j]
            layer_norm(hpre[j], yt, C, lnrow[0], lnrow[1])
            if l == DEPTH - 1:
                dst = y_s[:, :] if sample else y_p[sc * NTP + j * 128: sc * NTP + j * 128 + 128, :]
                P.dma("pool", dst, yt[:C, :], reads=[yt], final=True)
            else:
                ci = 16 if sample else sc * 2 + j
                P.dma("pool", xmid.ap[tok0 + j * 128: tok0 + j * 128 + C, :], yt[:C, :], reads=[yt], writes=[xmid_d[ci]])

    def mixers(l, j, cs, C, nblk, sample, last_chunk, Uc, Lc, lastsel, lastind, blkind, blkindb, strictT, cmT,
               maskb, maskbT, colmask):
        nb4 = nblk * 4
        sm = smt[j]
        gt = tmp([128, 40], tag="gt", n=2)

        def masked_cols(srcT_ap, R):
            o = tmp([128, NSB, 64], BF16, tag="mcol", n=2)
            tt("pool", o[:], srcT_ap.unsqueeze(1).broadcast_to([128, NSB, 64]),
               colmask[:].rearrange("p (b c) -> p b c", b=NSB), ALU.mult, R + [colmask], [o])
            return o

        def masked_rows(src_ap, ncol, R):
            o = tmp([64, NSB, 129], BF16, tag="mrow", n=1)
            tt("pool", o[:, :, 0:ncol], src_ap.unsqueeze(1).broadcast_to([64, NSB, ncol]),
               blkindb[:, :].unsqueeze(2).broadcast_to([64, NSB, ncol]), ALU.mult, R + [blkindb], [o])
            return o

        tg0 = P.tag
        def gen_dn():
            P.tag = tg0 + ".dn_gate"
            act(gt[:C, 0:4], sm[:C, 0:4], AF.Exp, [sm], [gt], scale=-1.0)
            act(gt[:C, 0:4], gt[:C, 0:4], AF.Ln, [gt], [gt], bias=1.0)
            act(gt[:C, 0:4], gt[:C, 0:4], AF.Exp, [gt], [gt], scale=-1.0)
            tt("dve", gt[:C, 4:8], sm[:C, 4:8], rowc[:C, 0:4], ALU.add, [sm, rowc], [gt])
            act(gt[:C, 8:12], gt[:C, 4:8], AF.Exp, [gt], [gt])
            act(gt[:C, 12:16], gt[:C, 8:12], AF.Ln, [gt], [gt], bias=1.0)
            tt("dve", gt[:C, 16:20], gt[:C, 12:16], rowc[:C, 4:8], ALU.mult, [gt, rowc], [gt])
            gbt = tmp([128, NSB * 4], tag="gbt", n=2)
            tt("dve", gbt[:C, 0:nb4].rearrange("p (b h) -> p b h", h=H),
               gt[:C, 16:20].unsqueeze(1).broadcast_to([C, nblk, H]),
               blkind[:C, 0:nblk].unsqueeze(2).broadcast_to([C, nblk, H]), ALU.mult, [gt, blkind], [gbt])
            p1 = pm()
            mm(p1[:C, 0:4], Uc[:C, :C], gt[:C, 16:20], [Uc, gt], [p1])
            mm(p1[:C, 4:8], Lc[:C, :C], gt[:C, 16:20], [Lc, gt], [p1])
            mm(p1[:, 8:8 + nb4], onesf[:C, :], gbt[:C, 0:nb4], [onesf, gbt], [p1])
            act(gt[:C, 20:28], p1[:C, 0:8], AF.Exp, [p1], [gt])
            egt = tmp([128, NSB * 4], tag="egt", n=2)
            act(egt[:, 0:nb4], p1[:, 8:8 + nb4], AF.Exp, [p1], [egt])
            yield
            if DBG_MIX <= 1:
                return
            dk_tm = tmp([128, H, 128], BF16, tag="dk_tm", n=1)
            dv_tm = tmp([128, H, 128], BF16, tag="dv_tm", n=1)
            for (srcT, dst) in ((dkT, dk_tm), (dvT, dv_tm)):
                pt = pm()
                pv = psum_bf(pt)
                for h in range(H):
                    tr(pv[:C, h * 128:(h + 1) * 128], srcT[:, h, cs], identb[:, :], [srcT, identb], [pt])
                cp("act", dst[:C].rearrange("p h v -> p (h v)"), pv[:C, 0:512], [pt], [dst])
                yield
            o_dn = tmp([128, H, 128], tag="o_all", n=2)
            def dn_pre(h):
                P.tag = tg0 + ".dn_pre"
                kg = tmp([128, 128], BF16, tag="kg", n=2)
                kdec = tmp([128, 128], BF16, tag="kdec", n=2)
                act(kg[:C, :], dk_tm[:C, h, :], AF.Copy, [dk_tm, gt], [kg], scale=gt[:C, 20 + h:21 + h])
                act(kdec[:C, :], dk_tm[:C, h, :], AF.Copy, [dk_tm, gt], [kdec], scale=gt[:C, 24 + h:25 + h])
                Lg = tmp([128, 128], tag="Lg", n=2)
                ts("dve", Lg[:C, :C], Lc[:C, :C], gt[:C, 16 + h:17 + h], ALU.mult, [Lc, gt], [Lg])
                pe_ = pm()
                mm(pe_[:C, 0:C], Lg[:C, :C], Uc[:C, :C], [Lg, Uc], [pe_], start=True, stop=False)
                mm(pe_[:C, 0:C], identb[:C, :C], maskbT[:C, :C], [identb, maskbT], [pe_], start=False, stop=True)
                DTt = tmp([128, 128], tag="DTt", n=2)
                act(DTt[:C, :C], pe_[:C, 0:C], AF.Exp, [pe_], [DTt])
                DTs = tmp([128, 128], tag="DTs", n=2)
                tt("dve", DTs[:C, :C], DTt[:C, :C], strictT[:C, :C], ALU.mult, [DTt, strictT], [DTs])
                pg = pm()
                mm(pg[:C, 0:C], dkT[:, h, cs], dkT[:, h, cs], [dkT], [pg])
                mm(pg[:C, 128:128 + C], dkT[:, h, cs], dqT[:, h, cs], [dkT, dqT], [pg])
                NM = tmp([128, 2, 128], F32, tag="NM", n=4)
                stt(NM[:C, 0, :C], pg[:C, 0:C], gt[:C, h:h + 1], DTs[:C, :C], ALU.mult, ALU.mult, [pg, gt, DTs], [NM])
                PTt = tmp([128, 128], BF16, tag="PTt", n=2)
                tt("dve", PTt[:C, :C], pg[:C, 128:128 + C], DTt[:C, :C], ALU.mult, [pg, DTt], [PTt])
                X = tmp([128, 128], F32, tag="X", n=4)
                tt("dve", X[:C, :C], identf[:C, :C], NM[:C, 0, :C], ALU.subtract, [identf, NM], [X])
                pt = pm()
                tr(pt[:C, 0:C], NM[:C, 0, :C], identf[:C, :C], [NM, identf], [pt])
                cp("act", NM[:C, 1, :C], pt[:C, 0:C], [pt], [NM])
                return dict(kg=kg, kdec=kdec, PTt=PTt, X=X, NM=NM)

            def dn_level(st, lastlev):
                P.tag = tg0 + ".dn_lev"
                X, NM = st["X"], st["NM"]
                pa = pm()
                mm(pa[:C, 0:C], NM[:C, 1, :C], NM[:C, 0, :C], [NM], [pa])
                mm(pa[:C, 128:128 + C], NM[:C, 0, :C], NM[:C, 1, :C], [NM], [pa])
                NM2 = tmp([128, 2, 128], F32, tag="NM", n=4)
                cp("act", NM2[:C, :, :C], pa[:C, 0:256].rearrange("p (a c) -> p a c", a=2)[:, :, 0:C], [pa], [NM2])
                pb = pm()
                mm(pb[:C, 0:C], NM2[:C, 1, :C], X[:C, :C], [NM2, X], [pb])
                if lastlev:
                    X2 = tmp([128, 128], BF16, tag="Xb", n=2)
                else:
                    X2 = tmp([128, 128], F32, tag="X", n=4)
                tt("dve", X2[:C, :C], pb[:C, 0:C], X[:C, :C], ALU.add, [pb, X], [X2])
                X, NM = X2, NM2
                st["X"], st["NM"] = X, NM

            def dn_post(h, kg, kdec, PTt, X):
                P.tag = tg0 + ".dn_post"
                pw = pm()
                mm(pw[:, 0:C], kg[:C, :], X[:C, :C], [kg, X], [pw])
                nwT = tmp([128, 128], BF16, tag="nwT", n=2)
                act(nwT[:, 0:C], pw[:, 0:C], AF.Copy, [pw], [nwT], scale=-1.0)
                yield
                if sample:
                    i2 = h % 2
                    P.dma("sp", S0[i2][:, :, 0:128], sS[l, :, h].rearrange("b k v -> k b v"), writes=[S0[i2]])
                    cp("act", S0b[i2][:, :, 0:128], S0[i2][:, :, 0:128], [S0[i2]], [S0b[i2]])
                    nwTm = masked_cols(nwT[:, 0:C], [nwT])
                    qTm = masked_cols(dqT[:, h, cs], [dqT])
                    kdm = masked_rows(kdec[:C, :], 128, [kdec])
                    st_r = [S0b[i2]]
                    S_b = lambda b: S0b[i2][:, b, 0:128]
                    nw_b = lambda b: nwTm[:, b, :]
                    q_b = lambda b: qTm[:, b, :]
                    kd_b = lambda b: kdm[:C, b, 0:128]
                    st_extra = [nwTm, qTm, kdm]
                else:
                    st_r = [Sdnb]
                    S_b = lambda b: Sdnb[:, h, :]
                    nw_b = lambda b: nwT[:, 0:C]
                    q_b = lambda b: dqT[:, h, cs]
                    kd_b = lambda b: kdec[:C, :]
                    st_extra = [nwT, dqT, kdec]
                pvn = pm()
                mm(pvn[:C, 0:128], X[:C, :C], dv_tm[:C, h, :], [X, dv_tm], [pvn], start=True, stop=False)
                for b in range(nblk):
                    mm(pvn[:C, 0:128], nw_b(b), S_b(b), st_r + st_extra, [pvn], start=False, stop=(b == nblk - 1))
                vnew = tmp([128, 128], BF16, tag="vnew", n=2)
                act(vnew[:C, :], pvn[:C, 0:128], AF.Copy, [pvn, gt], [vnew], scale=gt[:C, h:h + 1])
                yield
                pz = pm()
                for b in range(nblk):
                    mm(pz[:C, 0:128], q_b(b), S_b(b), st_r + st_extra, [pz], start=(b == 0), stop=(b == nblk - 1))
                pi_ = pm()
                mm(pi_[:C, 0:128], PTt[:C, :C], vnew[:C, :], [PTt, vnew], [pi_])
                isb = tmp([128, 128], tag="isb", n=2)
                cp("act", isb[:C, :], pi_[:C, 0:128], [pi_], [isb])
                stt(o_dn[:C, h, :], pz[:C, 0:128], gt[:C, 20 + h:21 + h], isb[:C, :], ALU.mult, ALU.add, [pz, gt, isb], [o_dn])
                yield
                if sample:
                    for b4 in range(0, nblk, 4):
                        pu = pm()
                        for bb in range(4):
                            b = b4 + bb
                            mm(pu[:, bb * 128:(bb + 1) * 128], kd_b(b), vnew[:C, :], st_extra + [vnew], [pu])
                        for bb in range(4):
                            b = b4 + bb
                            stt(S1[i2][:, b, :], S0[i2][:, b, 0:128], egt[:, b * 4 + h:b * 4 + h + 1],
                                pu[:, bb * 128:(bb + 1) * 128], ALU.mult, ALU.add, [S0[i2], egt, pu], [S1[i2]])
                    P.dma("pool", o_sS[l, :, h].rearrange("b k v -> k b v"), S1[i2][:, :, :], reads=[S1[i2]], final=True)
                else:
                    pu = pm()
                    mm(pu[:, 0:128], kdec[:C, :], vnew[:C, :], [kdec, vnew], [pu])
                    stt(Sdn[:, h, :], Sdn[:, h, :], egt[:, h:h + 1], pu[:, 0:128], ALU.mult, ALU.add, [Sdn, egt, pu], [Sdn])
                    cp("act", Sdnb[:, h, :], Sdn[:, h, :], [Sdn], [Sdnb])

            nlev = 1 if sample else 6
            for hp in range(0, H, 2):
                sts = []
                for hh in (hp, hp + 1):
                    sts.append(dn_pre(hh))
                    yield
                for lev in range(nlev):
                    for st in sts:
                        dn_level(st, lev == nlev - 1)
                        yield
                for hh, st in zip((hp, hp + 1), sts):
                    yield from dn_post(hh, st["kg"], st["kdec"], st["PTt"], st["X"])
                    yield
            if last_chunk:
                P.dma("pool", o_pS[l].rearrange("h k v -> k h v"), Sdn[:], reads=[Sdn], final=True)
            P.tag = tg0 + ".dn_out"
            branch_out(o_dn[:C], False, C, zs[j], oT[0], j * 128, [o_dn])


        def gen_ret():
            P.tag = tg0 + ".ret"
            rqT = tmp([128, H, 128], BF16, tag="rqT", n=1)
            rkT = tmp([128, H, 128], BF16, tag="rkT", n=1)
            for (src, dst) in ((rq_tm[j], rqT), (rk_tm[j], rkT)):
                pt = pm()
                pv = psum_bf(pt)
                for h in range(H):
                    tr(pv[:, h * C:(h + 1) * C], src[:C, h, :], identb[:C, :C], [src, identb], [pt])
                cp("act", dst[:, :, 0:C], pv[:, 0:H * C].rearrange("p (h c) -> p h c", h=H), [pt], [dst])
                yield
            pp = pm()
            for h in range(H):
                mm(pp[:C, h * C:(h + 1) * C], rkT[:, h, 0:C], rqT[:, h, 0:C], [rkT, rqT], [pp])
            PTm = tmp([128, H, 128], BF16, tag="PTm", n=1)
            tt("dve", PTm[:C, :, 0:C], pp[:C, 0:H * C].rearrange("p (h c) -> p h c", h=H),
               cmT[:C, :C].unsqueeze(1).broadcast_to([C, H, C]), ALU.mult, [pp, cmT], [PTm])
            yield
            po = PACC[0]
            for h in range(H):
                gam = 1.0 - 2.0 ** (-5.0 - h)
                if sample:
                    i2 = h % 2
                    P.dma("sp", S0[i2][:, :, 0:128], sR[l, :, h].rearrange("b k v -> k b v"), writes=[S0[i2]])
                    cp("act", S0b[i2][:, :, 0:128], S0[i2][:, :, 0:128], [S0[i2]], [S0b[i2]])
                    qTm = masked_cols(rqT[:, h, 0:C], [rqT])
                    kdm = masked_rows(rk_tm[j][:C, h, :], 128, [rk_tm[j]])
                mm(po[:C, h * 128:(h + 1) * 128], PTm[:C, h, 0:C], rv_tm[j][:C, h, :], [PTm, rv_tm[j]], [po], start=True, stop=False)
                for b in range(nblk):
                    if sample:
                        mm(po[:C, h * 128:(h + 1) * 128], qTm[:, b, :], S0b[i2][:, b, 0:128], [qTm, S0b[i2]], [po],
                           start=False, stop=(b == nblk - 1))
                    else:
                        mm(po[:C, h * 128:(h + 1) * 128], rqT[:, h, 0:C], Rrtb[:, h, :], [rqT, Rrtb], [po], start=False, stop=True)
                if sample:
                    gC = gam ** ST
                    for b4 in range(0, nblk, 4):
                        pu = pm()
                        for bb in range(4):
                            mm(pu[:, bb * 128:(bb + 1) * 128], kdm[:C, b4 + bb, 0:128], rv_tm[j][:C, h, :], [kdm, rv_tm[j]], [pu])
                        tq = tmp([128, 512], tag="rtmp", n=1)
                        tt("dve", tq[:, :].rearrange("p (b v) -> p b v", b=4), pu[:, 0:512].rearrange("p (b v) -> p b v", b=4),
                           S0[i2][:, b4:b4 + 4, 0:128], ALU.add, [pu, S0[i2]], [tq])
                        act(S1[i2][:, b4:b4 + 4, :], tq[:, :].rearrange("p (b v) -> p b v", b=4), AF.Copy, [tq], [S1[i2]], scale=gC)
                    P.dma("pool", o_sR[l, :, h].rearrange("b k v -> k b v"), S1[i2][:, :, :], reads=[S1[i2]], final=True)
            branch_out(po[:C, 0:512].rearrange("p (h v) -> p h v", h=H), True, C, gsl[j], oT[1], j * 128, [po])
            yield
            if not sample:
                pu = pm()
                for h in range(H):
                    mm(pu[:, h * 128:(h + 1) * 128], rk_tm[j][:C, h, :], rv_tm[j][:C, h, :], [rk_tm[j], rv_tm[j]], [pu])
                tq = tmp([128, 512], tag="rtmp", n=1)
                tt("dve", tq[:, :], pu[:, 0:512], Rrt[:].rearrange("p h v -> p (h v)"), ALU.add, [pu, Rrt], [tq])
                for h in range(H):
                    gC = (1.0 - 2.0 ** (-5.0 - h)) ** C
                    act(Rrt[:, h, :], tq[:, h * 128:(h + 1) * 128], AF.Copy, [tq], [Rrt], scale=gC)
                cp("dve", Rrtb[:], Rrt[:], [Rrt], [Rrtb])
                if last_chunk:
                    P.dma("pool", o_pR[l].rearrange("h k v -> k h v"), Rrt[:], reads=[Rrt], final=True)


        def gen_ml():
            P.tag = tg0 + ".ml_gate"
            g2 = tmp([128, 64], tag="g2", n=2)
            if sample:
                for t4 in range(ST):
                    P.dma("sp", msp[t4::ST, :], sM[l], writes=[msp])
                mprev = msp
            else:
                mprev = mcar
            tt("dve", g2[:C, 0:4], sm[:C, 8:12], rowc[:C, 8:12], ALU.add, [sm, rowc], [g2])
            tt("dve", g2[:C, 4:8], sm[:C, 12:16], rowc[:C, 12:16], ALU.add, [sm, rowc], [g2])
            act(g2[:C, 8:12], g2[:C, 4:8], AF.Exp, [g2], [g2], scale=-1.0)
            act(g2[:C, 12:16], g2[:C, 8:12], AF.Ln, [g2], [g2], bias=1.0)
            SUB = int(os.environ.get("KDBG_SUB", "99"))
            if SUB <= 1:
                return
            p1 = pm()
            mm(p1[:C, 0:4], Uc[:C, :C], g2[:C, 12:16], [Uc, g2], [p1])
            act(g2[:C, 16:20], p1[:C, 0:4], AF.Copy, [p1], [g2], scale=-1.0)
            if SUB <= 2:
                return
            tt("dve", g2[:C, 20:24], g2[:C, 0:4], g2[:C, 16:20], ALU.subtract, [g2], [g2])
            yield
            pls = []
            for h in range(H):
                dA = tmp([128, 128], tag="dA", n=2)
                ts("dve", dA[:C, :C], identf[:C, :C], g2[:C, 20 + h:21 + h], ALU.mult, [identf, g2], [dA])
                if SUB <= 3:
                    continue
                pl = PS[4 + h] if False else pm()
                mm(pl[:C, 0:C], onesf[:C, :C], dA[:C, :C], [onesf, dA], [pl], start=True, stop=False)
                mm(pl[:C, 0:C], identb[:C, :C], maskb[:C, :C], [identb, maskb], [pl], start=False, stop=True)
                if SUB <= 4:
                    continue
                P.op("dve", (lambda pl, h: lambda e: e.tensor_reduce(out=g2[:C, 24 + h:25 + h], in_=pl[:C, 0:C], axis=AX.X, op=ALU.max))(pl, h),
                     reads=[pl], writes=[g2])
                if SUB <= 5:
                    continue
                Lat = tmp([128, 128], tag="Lat", n=4)
                cp(os.environ.get("KDBG_LATENG", "act"), Lat[:C, :C], pl[:C, 0:C], [pl], [Lat])
                pls.append(Lat)
                yield
            if DBG_ML <= 1:
                return
            tt("dve", g2[:C, 28:32], g2[:C, 24:28], mprev[:C, 0:4], ALU.max, [g2, mprev], [g2])
            ts("dve", g2[:C, 32:36], g2[:C, 28:32], -1.0, ALU.mult, [g2], [g2])
            tt("dve", g2[:C, 36:40], mprev[:C, 0:4], g2[:C, 28:32], ALU.subtract, [g2, mprev], [g2])
            act(g2[:C, 36:40], g2[:C, 36:40], AF.Exp, [g2], [g2])
            tt("dve", g2[:C, 40:44], g2[:C, 16:20], g2[:C, 28:32], ALU.add, [g2], [g2])
            act(g2[:C, 44:48], g2[:C, 40:44], AF.Exp, [g2], [g2], scale=-1.0)
            p2 = pm()
            mm(p2[:C, 0:4], lastsel[:C, :C], g2[:C, 32:36], [lastsel, g2], [p2])
            tt("dve", g2[:C, 48:52], p2[:C, 0:4], g2[:C, 20:24], ALU.add, [p2, g2], [g2])
            act(g2[:C, 48:52], g2[:C, 48:52], AF.Exp, [g2], [g2])
            tt("dve", g2[:C, 52:56], mprev[:C, 0:4], g2[:C, 32:36], ALU.add, [g2, mprev], [g2])
            xb = tmp([128, 2, NSB * 4], tag="xb", n=2)
            tt("dve", xb[:C, 0, 0:nb4].rearrange("p (b h) -> p b h", h=H),
               g2[:C, 52:56].unsqueeze(1).broadcast_to([C, nblk, H]),
               lastind[:C, 0:nblk].unsqueeze(2).broadcast_to([C, nblk, H]), ALU.mult, [g2, lastind], [xb])
            tt("dve", xb[:C, 1, 0:nb4].rearrange("p (b h) -> p b h", h=H),
               g2[:C, 40:44].unsqueeze(1).broadcast_to([C, nblk, H]),
               lastind[:C, 0:nblk].unsqueeze(2).broadcast_to([C, nblk, H]), ALU.mult, [g2, lastind], [xb])
            mm(p2[:, 64:64 + nb4], onesf[:C, :], xb[:C, 0, 0:nb4], [onesf, xb], [p2])
            mm(p2[:, 128:128 + nb4], onesf[:C, :], xb[:C, 1, 0:nb4], [onesf, xb], [p2])
            sdec = tmp([128, NSB * 4], tag="sdec", n=2)
            act(sdec[:, 0:nb4], p2[:, 64:64 + nb4], AF.Exp, [p2], [sdec])
            mnew = tmp([128, NSB * 4], tag="mnew", n=2)
            cp("act", mnew[:, 0:nb4], p2[:, 128:128 + nb4], [p2], [mnew])
            yield
            if DBG_ML <= 2:
                return
            mk_tm = tmp([128, H, 128], BF16, tag="mk_tm", n=1)
            pt = pm()
            pv = psum_bf(pt)
            for h in range(H):
                tr(pv[:C, h * 128:(h + 1) * 128], mkT[:, h, cs], identb[:, :], [mkT, identb], [pt])
            cp("act", mk_tm[:C].rearrange("p h v -> p (h v)"), pv[:C, 0:512], [pt], [mk_tm])
            yield
            P.tag = tg0 + ".ml_head"
            hall = tmp([128, H, 128], tag="o_all", n=2)
            rs = tmp([128, 16], tag="rs", n=2)
            nt_s = tmp([128, NSB * 4], tag="nt_s", n=1)
            for h in range(H):
                E = tmp([128, 128], tag="E", n=2)
                act(E[:C, :C], pls[h][:C, :C], AF.Exp, [pls[h], g2], [E], bias=g2[:C, 32 + h:33 + h])
                pq = pm()
                mm(pq[:C, 0:C], mqT[:, h, cs], mkT[:, h, cs], [mqT, mkT], [pq])
                Dm = tmp([128, 128], BF16, tag="Dm", n=2)
                stt(Dm[:C, :C], pq[:C, 0:C], 1.0, E[:C, :C], ALU.mult, ALU.mult, [pq, E], [Dm, rs], accum=rs[:C, h:h + 1])
                yield
                pt = pm()
                pv = psum_bf(pt)
                tr(pv[:C, 0:C], Dm[:C, :C], identb[:C, :C], [Dm, identb], [pt])
                DmT = tmp([128, 128], BF16, tag="DmT", n=2)
                cp("act", DmT[:C, :C], pv[:C, 0:C], [pt], [DmT])
                yield
                if DBG_ML <= 3:
                    continue
                kw = tmp([128, 128], BF16, tag="kw", n=2)
                act(kw[:C, :], mk_tm[:C, h, :], AF.Copy, [mk_tm, g2], [kw], scale=g2[:C, 48 + h:49 + h])
                if sample:
                    i2 = h % 2
                    P.dma("sp", S0[i2][:, :, 0:128], sC[l, :, h].rearrange("b k v -> k b v"), writes=[S0[i2]])
                    P.dma("sp", S0[i2][:, :, 128], sN[l, :, h, :].rearrange("b k -> k b"), writes=[S0[i2]],
                          allow_slow_non_contiguous=True)
                    cp("act", S0b[i2][:], S0[i2][:], [S0[i2]], [S0b[i2]])
                    qTm = masked_cols(mqT[:, h, cs], [mqT])
                    kwm = masked_rows(kw[:C, :], 128, [kw])
                pi_ = pm()
                mm(pi_[:C, 0:128], DmT[:C, :C], vaug[j][:C, h, 0:128], [DmT, vaug[j]], [pi_])
                pz = pm()
                for b in range(nblk):
                    if sample:
                        mm(pz[:C, 0:129], qTm[:, b, :], S0b[i2][:, b, :], [qTm, S0b[i2]], [pz], start=(b == 0), stop=(b == nblk - 1))
                    else:
                        mm(pz[:C, 0:129], mqT[:, h, cs], Cmlb[:, h, :], [mqT, Cmlb], [pz])
                isb = tmp([128, 128], tag="isb", n=2)
                cp("act", isb[:C, :], pi_[:C, 0:128], [pi_], [isb])
                numt = tmp([128, 128], tag="numt", n=2)
                stt(numt[:C, :], pz[:C, 0:128], g2[:C, 36 + h:37 + h], isb[:C, :], ALU.mult, ALU.add, [pz, g2, isb], [numt])
                stt(rs[:C, 4 + h:5 + h], pz[:C, 128:129], g2[:C, 36 + h:37 + h], rs[:C, h:h + 1], ALU.mult, ALU.add,
                    [pz, g2, rs], [rs])
                act(rs[:C, 8 + h:9 + h], rs[:C, 4 + h:5 + h], AF.Abs, [rs], [rs])
                tt("dve", rs[:C, 8 + h:9 + h], rs[:C, 8 + h:9 + h], g2[:C, 44 + h:45 + h], ALU.max, [rs, g2], [rs])
                P.op("dve", (lambda h: lambda e: e.reciprocal(out=rs[:C, 12 + h:13 + h], in_=rs[:C, 8 + h:9 + h]))(h),
                     reads=[rs], writes=[rs])
                ts("dve", hall[:C, h, :], numt[:C, :], rs[:C, 12 + h:13 + h], ALU.mult, [numt, rs], [hall])
                yield
                if DBG_ML <= 4:
                    continue
                if sample:
                    for b in range(nblk):
                        pu = pm()
                        mm(pu[:, 0:129], kwm[:C, b, 0:128], vaug[j][:C, h, :], [kwm, vaug[j]], [pu])
                        stt(S1[i2][:, b, :], S0[i2][:, b, 0:128], sdec[:, b * 4 + h:b * 4 + h + 1], pu[:, 0:128], ALU.mult, ALU.add,
                            [S0[i2], sdec, pu], [S1[i2]])
                        stt(nt_s[:, b * 4 + h:b * 4 + h + 1], S0[i2][:, b, 128:129], sdec[:, b * 4 + h:b * 4 + h + 1],
                            pu[:, 128:129], ALU.mult, ALU.add, [S0[i2], sdec, pu], [nt_s])
                    P.dma("pool", o_sC[l, :, h].rearrange("b k v -> k b v"), S1[i2][:, :, :], reads=[S1[i2]], final=True)
                else:
                    pu = pm()
                    mm(pu[:, 0:129], kw[:C, :], vaug[j][:C, h, :], [kw, vaug[j]], [pu])
                    stt(Cml[:, h, :], Cml[:, h, :], sdec[:, h:h + 1], pu[:, 0:129], ALU.mult, ALU.add, [Cml, sdec, pu], [Cml])
                    cp("act", Cmlb[:, h, :], Cml[:, h, :], [Cml], [Cmlb])
                    yield
            if sample:
                pt = pm()
                tr(pt[:64, 0:128], nt_s[:, 0:64], identf[:, :], [nt_s, identf], [pt])
                nto = tmp([64, 128], tag="nto", n=1)
                cp("dve", nto[:, :], pt[:64, 0:128], [pt], [nto])
                P.dma("pool", o_sN[l], nto[:, :], reads=[nto], final=True)
                P.dma("pool", o_sM[l:l + 1, :], mnew[0:1, 0:64], reads=[mnew], final=True)
            else:
                cp("dve", mcar[:, :], mnew[:, 0:4], [mnew], [mcar])
                if last_chunk:
                    P.dma("pool", o_pC[l].rearrange("h k v -> k h v"), Cml[:, :, 0:128], reads=[Cml], final=True)
                    nt_p = tmp([128, 4], tag="nt_p", n=1)
                    cp("pool", nt_p[:, :], Cml[:, :, 128], [Cml], [nt_p])
                    pt = pm()
                    tr(pt[:4, 0:128], nt_p[:, 0:4], identf[:, :], [nt_p, identf], [pt])
                    nto = tmp([64, 128], tag="nto", n=1)
                    cp("dve", nto[:4, :], pt[:4, 0:128], [pt], [nto])
                    P.dma("pool", o_pN[l], nto[:4, :], reads=[nto], final=True)
                    P.dma("pool", o_pM[l:l + 1, :], mnew[0:1, 0:4], reads=[mnew], final=True)
            branch_out(hall[:C], False, C, osg[j], oT[2], j * 128, [hall])


        gens = [gen_dn()]
        if DBG_MIX > 2:
            gens.append(gen_ret())
        if DBG_MIX > 3:
            gens.append(gen_ml())
        if sample or os.environ.get("KDBG_NOILV"):
            for g in gens:
                for _ in g:
                    pass
        else:
            live = list(gens)
            dn_g = gens[0]
            while live:
                for g in list(live):
                    try:
                        next(g)
                        if g is dn_g and len(live) > 1:
                            next(g)
                    except StopIteration:
                        live.remove(g)

    for l in range(DEPTH):
        for sc in range(NPASS + 1):
            if npass_done[0] >= DBG_NP:
                break
            if DBG_SAMPLE and sc != NPASS:
                continue
            layer_pass(l, sc)
            npass_done[0] += 1
    if os.environ.get("KDBG_DUMP"):
        import json
        json.dump(P.tags, open(os.environ["KDBG_DUMP"], "w"))
    P.build()
    return nc


_CACHE = {}
_REG = {}


def kernel(x_prompt, x_sample, state_dn_conv, state_dn_S, state_ret_R, state_ml_C, state_ml_n, state_ml_m,
           w_in, dn_conv_w, dn_A_log, dn_dt_bias, dn_norm_w, ml_i_bias, ml_f_bias, ml_norm_w,
           w_br_a, w_br_b, w_br_c, w_out, ln1_g, ln1_b, w_ff1, w_ff2, ln2_g, ln2_b):
    f = lambda a: np.ascontiguousarray(np.asarray(a, dtype=np.float32))
    consts = _host_consts()
    if "nc" not in _CACHE:
        _CACHE["nc"] = build_program(consts)
    nc = _CACHE["nc"]
    shared = dict(w_in=f(w_in), conv_w=f(dn_conv_w), A_log=f(dn_A_log), dt_bias=f(dn_dt_bias), dn_nw=f(dn_norm_w),
                  ibias=f(ml_i_bias), fbias=f(ml_f_bias), ml_nw=f(ml_norm_w), w_br_a=f(w_br_a), w_br_b=f(w_br_b),
                  w_br_c=f(w_br_c), w_out=f(w_out), ln1g=f(ln1_g), ln1b=f(ln1_b), ln2g=f(ln2_g), ln2b=f(ln2_b),
                  w_ff1=f(w_ff1), w_ff2=f(w_ff2))
    for k, v in consts.items():
        shared["c_" + k] = f(v)
    in_maps = []
    for c in range(NCORES):
        b0 = c * NSB
        m = dict(shared)
        m["xp"] = f(x_prompt[c])
        m["xs"] = f(x_sample[b0:b0 + NSB]).reshape(NSB * ST, D)
        m["cconv"] = f(state_dn_conv[:, b0:b0 + NSB]).reshape(DEPTH, NSB * 3, 1536)
        m["sS"] = f(state_dn_S[:, b0:b0 + NSB])
        m["sR"] = f(state_ret_R[:, b0:b0 + NSB])
        m["sC"] = f(state_ml_C[:, b0:b0 + NSB])
        m["sN"] = f(state_ml_n[:, b0:b0 + NSB])
        m["sM"] = f(state_ml_m[:, b0:b0 + NSB])
        in_maps.append(m)
    ncr = int(os.environ.get("KDBG_CORES", str(NCORES)))
    res = run_bass_kernel_spmd(nc, in_maps[:ncr], core_ids=list(range(ncr)))
    R = list(res.results)
    while len(R) < NCORES:
        R.append(R[0])
    cat = lambda key, shp, ax: np.concatenate([np.asarray(R[c][key], dtype=np.float32).reshape(shp) for c in range(NCORES)], axis=ax)
    y_prompt = cat("y_p", (1, SEQ, D), 0)
    y_sample = cat("y_s", (NSB, ST, D), 0)
    p_conv = cat("o_pconv", (DEPTH, 1, 3, 1536), 1)
    p_S = cat("o_pS", (DEPTH, 1, H, 128, 128), 1)
    p_R = cat("o_pR", (DEPTH, 1, H, 128, 128), 1)
    p_C = cat("o_pC", (DEPTH, 1, H, 128, 128), 1)
    p_n = cat("o_pN", (DEPTH, 1, H, 128), 1)
    p_m = cat("o_pM", (DEPTH, 1, H), 1)
    s_conv = cat("o_sconv", (DEPTH, NSB, 3, 1536), 1)
    s_S = cat("o_sS", (DEPTH, NSB, H, 128, 128), 1)
    s_R = cat("o_sR", (DEPTH, NSB, H, 128, 128), 1)
    s_C = cat("o_sC", (DEPTH, NSB, H, 128, 128), 1)
    s_n = cat("o_sN", (DEPTH, NSB, H, 128), 1)
    s_m = cat("o_sM", (DEPTH, NSB, H), 1)
    return (y_prompt, y_sample, p_conv, p_S, p_R, p_C, p_n, p_m, s_conv, s_S, s_R, s_C, s_n, s_m)
```

```python
import os
import numpy as np
import concourse.bass as bass
import concourse.mybir as mybir
from concourse.bass_utils import run_bass_kernel_spmd

F32 = mybir.dt.float32
BF16 = mybir.dt.bfloat16
AF = mybir.ActivationFunctionType
ALU = mybir.AluOpType
AX = mybir.AxisListType

NDMASEM = 48
NCORES = 8
D = 1024
H = 4
DEPTH = 2
SEQ = 2048
PAST = 16384
NSB = 16
ST = 4
ALPHA = (2 * DEPTH) ** 0.25
LN_EPS = 1e-5
RMS_EPS = 1e-6
PW = 9232
OFF = dict(a_qkv=0, a_z=1536, a_beta=2048, a_decay=2052, r_q=2056, r_k=2568, r_v=3080, r_g=3592,
           m_q=4104, m_k=4616, m_v=5128, m_o=5640, m_i=6152, m_f=6156, gates=6160)
NTP = 256
NPASS = SEQ // NTP
BIG = 30000.0


class Dep:
    __slots__ = ("w", "r", "excl")

    def __init__(self):
        self.w = None
        self.r = []
        self.excl = False


class Op:
    __slots__ = ("eng", "fn", "deps", "sig", "cnt", "dma", "dsem", "dval", "reuse")

    def __init__(self, eng, fn, dma):
        self.eng = eng
        self.fn = fn
        self.dma = dma
        self.deps = ()
        self.sig = dma
        self.cnt = 0
        self.dsem = None
        self.dval = 0
        self.reuse = None


def _flat(xs):
    out = []
    for x in xs:
        if isinstance(x, Dep):
            out.append(x)
        elif hasattr(x, "ds"):
            out.extend(x.ds)
        else:
            out.append(x.d)
    return out


class Vw:
    def __init__(self, ap, deps):
        self.t = ap
        self.ds = deps

    def __getitem__(self, k):
        return self.t[k]


class Prog:
    ENG = ("pe", "dve", "act", "pool", "sp")

    def __init__(self, nc):
        self.nc = nc
        self.ops = []
        self.final = []
        self.tag = ""
        self.tags = []

    def op(self, eng, fn, reads=(), writes=(), dma=False):
        reads = _flat(reads)
        writes = _flat(writes)
        for d in reads:
            if d.excl and d not in writes:
                writes.append(d)
        i = len(self.ops)
        o = Op(eng, fn, dma)
        self.tags.append((eng, self.tag))
        deps = set()
        hard = set()
        for d in reads:
            if d.w is not None:
                deps.add(d.w)
                hard.add(d.w)
        for d in writes:
            if d.w is not None:
                deps.add(d.w)
                hard.add(d.w)
            deps.update(d.r)
        keep = []
        latest = {}
        for p in deps:
            po = self.ops[p]
            if po.dma:
                keep.append(p)
                continue
            if (not dma) and po.eng == eng:
                if eng == "pe" or p not in hard:
                    continue
            if latest.get(po.eng, -1) < p:
                latest[po.eng] = p
        keep.extend(latest.values())
        for p in keep:
            self.ops[p].sig = True
        o.deps = tuple(sorted(keep))
        for d in reads:
            d.r.append(i)
        for d in writes:
            d.w = i
            d.r = []
        self.ops.append(o)
        return i

    def dma(self, q, out, in_, reads=(), writes=(), final=False, **kw):
        def fn(e):
            return e.dma_start(out=out, in_=in_, **kw)
        i = self.op(q, fn, reads, writes, dma=True)
        if final:
            self.final.append(i)
        return i

    def build(self):
        nc = self.nc
        esem = {e: nc.alloc_semaphore("sem_" + e) for e in self.ENG}
        dsems = [nc.alloc_semaphore("dsem%d" % i) for i in range(NDMASEM)]
        cnt = {e: 0 for e in self.ENG}
        slot_used = [False] * NDMASEM
        slot_val = [0] * NDMASEM
        NSW = 20
        kq = {"sw": 0, "hw": 0}
        for o in self.ops:
            if o.dma:
                if o.eng == "pool":
                    s = kq["sw"] % NSW
                    kq["sw"] += 1
                else:
                    s = NSW + kq["hw"] % (NDMASEM - NSW)
                    kq["hw"] += 1
                o.dsem = dsems[s]
                if slot_used[s]:
                    o.reuse = (dsems[s], slot_val[s])
                slot_used[s] = True
                slot_val[s] += 16
                o.dval = slot_val[s]
            elif o.sig:
                cnt[o.eng] += 1
                o.cnt = cnt[o.eng]
        ops = self.ops
        finals = [(ops[i].dsem, ops[i].dval) for i in self.final]
        if os.environ.get("KDBG_DUMP"):
            import json
            json.dump([(o.eng, o.cnt, self.tags[i][1]) for i, o in enumerate(ops) if o.sig and not o.dma],
                      open(os.environ["KDBG_DUMP"] + ".cnt", "w"))

        def emit(ename):
            def body(e):
                waited = {}

                def wait(sem, val):
                    key = id(sem)
                    if waited.get(key, 0) >= val:
                        return
                    waited[key] = val
                    e.wait_ge(sem, val)

                for o in ops:
                    if o.eng != ename:
                        continue
                    for p in o.deps:
                        po = ops[p]
                        if po.dma:
                            wait(po.dsem, po.dval)
                        else:
                            wait(esem[po.eng], po.cnt)
                    if o.reuse is not None:
                        wait(*o.reuse)
                    inst = o.fn(e)
                    if o.dma:
                        inst.then_inc(o.dsem, 16)
                    elif o.sig:
                        inst.then_inc(esem[ename], 1)
                if ename == "pool":
                    for (s, v) in finals:
                        wait(s, v)
            return body

        with nc.Block() as block:
            block.tensor(emit("pe"))
            block.vector(emit("dve"))
            block.scalar(emit("act"))
            block.gpsimd(emit("pool"))
            block.sync(emit("sp"))


class Tl:
    def __init__(self, nc, name, shape, dt=F32, psum=False):
        if psum:
            self.t = nc.alloc_psum_tensor(name, list(shape), dt)
        else:
            self.t = nc.alloc_sbuf_tensor(name, list(shape), dt)
        self.d = Dep()
        self.d.excl = psum

    def __getitem__(self, k):
        return self.t[k]


class DT_:
    def __init__(self, ap):
        self.ap = ap
        self.d = Dep()


def _host_consts():
    c = {}
    f = np.float32
    idx = np.arange(128)
    c["identf"] = np.eye(128, dtype=f)
    c["onesf"] = np.ones((128, 128), f)
    for m, C, blk in (("p", 128, 128), ("s", 64, 4)):
        i = np.arange(C)
        b = i // blk
        same = (b[:, None] == b[None, :])
        c["U_" + m] = (same & (i[:, None] <= i[None, :])).astype(f)
        c["L_" + m] = (same & (i[:, None] > i[None, :])).astype(f)
        last = (b + 1) * blk - 1
        c["lastsel_" + m] = (i[:, None] == last[None, :]).astype(f)
        nb = C // blk
        c["lastind_" + m] = (i[:, None] == (np.arange(nb)[None, :] + 1) * blk - 1).astype(f)
        c["blkind_" + m] = (b[:, None] == np.arange(nb)[None, :]).astype(f)
        c["strictT_" + m] = (same & (i[:, None] < i[None, :])).astype(f)
        c["cmT_" + m] = (same & (i[:, None] <= i[None, :])).astype(f)
        c["maskb_" + m] = np.where(same & (i[None, :] <= i[:, None]), 0.0, -BIG).astype(f)
        c["maskbT_" + m] = np.where(same & (i[:, None] <= i[None, :]), 0.0, -BIG).astype(f)
    cm = np.zeros((128, NSB, 64), f)
    for bb in range(NSB):
        cm[:, bb, bb * ST:(bb + 1) * ST] = 1.0
    c["colmask"] = cm.reshape(128, NSB * 64)
    half = 64
    inv_freq = (1.0 / (np.float32(10000.0) ** np.linspace(0.0, 1.0, half, dtype=f))).astype(f)
    lg = np.log(1.0 - 2.0 ** (-5.0 - np.arange(H, dtype=np.float64)))
    tab = np.zeros((17, 128, 4, H, half), f)
    for ci in range(17):
        if ci < 16:
            pos = (ci * 128 + idx).astype(f)
            tb = idx.astype(np.float64)
        else:
            pos = np.zeros(128, f)
            tb = np.zeros(128)
            pos[:64] = (PAST + (np.arange(64) % ST)).astype(f)
            tb[:64] = (np.arange(64) % ST)
        ang = (pos[:, None] * inv_freq[None, :]).astype(f)
        cs, sn = np.cos(ang.astype(np.float64)), np.sin(ang.astype(np.float64))
        for h in range(H):
            qs = np.exp((tb + 1.0) * lg[h])[:, None]
            ks = np.exp(-(tb + 1.0) * lg[h])[:, None] * (128.0 ** -0.5)
            tab[ci, :, 0, h] = cs * qs
            tab[ci, :, 1, h] = sn * qs
            tab[ci, :, 2, h] = cs * ks
            tab[ci, :, 3, h] = sn * ks
    c["rot"] = tab.reshape(17, 128, 4 * H * half)
    return c


_F32C = ["identf", "onesf", "U_p", "L_p", "lastsel_p", "lastind_p", "blkind_p", "strictT_p", "cmT_p",
         "U_s", "L_s", "lastsel_s", "lastind_s", "blkind_s", "strictT_s", "cmT_s"]
_BF16C = ["maskb_p", "maskbT_p", "maskb_s", "maskbT_s", "colmask", "identf", "onesf", "blkind_s"]


def build_program(consts):
    nc = bass.Bass("TRN2", target_bir_lowering=False)
    P = Prog(nc)
    uid = [0]

    def nm(s):
        uid[0] += 1
        return "%s_%d" % (s, uid[0])

    def din(name, shape, dt=F32):
        return nc.dram_tensor(name, list(shape), dt, kind="ExternalInput").ap()

    def dout(name, shape):
        return nc.dram_tensor(name, list(shape), F32, kind="ExternalOutput").ap()

    def dscr(name, shape, dt):
        return nc.dram_tensor(name, list(shape), dt, kind="Internal").ap()

    def T(name, shape, dt=F32):
        n_ = nm(name)
        _REG.setdefault(name, []).append(n_)
        return Tl(nc, n_, shape, dt)

    xp = din("xp", [SEQ, D])
    xs = din("xs", [NSB * ST, D])
    cconv = din("cconv", [DEPTH, NSB * 3, 1536])
    sS = din("sS", [DEPTH, NSB, H, 128, 128])
    sR = din("sR", [DEPTH, NSB, H, 128, 128])
    sC = din("sC", [DEPTH, NSB, H, 128, 128])
    sN = din("sN", [DEPTH, NSB, H, 128])
    sM = din("sM", [DEPTH, NSB, H])
    w_in = din("w_in", [DEPTH, D, PW])
    conv_w = din("conv_w", [DEPTH, 4, 1536])
    A_log = din("A_log", [DEPTH, H])
    dt_bias = din("dt_bias", [DEPTH, H])
    dn_nw = din("dn_nw", [DEPTH, 128])
    ibias = din("ibias", [DEPTH, H])
    fbias = din("fbias", [DEPTH, H])
    ml_nw = din("ml_nw", [DEPTH, 512])
    w_br = [din("w_br_" + s, [DEPTH, 512, D]) for s in "abc"]
    w_out = din("w_out", [DEPTH, D, D])
    ln1g = din("ln1g", [DEPTH, D]); ln1b = din("ln1b", [DEPTH, D])
    ln2g = din("ln2g", [DEPTH, D]); ln2b = din("ln2b", [DEPTH, D])
    w_ff1 = din("w_ff1", [DEPTH, D, 4 * D])
    w_ff2 = din("w_ff2", [DEPTH, 4 * D, D])
    cin = {k: din("c_" + k, list(consts[k].shape)) for k in consts}

    y_p = dout("y_p", [SEQ, D])
    y_s = dout("y_s", [NSB * ST, D])
    o_pconv = dout("o_pconv", [DEPTH, 3, 1536])
    o_pS = dout("o_pS", [DEPTH, H, 128, 128])
    o_pR = dout("o_pR", [DEPTH, H, 128, 128])
    o_pC = dout("o_pC", [DEPTH, H, 128, 128])
    o_pN = dout("o_pN", [DEPTH, H, 128])
    o_pM = dout("o_pM", [DEPTH, H])
    o_sconv = dout("o_sconv", [DEPTH, NSB * 3, 1536])
    o_sS = dout("o_sS", [DEPTH, NSB, H, 128, 128])
    o_sR = dout("o_sR", [DEPTH, NSB, H, 128, 128])
    o_sC = dout("o_sC", [DEPTH, NSB, H, 128, 128])
    o_sN = dout("o_sN", [DEPTH, NSB * H, 128])
    o_sM = dout("o_sM", [DEPTH, NSB * H])

    xmid = DT_(dscr("xmid", [SEQ + NSB * ST, D], F32))
    xmid_d = [Dep() for _ in range(17)]

    wsrc, wscr = {}, {}
    for l in range(DEPTH):
        for key, src in ((("in", l), w_in[l]), (("br0", l), w_br[0][l]), (("br1", l), w_br[1][l]),
                         (("br2", l), w_br[2][l]), (("out", l), w_out[l]), (("ff1", l), w_ff1[l]), (("ff2", l), w_ff2[l])):
            wsrc[key] = src

    def block_plan(l):
        bl = []
        for blk in range(3):
            bl.append((("in", l), 0, 8, OFF["a_qkv"] + blk * 512, 512))
        for nm_, w_ in (("m_q", 512), ("m_k", 512), ("a_z", 512), ("a_beta", 8), ("r_q", 512), ("r_k", 512), ("r_v", 512),
                        ("r_g", 512), ("m_v", 512), ("m_o", 512), ("m_i", 8)):
            bl.append((("in", l), 0, 8, OFF[nm_], w_))
        for br in range(3):
            bl.append((("br%d" % br, l), 0, 4, 0, D))
            for half in range(2):
                bl.append((("in", l), 0, 8, OFF["gates"] + br * D + half * 512, 512))
        for half in range(2):
            bl.append((("out", l), 0, 8, half * 512, 512))
        for hb in range(8):
            bl.append((("ff1", l), 0, 8, hb * 512, 512))
        for half in range(2):
            for q in range(4):
                bl.append((("ff2", l), q * 8, 8, half * 512, 512))
        return bl

    plans = [block_plan(l) for l in range(DEPTH)]
    plan_idx = [{sp: i for i, sp in enumerate(plans[l])} for l in range(DEPTH)]
    cast_dep = {}
    cast_cur = [0] * DEPTH
    LOOKAHEAD = 6

    def emit_cast(l):
        if cast_cur[l] >= len(plans[l]):
            return False
        sp = plans[l][cast_cur[l]]
        cast_cur[l] += 1
        key, k0, kc, c0, ncols = sp
        d = Dep()
        scr = dscr("wsc_%d_%d" % (l, cast_cur[l]), [128, kc * ncols], BF16)
        wscr[sp] = scr
        P.dma("pool", scr.rearrange("p (k e) -> p k e", k=kc),
              wsrc[key].rearrange("(k p) e -> p k e", p=128)[:, k0:k0 + kc, c0:c0 + ncols], writes=[d])
        cast_dep[sp] = d
        return True

    K = {}
    for k in _F32C:
        sh = consts[k].shape
        K[k] = T("k_" + k, list(sh))
        P.dma("sp", K[k][:], cin[k], writes=[K[k]])
    KB = {}
    for k in _BF16C:
        sh = consts[k].shape
        KB[k] = T("kb_" + k, list(sh), BF16)
        P.dma("pool", KB[k][:], cin[k], writes=[KB[k]])
    identf, onesf = K["identf"], K["onesf"]
    identb, onesb = KB["identf"], KB["onesf"]

    PS = [Tl(nc, "psb%d" % i, [128, 512], F32, psum=True) for i in range(8)]
    rr = {"d": 0, "m": 0}

    def pd():
        rr["d"] = (rr["d"] + 1) % 4
        return PS[rr["d"]]

    def pm():
        rr["m"] = (rr["m"] + 1) % 4
        return PS[4 + rr["m"]]

    PACC = [PS[2], PS[3]]

    def mm(out, lhsT, rhs, R, W, start=True, stop=True):
        P.op("pe", lambda e: e.matmul(out, lhsT=lhsT, rhs=rhs, start=start, stop=stop, skip_group_check=True),
             reads=R, writes=W)

    def tr(out, in_, ident, R, W):
        P.op("pe", lambda e: e.transpose(out, in_, ident), reads=R, writes=W)

    def act(out, in_, func, R, W, **kw):
        P.op("act", lambda e: e.activation(out=out, in_=in_, func=func, **kw), reads=R, writes=W)

    def cp(eng, out, in_, R, W):
        if eng == "act":
            P.op("act", lambda e: e.copy(out=out, in_=in_), reads=R, writes=W)
        else:
            P.op(eng, lambda e: e.tensor_copy(out=out, in_=in_), reads=R, writes=W)

    def tt(eng, out, in0, in1, op, R, W):
        P.op(eng, lambda e: e.tensor_tensor(out=out, in0=in0, in1=in1, op=op), reads=R, writes=W)

    def ts(eng, out, in0, s1, op0, R, W, s2=None, op1=None):
        if op1 is None:
            P.op(eng, lambda e: e.tensor_scalar(out=out, in0=in0, scalar1=s1, scalar2=None, op0=op0), reads=R, writes=W)
        else:
            P.op(eng, lambda e: e.tensor_scalar(out=out, in0=in0, scalar1=s1, scalar2=s2, op0=op0, op1=op1),
                 reads=R, writes=W)

    def stt(out, in0, scalar, in1, op0, op1, R, W, accum=None):
        if accum is None:
            P.op("dve", lambda e: e.scalar_tensor_tensor(out=out, in0=in0, scalar=scalar, in1=in1, op0=op0, op1=op1),
                 reads=R, writes=W)
        else:
            P.op("dve", lambda e: e.scalar_tensor_tensor(out=out, in0=in0, scalar=scalar, in1=in1, op0=op0, op1=op1,
                                                         accum_out=accum), reads=R, writes=W)

    def memset(eng, ap, val, W):
        P.op(eng, lambda e: e.memset(ap, val), writes=W)

    NWB = 4
    wring = [T("wring", [128, 4096], BF16) for _ in range(NWB)]
    wri = [0]

    def getw_rows(key, k0, kc, c0, ncols):
        sp = (key, k0, kc, c0, ncols)
        l = key[1]
        want = plan_idx[l][sp] + 1 + LOOKAHEAD
        while cast_cur[l] < min(want, len(plans[l])):
            emit_cast(l)
        if l + 1 < DEPTH and cast_cur[l] >= len(plans[l]):
            emit_cast(l + 1)
        buf = wring[wri[0] % NWB]
        wri[0] += 1
        view = buf.t[:, 0:kc * ncols].rearrange("p (k e) -> p k e", k=kc)
        P.dma("sp", buf.t[:, 0:kc * ncols], wscr[sp], reads=[cast_dep[sp]], writes=[buf])
        return buf, view

    def getw(key, kc, c0, ncols):
        return getw_rows(key, 0, kc, c0, ncols)

    xt = [T("xt", [128, D]) for _ in range(2)]
    xT = T("xT", [128, 8, NTP], BF16)
    big1 = T("big1", [128, 4096])
    dbig = [big1.d, Dep()]
    pre_c = [Dep() for _ in range(12)]
    pre = Vw(big1.t[:, 0:12 * (NTP + 3)].rearrange("p (c n) -> p c n", c=12), dbig + pre_c)
    actT = Vw(big1.t[:, :].bitcast(BF16).rearrange("p (k n) -> p k n", k=32), dbig + pre_c)
    cvt = [T("cvt", [128, NTP]) for _ in range(2)]
    slt = [T("slt", [128, NTP]) for _ in range(2)]
    sqt = [T("sqt", [128, NTP], BF16) for _ in range(2)]
    rnt = [T("rnt", [128, NTP]) for _ in range(2)]
    dqT = T("dqT", [128, H, NTP], BF16)
    dkT = T("dkT", [128, H, NTP], BF16)
    dvT = T("dvT", [128, H, NTP], BF16)
    mqT = T("mqT", [128, H, NTP], BF16)
    mkT = T("mkT", [128, H, NTP], BF16)
    zs = [T("zs", [128, 512], BF16) for _ in range(2)]
    gsl = [T("gsl", [128, 512], BF16) for _ in range(2)]
    osg = [T("osg", [128, 512], BF16) for _ in range(2)]
    rq_tm = [T("rq_tm", [128, H, 128], BF16) for _ in range(2)]
    rk_tm = [T("rk_tm", [128, H, 128], BF16) for _ in range(2)]
    rv_tm = [T("rv_tm", [128, H, 128], BF16) for _ in range(2)]
    vaug = [T("vaug", [128, H, 129], BF16) for _ in range(2)]
    smt = [T("smt", [128, 16]) for _ in range(2)]
    rott = [T("rott", [128, 4, H, 64]) for _ in range(2)]
    oT = [T("oT%d" % i, [128, H, NTP], BF16) for i in range(3)]
    mergedT = T("mergedT", [128, 8, NTP], BF16)
    hp = nc.alloc_sbuf_tensor("hp", [128, 2064], F32)
    hTt = nc.alloc_sbuf_tensor("hTt", [128, 2064], BF16)
    dhp = [Dep(), Dep()]
    dhT = Dep()
    hpre = [Vw(hp[:, jj * D:(jj + 1) * D], [dhp[jj]]) for jj in range(2)]
    macc = Vw(hp[:, 0:8 * NTP].rearrange("p (d n) -> p d n", d=8), dhp)
    htl = hpre
    hT = Vw(hTt[:, 0:8 * NTP].rearrange("p (k n) -> p k n", k=8), [dhT])
    lnrow = [T("lnrow", [128, D]) for _ in range(2)]
    cw = T("cw", [128, 12, 4])
    halo = T("halo", [128, 12, 3])
    rowc = T("rowc", [128, 16])
    alog = T("alog", [128, 4])
    dnnw = T("dnnw", [128, 128])
    mlnw = T("mlnw", [128, 512])
    Sdn = T("Sdn", [128, H, 128]); Sdnb = T("Sdnb", [128, H, 128], BF16)
    Rrt = T("Rrt", [128, H, 128]); Rrtb = T("Rrtb", [128, H, 128], BF16)
    Cml = T("Cml", [128, H, 129]); Cmlb = T("Cmlb", [128, H, 129], BF16)
    mcar = T("mcar", [128, H])
    S0 = [Vw(hp[:, 0:NSB * 129].rearrange("p (b v) -> p b v", b=NSB), dhp)] * 2
    S1 = [Vw(big1.t[:, jj * 2048:(jj + 1) * 2048].rearrange("p (b v) -> p b v", b=NSB),
             [dbig[jj]] + (pre_c[0:8] if jj == 0 else pre_c[7:12])) for jj in range(2)]
    S0b = [Vw(hTt[:, 0:NSB * 129].rearrange("p (b v) -> p b v", b=NSB), [dhT])] * 2
    msp = T("msp", [64, H])

    for v in vaug:
        memset("pool", v[:, :, 128:129], 1.0, [v])

    scr_i = [0]
    scr_pool = {}

    def tmp(shape, dt=F32, n=4, tag=""):
        key = (tuple(shape), str(dt), tag)
        if key not in scr_pool:
            scr_pool[key] = [[T("tmp", list(shape), dt) for _ in range(n)], 0]
        ent = scr_pool[key]
        ent[1] = (ent[1] + 1) % len(ent[0])
        return ent[0][ent[1]]

    def psum_bf(pt):
        return pt.t[:].bitcast(BF16)

    def layer_norm(src, dst, C, grow, brow):
        st = tmp([128, 2, 6], tag="bnst")
        for hh in range(2):
            P.op("dve", (lambda hh: lambda e: e.bn_stats(out=st[:C, hh, :], in_=src[:C, hh * 512:(hh + 1) * 512]))(hh),
                 reads=[src], writes=[st])
        mv = tmp([128, 4], tag="bnmv")
        P.op("dve", lambda e: e.bn_aggr(out=mv[:C, 0:2], in_=st[:C].rearrange("p a b -> p (a b)")), reads=[st], writes=[mv])
        ts("dve", mv[:C, 2:3], mv[:C, 1:2], LN_EPS, ALU.add, [mv], [mv])
        act(mv[:C, 2:3], mv[:C, 2:3], AF.Ln, [mv], [mv])
        act(mv[:C, 2:3], mv[:C, 2:3], AF.Exp, [mv], [mv], scale=-0.5)
        stt(mv[:C, 3:4], mv[:C, 0:1], -1.0, mv[:C, 2:3], ALU.mult, ALU.mult, [mv], [mv])
        act(dst[:C, :], src[:C, :], AF.Identity, [src, mv], [dst], scale=mv[:C, 2:3], bias=mv[:C, 3:4])
        tt("dve", dst[:C, :], dst[:C, :], grow[:C, :], ALU.mult, [dst, grow], [dst])
        tt("dve", dst[:C, :], dst[:C, :], brow[:C, :], ALU.add, [dst, brow], [dst])

    def branch_out(o_src, o_is_psum, C, gate_tile, oTdst, col0, R_extra):
        ss = tmp([128, 8], tag="rms")
        junk = tmp([128, 128], tag="junk", n=2)
        for h in range(H):
            P.op("act", (lambda h: lambda e: e.activation(out=junk[:C, :], in_=o_src[:, h, :], func=AF.Square,
                                                           accum_out=ss[:C, h:h + 1]))(h),
                 reads=R_extra, writes=[ss, junk])
        ts("dve", ss[:C, 4:8], ss[:C, 0:4], 1.0 / 128.0, ALU.mult, [ss], [ss], s2=RMS_EPS, op1=ALU.add)
        act(ss[:C, 4:8], ss[:C, 4:8], AF.Ln, [ss], [ss])
        act(ss[:C, 4:8], ss[:C, 4:8], AF.Exp, [ss], [ss], scale=-0.5)
        ob = tmp([128, H, 128], BF16, tag="ob", n=2)
        for h in range(H):
            stt(ob[:C, h, :], o_src[:, h, :], ss[:C, 4 + h:5 + h], gate_tile[:C, h * 128:(h + 1) * 128],
                ALU.mult, ALU.mult, R_extra + [ss, gate_tile], [ob])
        pt = pm()
        pv = psum_bf(pt)
        for h in range(H):
            tr(pv[:, h * C:(h + 1) * C], ob[:C, h, :], identb[:C, :C], [ob, identb], [pt])
        cp("act", oTdst[:, :, col0:col0 + C], pv[:, 0:H * C].rearrange("p (h c) -> p h c", h=H), [pt], [oTdst])

    DBG_NP = int(os.environ.get("KDBG_PASSES", "999"))
    DBG_ST = int(os.environ.get("KDBG_STAGE", "999"))
    DBG_SAMPLE = int(os.environ.get("KDBG_SAMPLE", "0"))
    DBG_MIX = int(os.environ.get("KDBG_MIX", "999"))
    DBG_ML = int(os.environ.get("KDBG_ML", "999"))
    npass_done = [0]

    def stop(n):
        return npass_done[0] == DBG_NP - 1 and DBG_ST <= n

    def layer_pass(l, sc):
        sample = (sc == NPASS)
        m = "s" if sample else "p"
        NT = NSB * ST if sample else NTP
        C = 64 if sample else 128
        nch = 1 if sample else 2
        nblk = NSB if sample else 1
        first = (sc == 0)
        lastp = (sc == NPASS - 1)
        Uc, Lc = K["U_" + m], K["L_" + m]
        lastsel, lastind, blkind = K["lastsel_" + m], K["lastind_" + m], K["blkind_" + m]
        strictT, cmT = K["strictT_" + m], K["cmT_" + m]
        maskb, maskbT = KB["maskb_" + m], KB["maskbT_" + m]
        colmask = KB["colmask"]
        blkindb = KB["blkind_s"]
        tok0 = SEQ if sample else sc * NTP
        P.tag = "L%d.P%d.xT" % (l, sc)

        if first:
            for c12 in range(12):
                P.dma("sp", cw[:, c12, :], conv_w[l, :, c12 * 128:(c12 + 1) * 128].rearrange("j p -> p j"),
                      writes=[cw], allow_slow_non_contiguous=True)
            P.dma("sp", rowc[:, 0:4], dt_bias[l:l + 1, :].broadcast_to([128, H]), writes=[rowc])
            P.dma("sp", alog[:], A_log[l:l + 1, :].broadcast_to([128, H]), writes=[alog])
            P.dma("sp", rowc[:, 8:12], ibias[l:l + 1, :].broadcast_to([128, H]), writes=[rowc])
            P.dma("sp", rowc[:, 12:16], fbias[l:l + 1, :].broadcast_to([128, H]), writes=[rowc])
            P.dma("sp", dnnw[:], dn_nw[l:l + 1, :].broadcast_to([128, 128]), writes=[dnnw])
            P.dma("sp", mlnw[:], ml_nw[l:l + 1, :].broadcast_to([128, 512]), writes=[mlnw])
            act(alog[:], alog[:], AF.Exp, [alog], [alog])
            ts("dve", rowc[:, 4:8], alog[:], -1.0, ALU.mult, [alog, rowc], [rowc])
            memset("pool", Sdn[:], 0.0, [Sdn]); memset("pool", Sdnb[:], 0.0, [Sdnb])
            memset("pool", Rrt[:], 0.0, [Rrt]); memset("pool", Rrtb[:], 0.0, [Rrtb])
            memset("pool", Cml[:], 0.0, [Cml]); memset("pool", Cmlb[:], 0.0, [Cmlb])
            memset("pool", mcar[:], 0.0, [mcar])

        for j in range(nch):
            r0 = j * 128
            if l == 0:
                src = xs[:, :] if sample else xp[sc * NTP + r0: sc * NTP + r0 + 128, :]
                P.dma("sp", xt[j][:C, :], src, writes=[xt[j]])
            else:
                ci = 16 if sample else sc * 2 + j
                P.dma("sp", xt[j][:C, :], xmid.ap[tok0 + r0: tok0 + r0 + C, :], reads=[xmid_d[ci]], writes=[xt[j]])
            for g in range(2):
                pt = pm()
                for i in range(4):
                    k = g * 4 + i
                    tr(pt[:, i * C:(i + 1) * C], xt[j][:C, k * 128:(k + 1) * 128], identf[:C, :C], [xt[j], identf], [pt])
                cp("act" if g == 0 else "dve", xT[:, g * 4:(g + 1) * 4, r0:r0 + C],
                   pt[:, 0:4 * C].rearrange("p (i c) -> p i c", i=4), [pt], [xT])

        if stop(1):
            return
        if sample:
            pre_v = pre.t[:, :, 0:NSB * 7].rearrange("p c (b j) -> p c b j", j=7)
            for g in range(3):
                cst = tmp([48, 512], tag="cst", n=1)
                P.dma("sp", cst[:, :], cconv[l, :, g * 512:(g + 1) * 512], writes=[cst])
                pt = pm()
                for i in range(4):
                    tr(pt[:, i * 48:(i + 1) * 48], cst[:48, i * 128:(i + 1) * 128], identf[:48, :48], [cst, identf], [pt])
                cp("dve", pre_v[:, g * 4:(g + 1) * 4, :, 0:3],
                   pt[:, 0:4 * 48].rearrange("p (i b j) -> p i b j", i=4, j=3), [pt], [pre])
        else:
            if first:
                memset("pool", pre[:, :, 0:3], 0.0, [pre])
            else:
                cp("pool", pre[:, :, 0:3], halo[:, :, :], [halo], [pre])

        P.tag = "L%d.P%d.fm" % (l, sc)

        def dn_gen(c12, pt):
            i2 = c12 % 2
            pc = pre_c[c12]
            if sample:
                cp("act", pre_v[:, c12, :, 3:7], pt[:, 0:NT].rearrange("p (b t) -> p b t", t=ST), [pt], [pc])
                src = lambda jj: pre_v[:, c12, :, jj:jj + ST]
                cv = cvt[i2].t[:, 0:NT].rearrange("p (b t) -> p b t", t=ST)
            else:
                cp("act", pre[:, c12, 3:3 + NT], pt[:, 0:NT], [pt], [pc])
                src = lambda jj: pre[:, c12, jj:jj + NT]
                cv = cvt[i2][:, 0:NT]
            yield
            act(cv, src(0), AF.Copy, [pc, cw], [cvt[i2]], scale=cw[:, c12, 0:1])
            yield
            for jj in range(1, 4):
                stt(cv, src(jj), cw[:, c12, jj:jj + 1], cv, ALU.mult, ALU.add, [pc, cw, cvt[i2]], [cvt[i2]])
            yield
            h = c12 % 4
            act(slt[i2][:, 0:NT], cvt[i2][:, 0:NT], AF.Exp, [cvt[i2]], [slt[i2]], scale=-1.0)
            act(slt[i2][:, 0:NT], slt[i2][:, 0:NT], AF.Ln, [slt[i2]], [slt[i2]], bias=1.0)
            act(slt[i2][:, 0:NT], slt[i2][:, 0:NT], AF.Exp, [slt[i2]], [slt[i2]], scale=-1.0)
            yield
            if c12 >= 8:
                tt("dve", dvT[:, h, 0:NT], cvt[i2][:, 0:NT], slt[i2][:, 0:NT], ALU.mult, [cvt[i2], slt[i2]], [dvT])
                return
            tt("dve", slt[i2][:, 0:NT], cvt[i2][:, 0:NT], slt[i2][:, 0:NT], ALU.mult, [cvt[i2], slt[i2]], [slt[i2]])
            yield
            act(sqt[i2][:, 0:NT], slt[i2][:, 0:NT], AF.Square, [slt[i2]], [sqt[i2]])
            yield
            p2 = pm()
            mm(p2[:, 0:NT], onesb[:, :], sqt[i2][:, 0:NT], [onesb, sqt[i2]], [p2])
            if c12 < 4:
                act(rnt[i2][:, 0:NT], p2[:, 0:NT], AF.Ln, [p2], [rnt[i2]], scale=128.0, bias=128.0 * RMS_EPS)
            else:
                act(rnt[i2][:, 0:NT], p2[:, 0:NT], AF.Ln, [p2], [rnt[i2]], scale=1.0, bias=RMS_EPS)
            act(rnt[i2][:, 0:NT], rnt[i2][:, 0:NT], AF.Exp, [rnt[i2]], [rnt[i2]], scale=-0.5)
            yield
            dst = dqT if c12 < 4 else dkT
            tt("dve", dst[:, h, 0:NT], slt[i2][:, 0:NT], rnt[i2][:, 0:NT], ALU.mult, [slt[i2], rnt[i2]], [dst])

        def mlq_gen(cc, pt):
            cp("act", mqT[:, cc, 0:NT], pt[:, 0:NT], [pt], [mqT])
            return
            yield

        def mlk_gen(cc, pt):
            act(mkT[:, cc, 0:NT], pt[:, 0:NT], AF.Copy, [pt], [mkT], scale=128.0 ** -0.5)
            return
            yield

        if stop(2):
            return
        active = []

        def pump(limit):
            while len(active) > limit:
                for g in list(active):
                    try:
                        next(g)
                    except StopIteration:
                        active.remove(g)

        fm_list = [(OFF["a_qkv"] + blk * 512, "dn", blk) for blk in range(3)] + [(OFF["m_q"], "mq", 0), (OFF["m_k"], "mk", 0)]
        for (c0, kind, blk) in fm_list:
            buf, wv = getw(("in", l), 8, c0, 512)
            for cc in range(4):
                pt = pd()
                for k in range(8):
                    mm(pt[:, 0:NT], wv[:, k, cc * 128:(cc + 1) * 128], xT[:, k, 0:NT], [buf, xT], [pt],
                       start=(k == 0), stop=(k == 7))
                if kind == "dn":
                    active.append(dn_gen(blk * 4 + cc, pt))
                elif kind == "mq":
                    active.append(mlq_gen(cc, pt))
                else:
                    active.append(mlk_gen(cc, pt))
                pump(1)
        pump(0)
        if not sample:
            cp("pool", halo[:, :, :], pre[:, :, NTP:NTP + 3], [pre], [halo])

        if stop(3):
            return
        if sample:
            for g in range(3):
                cso = tmp([48, 512], tag="cst", n=1)
                cso_src = tmp([128, 4, NSB, 3], tag="csos", n=1)
                cp("pool", cso_src[:], pre_v[:, g * 4:(g + 1) * 4, :, 4:7], [pre], [cso_src])
                pt = pm()
                for i in range(4):
                    tr(pt[:48, i * 128:(i + 1) * 128], cso_src[:, i].rearrange("p b j -> p (b j)"), identf[:, :],
                       [cso_src, identf], [pt])
                cp("dve", cso[:48, :], pt[:48, 0:512], [pt], [cso])
                P.dma("pool", o_sconv[l, :, g * 512:(g + 1) * 512], cso[:48, :], reads=[cso], final=True)
        elif lastp:
            for g in range(3):
                cso = tmp([48, 512], tag="cst", n=1)
                cso_src = tmp([128, 4, NSB, 3], tag="csos", n=1)
                cp("pool", cso_src[:, :, 0, :], pre[:, g * 4:(g + 1) * 4, NTP:NTP + 3], [pre], [cso_src])
                pt = pm()
                for i in range(4):
                    tr(pt[:3, i * 128:(i + 1) * 128], cso_src[:, i, 0, :], identf[:, :], [cso_src, identf], [pt])
                cp("dve", cso[:3, :], pt[:3, 0:512], [pt], [cso])
                P.dma("pool", o_pconv[l, :, g * 512:(g + 1) * 512], cso[:3, :], reads=[cso], final=True)

        if stop(4):
            return
        P.tag = "L%d.P%d.tm" % (l, sc)

        def tm_block(c0, ncols, handler):
            buf, wv = getw(("in", l), 8, c0, ncols)
            for j in range(nch):
                pt = pd()
                for k in range(8):
                    mm(pt[:C, 0:ncols], xT[:, k, j * 128:j * 128 + C], wv[:, k, :], [buf, xT], [pt],
                       start=(k == 0), stop=(k == 7))
                handler(j, pt)

        for j in range(nch):
            ci = 16 if sample else sc * 2 + j
            P.dma("sp", rott[j][:].rearrange("p a h f -> p (a h f)"), cin["rot"][ci], writes=[rott[j]])

        def z_h(j, pt):
            act(zs[j][:C, :], pt[:C, 0:512], AF.Silu, [pt], [zs[j]])
            tt("pool", zs[j][:C, :].rearrange("p (h v) -> p h v", h=H), zs[j][:C, :].rearrange("p (h v) -> p h v", h=H),
               dnnw[:C, :].unsqueeze(1).broadcast_to([C, H, 128]), ALU.mult, [zs[j], dnnw], [zs[j]])

        def g_h(j, pt):
            act(gsl[j][:C, :], pt[:C, 0:512], AF.Silu, [pt], [gsl[j]])

        def o_h(j, pt):
            act(osg[j][:C, :], pt[:C, 0:512], AF.Sigmoid, [pt], [osg[j]])
            tt("pool", osg[j][:C, :], osg[j][:C, :], mlnw[:C, :], ALU.mult, [osg[j], mlnw], [osg[j]])

        def rot_h(dst_list, ta, tb_):
            def hnd(j, pt):
                x = pt[:C, 0:512].rearrange("p (h two f) -> p h two f", h=H, two=2)
                x1, x2 = x[:, :, 0, :], x[:, :, 1, :]
                cq, sq = rott[j][:C, ta], rott[j][:C, tb_]
                dst = dst_list[j]
                dv_ = dst[:C].rearrange("p h (two f) -> p h two f", two=2)
                t1 = tmp([128, H, 64], tag="rot", n=4); t2 = tmp([128, H, 64], tag="rot", n=4)
                t3 = tmp([128, H, 64], tag="rot", n=4); t4 = tmp([128, H, 64], tag="rot", n=4)
                tt("dve", t1[:C], x1, cq, ALU.mult, [pt, rott[j]], [t1])
                tt("dve", t2[:C], x2, sq, ALU.mult, [pt, rott[j]], [t2])
                tt("dve", t3[:C], x1, sq, ALU.mult, [pt, rott[j]], [t3])
                tt("dve", t4[:C], x2, cq, ALU.mult, [pt, rott[j]], [t4])
                tt("pool", dv_[:, :, 0, :], t1[:C], t2[:C], ALU.subtract, [t1, t2], [dst])
                tt("pool", dv_[:, :, 1, :], t3[:C], t4[:C], ALU.add, [t3, t4], [dst])
            return hnd

        def rv_h(j, pt):
            cp("act", rv_tm[j][:C].rearrange("p h v -> p (h v)"), pt[:C, 0:512], [pt], [rv_tm[j]])

        def mv_h(j, pt):
            cp("act", vaug[j][:C, :, 0:128], pt[:C, 0:512].rearrange("p (h v) -> p h v", h=H), [pt], [vaug[j]])

        def sm_h(col):
            def hnd(j, pt):
                cp("dve", smt[j][:C, col:col + 8], pt[:C, 0:8], [pt], [smt[j]])
            return hnd

        tm_block(OFF["a_z"], 512, z_h)
        tm_block(OFF["a_beta"], 8, sm_h(0))
        tm_block(OFF["r_q"], 512, rot_h(rq_tm, 0, 1))
        tm_block(OFF["r_k"], 512, rot_h(rk_tm, 2, 3))
        tm_block(OFF["r_v"], 512, rv_h)
        tm_block(OFF["r_g"], 512, g_h)
        tm_block(OFF["m_v"], 512, mv_h)
        tm_block(OFF["m_o"], 512, o_h)
        tm_block(OFF["m_i"], 8, sm_h(8))

        if stop(5):
            return
        for j in range(nch):
            P.tag = "L%d.P%d.mix%d" % (l, sc, j)
            cs = slice(j * 128, j * 128 + C)
            last_chunk = lastp and j == nch - 1
            mixers(l, j, cs, C, nblk, sample, last_chunk, Uc, Lc, lastsel, lastind, blkind, blkindb, strictT, cmT,
                   maskb, maskbT, colmask)

        if stop(6):
            return
        P.tag = "L%d.P%d.merge" % (l, sc)
        for br in range(3):
            bbuf, bwv = getw(("br%d" % br, l), 4, 0, D)
            for half in range(2):
                gbuf, gwv = getw(("in", l), 8, OFF["gates"] + br * D + half * 512, 512)
                for cc in range(4):
                    d = half * 4 + cc
                    pg = pd()
                    for k in range(8):
                        mm(pg[:, 0:NT], gwv[:, k, cc * 128:(cc + 1) * 128], xT[:, k, 0:NT], [gbuf, xT], [pg],
                           start=(k == 0), stop=(k == 7))
                    gs = tmp([128, NTP], tag="gs", n=1)
                    act(gs[:, 0:NT], pg[:, 0:NT], AF.Sigmoid, [pg], [gs])
                    pb = pd()
                    for k in range(4):
                        mm(pb[:, 0:NT], bwv[:, k, d * 128:(d + 1) * 128], oT[br][:, k, 0:NT], [bbuf, oT[br]], [pb],
                           start=(k == 0), stop=(k == 3))
                    if br == 0:
                        tt("dve", macc[:, d, 0:NT], pb[:, 0:NT], gs[:, 0:NT], ALU.mult, [pb, gs], [macc])
                    else:
                        t_ = tmp([128, NTP], tag="mt", n=1)
                        tt("dve", t_[:, 0:NT], pb[:, 0:NT], gs[:, 0:NT], ALU.mult, [pb, gs], [t_])
                        if br == 1:
                            tt("pool", macc[:, d, 0:NT], macc[:, d, 0:NT], t_[:, 0:NT], ALU.add, [macc, t_], [macc])
                        else:
                            tt("pool", mergedT[:, d, 0:NT], macc[:, d, 0:NT], t_[:, 0:NT], ALU.add, [macc, t_], [mergedT])

        if stop(7):
            return
        P.tag = "L%d.P%d.wout" % (l, sc)
        P.dma("sp", lnrow[0][:], ln1g[l:l + 1, :].broadcast_to([128, D]), writes=[lnrow[0]])
        P.dma("sp", lnrow[1][:], ln1b[l:l + 1, :].broadcast_to([128, D]), writes=[lnrow[1]])
        wo = [getw(("out", l), 8, half * 512, 512) for half in range(2)]
        for j in range(nch):
            for half in range(2):
                wbuf, wv = wo[half]
                pt = pd()
                for k in range(8):
                    mm(pt[:C, 0:512], mergedT[:, k, j * 128:j * 128 + C], wv[:, k, :], [wbuf, mergedT], [pt],
                       start=(k == 0), stop=(k == 7))
                stt(hpre[j][:C, half * 512:(half + 1) * 512], xt[j][:C, half * 512:(half + 1) * 512], ALPHA,
                    pt[:C, 0:512], ALU.mult, ALU.add, [xt[j], pt], [hpre[j]])
            if j == 0:
                layer_norm(hpre[0], htl[0], C, lnrow[0], lnrow[1])
        for j in range(nch):
            if j > 0:
                layer_norm(hpre[j], htl[j], C, lnrow[0], lnrow[1])
            for g in range(2):
                pt = pm()
                for i in range(4):
                    k = g * 4 + i
                    tr(pt[:, i * C:(i + 1) * C], htl[j][:C, k * 128:(k + 1) * 128], identf[:C, :C], [htl[j], identf], [pt])
                cp("act" if g == 0 else "dve", hT[:, g * 4:(g + 1) * 4, j * 128:j * 128 + C],
                   pt[:, 0:4 * C].rearrange("p (i c) -> p i c", i=4), [pt], [hT])

        if stop(8):
            return
        P.tag = "L%d.P%d.ff1" % (l, sc)
        for hb in range(8):
            fbuf, fv = getw(("ff1", l), 8, hb * 512, 512)
            for cc in range(4):
                pt = pd()
                for k in range(8):
                    mm(pt[:, 0:NT], fv[:, k, cc * 128:(cc + 1) * 128], hT[:, k, 0:NT], [fbuf, hT], [pt],
                       start=(k == 0), stop=(k == 7))
                r_ = tmp([128, NTP], tag="relu", n=1)
                act(r_[:, 0:NT], pt[:, 0:NT], AF.Relu, [pt], [r_])
                tt("pool" if cc % 2 else "dve", actT[:, hb * 4 + cc, 0:NT], r_[:, 0:NT], r_[:, 0:NT], ALU.mult, [r_], [actT])
        P.tag = "L%d.P%d.ff2" % (l, sc)
        for half in range(2):
            for q in range(4):
                fbuf, fv = getw_rows(("ff2", l), q * 8, 8, half * 512, 512)
                for j in range(nch):
                    for k in range(8):
                        mm(PACC[j][:C, 0:512], actT[:, q * 8 + k, j * 128:j * 128 + C], fv[:, k, :], [fbuf, actT], [PACC[j]],
                           start=(q == 0 and k == 0), stop=(q == 3 and k == 7))
            for j in range(nch):
                stt(hpre[j][:C, half * 512:(half + 1) * 512], htl[j][:C, half * 512:(half + 1) * 512], ALPHA,
                    PACC[j][:C, 0:512], ALU.mult, ALU.add, [htl[j], PACC[j]], [hpre[j]])
        P.dma("sp", lnrow[0][:], ln2g[l:l + 1, :].broadcast_to([128, D]), writes=[lnrow[0]])
        P.dma("sp", lnrow[1][:], ln2b[l:l + 1, :].broadcast_to([128, D]), writes=[lnrow[1]])
        for j in range(nch):
            yt = hpre[j]
            layer_norm(hpre[j], yt, C, lnrow[0], lnrow[1])
            if l == DEPTH - 1:
                dst = y_s[:, :] if sample else y_p[sc * NTP + j * 128: sc * NTP + j * 128 + 128, :]
                P.dma("pool", dst, yt[:C, :], reads=[yt], final=True)
            else:
                ci = 16 if sample else sc * 2 + j
                P.dma("pool", xmid.ap[tok0 + j * 128: tok0 + j * 128 + C, :], yt[:C, :], reads=[yt], writes=[xmid_d[ci]])

    def mixers(l, j, cs, C, nblk, sample, last_chunk, Uc, Lc, lastsel, lastind, blkind, blkindb, strictT, cmT,
               maskb, maskbT, colmask):
        nb4 = nblk * 4
        sm = smt[j]
        gt = tmp([128, 40], tag="gt", n=2)

        def masked_cols(srcT_ap, R):
            o = tmp([128, NSB, 64], BF16, tag="mcol", n=2)
            tt("pool", o[:], srcT_ap.unsqueeze(1).broadcast_to([128, NSB, 64]),
               colmask[:].rearrange("p (b c) -> p b c", b=NSB), ALU.mult, R + [colmask], [o])
            return o

        def masked_rows(src_ap, ncol, R):
            o = tmp([64, NSB, 129], BF16, tag="mrow", n=1)
            tt("pool", o[:, :, 0:ncol], src_ap.unsqueeze(1).broadcast_to([64, NSB, ncol]),
               blkindb[:, :].unsqueeze(2).broadcast_to([64, NSB, ncol]), ALU.mult, R + [blkindb], [o])
            return o

        tg0 = P.tag
        def gen_dn():
            P.tag = tg0 + ".dn_gate"
            act(gt[:C, 0:4], sm[:C, 0:4], AF.Exp, [sm], [gt], scale=-1.0)
            act(gt[:C, 0:4], gt[:C, 0:4], AF.Ln, [gt], [gt], bias=1.0)
            act(gt[:C, 0:4], gt[:C, 0:4], AF.Exp, [gt], [gt], scale=-1.0)
            tt("dve", gt[:C, 4:8], sm[:C, 4:8], rowc[:C, 0:4], ALU.add, [sm, rowc], [gt])
            act(gt[:C, 8:12], gt[:C, 4:8], AF.Exp, [gt], [gt])
            act(gt[:C, 12:16], gt[:C, 8:12], AF.Ln, [gt], [gt], bias=1.0)
            tt("dve", gt[:C, 16:20], gt[:C, 12:16], rowc[:C, 4:8], ALU.mult, [gt, rowc], [gt])
            gbt = tmp([128, NSB * 4], tag="gbt", n=2)
            tt("dve", gbt[:C, 0:nb4].rearrange("p (b h) -> p b h", h=H),
               gt[:C, 16:20].unsqueeze(1).broadcast_to([C, nblk, H]),
               blkind[:C, 0:nblk].unsqueeze(2).broadcast_to([C, nblk, H]), ALU.mult, [gt, blkind], [gbt])
            p1 = pm()
            mm(p1[:C, 0:4], Uc[:C, :C], gt[:C, 16:20], [Uc, gt], [p1])
            mm(p1[:C, 4:8], Lc[:C, :C], gt[:C, 16:20], [Lc, gt], [p1])
            mm(p1[:, 8:8 + nb4], onesf[:C, :], gbt[:C, 0:nb4], [onesf, gbt], [p1])
            act(gt[:C, 20:28], p1[:C, 0:8], AF.Exp, [p1], [gt])
            egt = tmp([128, NSB * 4], tag="egt", n=2)
            act(egt[:, 0:nb4], p1[:, 8:8 + nb4], AF.Exp, [p1], [egt])
            yield
            if DBG_MIX <= 1:
                return
            dk_tm = tmp([128, H, 128], BF16, tag="dk_tm", n=1)
            dv_tm = tmp([128, H, 128], BF16, tag="dv_tm", n=1)
            for (srcT, dst) in ((dkT, dk_tm), (dvT, dv_tm)):
                pt = pm()
                pv = psum_bf(pt)
                for h in range(H):
                    tr(pv[:C, h * 128:(h + 1) * 128], srcT[:, h, cs], identb[:, :], [srcT, identb], [pt])
                cp("act", dst[:C].rearrange("p h v -> p (h v)"), pv[:C, 0:512], [pt], [dst])
                yield
            o_dn = tmp([128, H, 128], tag="o_all", n=2)
            def dn_pre(h):
                P.tag = tg0 + ".dn_pre"
                kg = tmp([128, 128], BF16, tag="kg", n=2)
                kdec = tmp([128, 128], BF16, tag="kdec", n=2)
                act(kg[:C, :], dk_tm[:C, h, :], AF.Copy, [dk_tm, gt], [kg], scale=gt[:C, 20 + h:21 + h])
                act(kdec[:C, :], dk_tm[:C, h, :], AF.Copy, [dk_tm, gt], [kdec], scale=gt[:C, 24 + h:25 + h])
                Lg = tmp([128, 128], tag="Lg", n=2)
                ts("dve", Lg[:C, :C], Lc[:C, :C], gt[:C, 16 + h:17 + h], ALU.mult, [Lc, gt], [Lg])
                pe_ = pm()
                mm(pe_[:C, 0:C], Lg[:C, :C], Uc[:C, :C], [Lg, Uc], [pe_], start=True, stop=False)
                mm(pe_[:C, 0:C], identb[:C, :C], maskbT[:C, :C], [identb, maskbT], [pe_], start=False, stop=True)
                DTt = tmp([128, 128], tag="DTt", n=2)
                act(DTt[:C, :C], pe_[:C, 0:C], AF.Exp, [pe_], [DTt])
                DTs = tmp([128, 128], tag="DTs", n=2)
                tt("dve", DTs[:C, :C], DTt[:C, :C], strictT[:C, :C], ALU.mult, [DTt, strictT], [DTs])
                pg = pm()
                mm(pg[:C, 0:C], dkT[:, h, cs], dkT[:, h, cs], [dkT], [pg])
                mm(pg[:C, 128:128 + C], dkT[:, h, cs], dqT[:, h, cs], [dkT, dqT], [pg])
                NM = tmp([128, 2, 128], F32, tag="NM", n=4)
                stt(NM[:C, 0, :C], pg[:C, 0:C], gt[:C, h:h + 1], DTs[:C, :C], ALU.mult, ALU.mult, [pg, gt, DTs], [NM])
                PTt = tmp([128, 128], BF16, tag="PTt", n=2)
                tt("dve", PTt[:C, :C], pg[:C, 128:128 + C], DTt[:C, :C], ALU.mult, [pg, DTt], [PTt])
                X = tmp([128, 128], F32, tag="X", n=4)
                tt("dve", X[:C, :C], identf[:C, :C], NM[:C, 0, :C], ALU.subtract, [identf, NM], [X])
                pt = pm()
                tr(pt[:C, 0:C], NM[:C, 0, :C], identf[:C, :C], [NM, identf], [pt])
                cp("act", NM[:C, 1, :C], pt[:C, 0:C], [pt], [NM])
                return dict(kg=kg, kdec=kdec, PTt=PTt, X=X, NM=NM)

            def dn_level(st, lastlev):
                P.tag = tg0 + ".dn_lev"
                X, NM = st["X"], st["NM"]
                pa = pm()
                if not lastlev:
                    mm(pa[:C, 0:C], NM[:C, 1, :C], NM[:C, 0, :C], [NM], [pa])
                mm(pa[:C, 128:128 + C], NM[:C, 0, :C], NM[:C, 1, :C], [NM], [pa])
                NM2 = tmp([128, 2, 128], F32, tag="NM", n=4)
                if lastlev:
                    cp("act", NM2[:C, 1, :C], pa[:C, 128:128 + C], [pa], [NM2])
                else:
                    cp("act", NM2[:C, :, :C], pa[:C, 0:256].rearrange("p (a c) -> p a c", a=2)[:, :, 0:C], [pa], [NM2])
                pb = pm()
                mm(pb[:C, 0:C], NM2[:C, 1, :C], X[:C, :C], [NM2, X], [pb])
                if lastlev:
                    X2 = tmp([128, 128], BF16, tag="Xb", n=2)
                else:
                    X2 = tmp([128, 128], F32, tag="X", n=4)
                tt("dve", X2[:C, :C], pb[:C, 0:C], X[:C, :C], ALU.add, [pb, X], [X2])
                X, NM = X2, NM2
                st["X"], st["NM"] = X, NM

            def dn_post(h, kg, kdec, PTt, X):
                P.tag = tg0 + ".dn_post"
                pw = pm()
                mm(pw[:, 0:C], kg[:C, :], X[:C, :C], [kg, X], [pw])
                nwT = tmp([128, 128], BF16, tag="nwT", n=2)
                act(nwT[:, 0:C], pw[:, 0:C], AF.Copy, [pw], [nwT], scale=-1.0)
                yield
                if sample:
                    i2 = h % 2
                    P.dma("sp", S0[i2][:, :, 0:128], sS[l, :, h].rearrange("b k v -> k b v"), writes=[S0[i2]])
                    cp("act", S0b[i2][:, :, 0:128], S0[i2][:, :, 0:128], [S0[i2]], [S0b[i2]])
                    nwTm = masked_cols(nwT[:, 0:C], [nwT])
                    qTm = masked_cols(dqT[:, h, cs], [dqT])
                    kdm = masked_rows(kdec[:C, :], 128, [kdec])
                    st_r = [S0b[i2]]
                    S_b = lambda b: S0b[i2][:, b, 0:128]
                    nw_b = lambda b: nwTm[:, b, :]
                    q_b = lambda b: qTm[:, b, :]
                    kd_b = lambda b: kdm[:C, b, 0:128]
                    st_extra = [nwTm, qTm, kdm]
                else:
                    st_r = [Sdnb]
                    S_b = lambda b: Sdnb[:, h, :]
                    nw_b = lambda b: nwT[:, 0:C]
                    q_b = lambda b: dqT[:, h, cs]
                    kd_b = lambda b: kdec[:C, :]
                    st_extra = [nwT, dqT, kdec]
                pvn = pm()
                mm(pvn[:C, 0:128], X[:C, :C], dv_tm[:C, h, :], [X, dv_tm], [pvn], start=True, stop=False)
                for b in range(nblk):
                    mm(pvn[:C, 0:128], nw_b(b), S_b(b), st_r + st_extra, [pvn], start=False, stop=(b == nblk - 1))
                vnew = tmp([128, 128], BF16, tag="vnew", n=2)
                act(vnew[:C, :], pvn[:C, 0:128], AF.Copy, [pvn, gt], [vnew], scale=gt[:C, h:h + 1])
                yield
                pz = pm()
                for b in range(nblk):
                    mm(pz[:C, 0:128], q_b(b), S_b(b), st_r + st_extra, [pz], start=(b == 0), stop=(b == nblk - 1))
                pi_ = pm()
                mm(pi_[:C, 0:128], PTt[:C, :C], vnew[:C, :], [PTt, vnew], [pi_])
                isb = tmp([128, 128], tag="isb", n=2)
                cp("act", isb[:C, :], pi_[:C, 0:128], [pi_], [isb])
                stt(o_dn[:C, h, :], pz[:C, 0:128], gt[:C, 20 + h:21 + h], isb[:C, :], ALU.mult, ALU.add, [pz, gt, isb], [o_dn])
                yield
                if sample:
                    for b4 in range(0, nblk, 4):
                        pu = pm()
                        for bb in range(4):
                            b = b4 + bb
                            mm(pu[:, bb * 128:(bb + 1) * 128], kd_b(b), vnew[:C, :], st_extra + [vnew], [pu])
                        for bb in range(4):
                            b = b4 + bb
                            stt(S1[i2][:, b, :], S0[i2][:, b, 0:128], egt[:, b * 4 + h:b * 4 + h + 1],
                                pu[:, bb * 128:(bb + 1) * 128], ALU.mult, ALU.add, [S0[i2], egt, pu], [S1[i2]])
                    P.dma("pool", o_sS[l, :, h].rearrange("b k v -> k b v"), S1[i2][:, :, :], reads=[S1[i2]], final=True)
                else:
                    pu = pm()
                    mm(pu[:, 0:128], kdec[:C, :], vnew[:C, :], [kdec, vnew], [pu])
                    stt(Sdn[:, h, :], Sdn[:, h, :], egt[:, h:h + 1], pu[:, 0:128], ALU.mult, ALU.add, [Sdn, egt, pu], [Sdn])
                    cp("act", Sdnb[:, h, :], Sdn[:, h, :], [Sdn], [Sdnb])

            nlev = 1 if sample else 6
            for hp in range(0, H, 2):
                sts = []
                for hh in (hp, hp + 1):
                    sts.append(dn_pre(hh))
                    yield
                for lev in range(nlev):
                    for st in sts:
                        dn_level(st, lev == nlev - 1)
                        yield
                for hh, st in zip((hp, hp + 1), sts):
                    yield from dn_post(hh, st["kg"], st["kdec"], st["PTt"], st["X"])
                    yield
            if last_chunk:
                P.dma("pool", o_pS[l].rearrange("h k v -> k h v"), Sdn[:], reads=[Sdn], final=True)
            P.tag = tg0 + ".dn_out"
            branch_out(o_dn[:C], False, C, zs[j], oT[0], j * 128, [o_dn])


        def gen_ret():
            P.tag = tg0 + ".ret"
            rqT = tmp([128, H, 128], BF16, tag="rqT", n=1)
            rkT = tmp([128, H, 128], BF16, tag="rkT", n=1)
            for (src, dst) in ((rq_tm[j], rqT), (rk_tm[j], rkT)):
                pt = pm()
                pv = psum_bf(pt)
                for h in range(H):
                    tr(pv[:, h * C:(h + 1) * C], src[:C, h, :], identb[:C, :C], [src, identb], [pt])
                cp("act", dst[:, :, 0:C], pv[:, 0:H * C].rearrange("p (h c) -> p h c", h=H), [pt], [dst])
                yield
            pp = pm()
            for h in range(H):
                mm(pp[:C, h * C:(h + 1) * C], rkT[:, h, 0:C], rqT[:, h, 0:C], [rkT, rqT], [pp])
            PTm = tmp([128, H, 128], BF16, tag="PTm", n=1)
            tt("dve", PTm[:C, :, 0:C], pp[:C, 0:H * C].rearrange("p (h c) -> p h c", h=H),
               cmT[:C, :C].unsqueeze(1).broadcast_to([C, H, C]), ALU.mult, [pp, cmT], [PTm])
            yield
            po = PACC[0]
            for h in range(H):
                gam = 1.0 - 2.0 ** (-5.0 - h)
                if sample:
                    i2 = h % 2
                    P.dma("sp", S0[i2][:, :, 0:128], sR[l, :, h].rearrange("b k v -> k b v"), writes=[S0[i2]])
                    cp("act", S0b[i2][:, :, 0:128], S0[i2][:, :, 0:128], [S0[i2]], [S0b[i2]])
                    qTm = masked_cols(rqT[:, h, 0:C], [rqT])
                    kdm = masked_rows(rk_tm[j][:C, h, :], 128, [rk_tm[j]])
                mm(po[:C, h * 128:(h + 1) * 128], PTm[:C, h, 0:C], rv_tm[j][:C, h, :], [PTm, rv_tm[j]], [po], start=True, stop=False)
                for b in range(nblk):
                    if sample:
                        mm(po[:C, h * 128:(h + 1) * 128], qTm[:, b, :], S0b[i2][:, b, 0:128], [qTm, S0b[i2]], [po],
                           start=False, stop=(b == nblk - 1))
                    else:
                        mm(po[:C, h * 128:(h + 1) * 128], rqT[:, h, 0:C], Rrtb[:, h, :], [rqT, Rrtb], [po], start=False, stop=True)
                if sample:
                    gC = gam ** ST
                    for b4 in range(0, nblk, 4):
                        pu = pm()
                        for bb in range(4):
                            mm(pu[:, bb * 128:(bb + 1) * 128], kdm[:C, b4 + bb, 0:128], rv_tm[j][:C, h, :], [kdm, rv_tm[j]], [pu])
                        tq = tmp([128, 512], tag="rtmp", n=1)
                        tt("dve", tq[:, :].rearrange("p (b v) -> p b v", b=4), pu[:, 0:512].rearrange("p (b v) -> p b v", b=4),
                           S0[i2][:, b4:b4 + 4, 0:128], ALU.add, [pu, S0[i2]], [tq])
                        act(S1[i2][:, b4:b4 + 4, :], tq[:, :].rearrange("p (b v) -> p b v", b=4), AF.Copy, [tq], [S1[i2]], scale=gC)
                    P.dma("pool", o_sR[l, :, h].rearrange("b k v -> k b v"), S1[i2][:, :, :], reads=[S1[i2]], final=True)
            branch_out(po[:C, 0:512].rearrange("p (h v) -> p h v", h=H), True, C, gsl[j], oT[1], j * 128, [po])
            yield
            if not sample:
                pu = pm()
                for h in range(H):
                    mm(pu[:, h * 128:(h + 1) * 128], rk_tm[j][:C, h, :], rv_tm[j][:C, h, :], [rk_tm[j], rv_tm[j]], [pu])
                tq = tmp([128, 512], tag="rtmp", n=1)
                tt("dve", tq[:, :], pu[:, 0:512], Rrt[:].rearrange("p h v -> p (h v)"), ALU.add, [pu, Rrt], [tq])
                for h in range(H):
                    gC = (1.0 - 2.0 ** (-5.0 - h)) ** C
                    act(Rrt[:, h, :], tq[:, h * 128:(h + 1) * 128], AF.Copy, [tq], [Rrt], scale=gC)
                cp("dve", Rrtb[:], Rrt[:], [Rrt], [Rrtb])
                if last_chunk:
                    P.dma("pool", o_pR[l].rearrange("h k v -> k h v"), Rrt[:], reads=[Rrt], final=True)


        def gen_ml():
            P.tag = tg0 + ".ml_gate"
            g2 = tmp([128, 64], tag="g2", n=2)
            if sample:
                for t4 in range(ST):
                    P.dma("sp", msp[t4::ST, :], sM[l], writes=[msp])
                mprev = msp
            else:
                mprev = mcar
            tt("dve", g2[:C, 0:4], sm[:C, 8:12], rowc[:C, 8:12], ALU.add, [sm, rowc], [g2])
            tt("dve", g2[:C, 4:8], sm[:C, 12:16], rowc[:C, 12:16], ALU.add, [sm, rowc], [g2])
            act(g2[:C, 8:12], g2[:C, 4:8], AF.Exp, [g2], [g2], scale=-1.0)
            act(g2[:C, 12:16], g2[:C, 8:12], AF.Ln, [g2], [g2], bias=1.0)
            SUB = int(os.environ.get("KDBG_SUB", "99"))
            if SUB <= 1:
                return
            p1 = pm()
            mm(p1[:C, 0:4], Uc[:C, :C], g2[:C, 12:16], [Uc, g2], [p1])
            act(g2[:C, 16:20], p1[:C, 0:4], AF.Copy, [p1], [g2], scale=-1.0)
            if SUB <= 2:
                return
            tt("dve", g2[:C, 20:24], g2[:C, 0:4], g2[:C, 16:20], ALU.subtract, [g2], [g2])
            yield
            pls = []
            for h in range(H):
                dA = tmp([128, 128], tag="dA", n=2)
                ts("dve", dA[:C, :C], identf[:C, :C], g2[:C, 20 + h:21 + h], ALU.mult, [identf, g2], [dA])
                if SUB <= 3:
                    continue
                pl = PS[4 + h] if False else pm()
                mm(pl[:C, 0:C], onesf[:C, :C], dA[:C, :C], [onesf, dA], [pl], start=True, stop=False)
                mm(pl[:C, 0:C], identb[:C, :C], maskb[:C, :C], [identb, maskb], [pl], start=False, stop=True)
                if SUB <= 4:
                    continue
                P.op("dve", (lambda pl, h: lambda e: e.tensor_reduce(out=g2[:C, 24 + h:25 + h], in_=pl[:C, 0:C], axis=AX.X, op=ALU.max))(pl, h),
                     reads=[pl], writes=[g2])
                if SUB <= 5:
                    continue
                Lat = tmp([128, 128], tag="Lat", n=4)
                cp(os.environ.get("KDBG_LATENG", "act"), Lat[:C, :C], pl[:C, 0:C], [pl], [Lat])
                pls.append(Lat)
                yield
            if DBG_ML <= 1:
                return
            tt("dve", g2[:C, 28:32], g2[:C, 24:28], mprev[:C, 0:4], ALU.max, [g2, mprev], [g2])
            ts("dve", g2[:C, 32:36], g2[:C, 28:32], -1.0, ALU.mult, [g2], [g2])
            tt("dve", g2[:C, 36:40], mprev[:C, 0:4], g2[:C, 28:32], ALU.subtract, [g2, mprev], [g2])
            act(g2[:C, 36:40], g2[:C, 36:40], AF.Exp, [g2], [g2])
            tt("dve", g2[:C, 40:44], g2[:C, 16:20], g2[:C, 28:32], ALU.add, [g2], [g2])
            act(g2[:C, 44:48], g2[:C, 40:44], AF.Exp, [g2], [g2], scale=-1.0)
            p2 = pm()
            mm(p2[:C, 0:4], lastsel[:C, :C], g2[:C, 32:36], [lastsel, g2], [p2])
            tt("dve", g2[:C, 48:52], p2[:C, 0:4], g2[:C, 20:24], ALU.add, [p2, g2], [g2])
            act(g2[:C, 48:52], g2[:C, 48:52], AF.Exp, [g2], [g2])
            tt("dve", g2[:C, 52:56], mprev[:C, 0:4], g2[:C, 32:36], ALU.add, [g2, mprev], [g2])
            xb = tmp([128, 2, NSB * 4], tag="xb", n=2)
            tt("dve", xb[:C, 0, 0:nb4].rearrange("p (b h) -> p b h", h=H),
               g2[:C, 52:56].unsqueeze(1).broadcast_to([C, nblk, H]),
               lastind[:C, 0:nblk].unsqueeze(2).broadcast_to([C, nblk, H]), ALU.mult, [g2, lastind], [xb])
            tt("dve", xb[:C, 1, 0:nb4].rearrange("p (b h) -> p b h", h=H),
               g2[:C, 40:44].unsqueeze(1).broadcast_to([C, nblk, H]),
               lastind[:C, 0:nblk].unsqueeze(2).broadcast_to([C, nblk, H]), ALU.mult, [g2, lastind], [xb])
            mm(p2[:, 64:64 + nb4], onesf[:C, :], xb[:C, 0, 0:nb4], [onesf, xb], [p2])
            mm(p2[:, 128:128 + nb4], onesf[:C, :], xb[:C, 1, 0:nb4], [onesf, xb], [p2])
            sdec = tmp([128, NSB * 4], tag="sdec", n=2)
            act(sdec[:, 0:nb4], p2[:, 64:64 + nb4], AF.Exp, [p2], [sdec])
            mnew = tmp([128, NSB * 4], tag="mnew", n=2)
            cp("act", mnew[:, 0:nb4], p2[:, 128:128 + nb4], [p2], [mnew])
            yield
            if DBG_ML <= 2:
                return
            mk_tm = tmp([128, H, 128], BF16, tag="mk_tm", n=1)
            pt = pm()
            pv = psum_bf(pt)
            for h in range(H):
                tr(pv[:C, h * 128:(h + 1) * 128], mkT[:, h, cs], identb[:, :], [mkT, identb], [pt])
            cp("act", mk_tm[:C].rearrange("p h v -> p (h v)"), pv[:C, 0:512], [pt], [mk_tm])
            yield
            P.tag = tg0 + ".ml_head"
            hall = tmp([128, H, 128], tag="o_all", n=2)
            rs = tmp([128, 16], tag="rs", n=2)
            nt_s = tmp([128, NSB * 4], tag="nt_s", n=1)
            for h in range(H):
                E = tmp([128, 128], tag="E", n=2)
                act(E[:C, :C], pls[h][:C, :C], AF.Exp, [pls[h], g2], [E], bias=g2[:C, 32 + h:33 + h])
                pq = pm()
                mm(pq[:C, 0:C], mqT[:, h, cs], mkT[:, h, cs], [mqT, mkT], [pq])
                Dm = tmp([128, 128], BF16, tag="Dm", n=2)
                stt(Dm[:C, :C], pq[:C, 0:C], 1.0, E[:C, :C], ALU.mult, ALU.mult, [pq, E], [Dm, rs], accum=rs[:C, h:h + 1])
                yield
                pt = pm()
                pv = psum_bf(pt)
                tr(pv[:C, 0:C], Dm[:C, :C], identb[:C, :C], [Dm, identb], [pt])
                DmT = tmp([128, 128], BF16, tag="DmT", n=2)
                cp("act", DmT[:C, :C], pv[:C, 0:C], [pt], [DmT])
                yield
                if DBG_ML <= 3:
                    continue
                kw = tmp([128, 128], BF16, tag="kw", n=2)
                act(kw[:C, :], mk_tm[:C, h, :], AF.Copy, [mk_tm, g2], [kw], scale=g2[:C, 48 + h:49 + h])
                if sample:
                    i2 = h % 2
                    P.dma("sp", S0[i2][:, :, 0:128], sC[l, :, h].rearrange("b k v -> k b v"), writes=[S0[i2]])
                    P.dma("sp", S0[i2][:, :, 128], sN[l, :, h, :].rearrange("b k -> k b"), writes=[S0[i2]],
                          allow_slow_non_contiguous=True)
                    cp("act", S0b[i2][:], S0[i2][:], [S0[i2]], [S0b[i2]])
                    qTm = masked_cols(mqT[:, h, cs], [mqT])
                    kwm = masked_rows(kw[:C, :], 128, [kw])
                pi_ = pm()
                mm(pi_[:C, 0:128], DmT[:C, :C], vaug[j][:C, h, 0:128], [DmT, vaug[j]], [pi_])
                pz = pm()
                for b in range(nblk):
                    if sample:
                        mm(pz[:C, 0:129], qTm[:, b, :], S0b[i2][:, b, :], [qTm, S0b[i2]], [pz], start=(b == 0), stop=(b == nblk - 1))
                    else:
                        mm(pz[:C, 0:129], mqT[:, h, cs], Cmlb[:, h, :], [mqT, Cmlb], [pz])
                isb = tmp([128, 128], tag="isb", n=2)
                cp("act", isb[:C, :], pi_[:C, 0:128], [pi_], [isb])
                numt = tmp([128, 128], tag="numt", n=2)
                stt(numt[:C, :], pz[:C, 0:128], g2[:C, 36 + h:37 + h], isb[:C, :], ALU.mult, ALU.add, [pz, g2, isb], [numt])
                stt(rs[:C, 4 + h:5 + h], pz[:C, 128:129], g2[:C, 36 + h:37 + h], rs[:C, h:h + 1], ALU.mult, ALU.add,
                    [pz, g2, rs], [rs])
                act(rs[:C, 8 + h:9 + h], rs[:C, 4 + h:5 + h], AF.Abs, [rs], [rs])
                tt("dve", rs[:C, 8 + h:9 + h], rs[:C, 8 + h:9 + h], g2[:C, 44 + h:45 + h], ALU.max, [rs, g2], [rs])
                P.op("dve", (lambda h: lambda e: e.reciprocal(out=rs[:C, 12 + h:13 + h], in_=rs[:C, 8 + h:9 + h]))(h),
                     reads=[rs], writes=[rs])
                ts("dve", hall[:C, h, :], numt[:C, :], rs[:C, 12 + h:13 + h], ALU.mult, [numt, rs], [hall])
                yield
                if DBG_ML <= 4:
                    continue
                if sample:
                    for b in range(nblk):
                        pu = pm()
                        mm(pu[:, 0:129], kwm[:C, b, 0:128], vaug[j][:C, h, :], [kwm, vaug[j]], [pu])
                        stt(S1[i2][:, b, :], S0[i2][:, b, 0:128], sdec[:, b * 4 + h:b * 4 + h + 1], pu[:, 0:128], ALU.mult, ALU.add,
                            [S0[i2], sdec, pu], [S1[i2]])
                        stt(nt_s[:, b * 4 + h:b * 4 + h + 1], S0[i2][:, b, 128:129], sdec[:, b * 4 + h:b * 4 + h + 1],
                            pu[:, 128:129], ALU.mult, ALU.add, [S0[i2], sdec, pu], [nt_s])
                    P.dma("pool", o_sC[l, :, h].rearrange("b k v -> k b v"), S1[i2][:, :, :], reads=[S1[i2]], final=True)
                else:
                    pu = pm()
                    mm(pu[:, 0:129], kw[:C, :], vaug[j][:C, h, :], [kw, vaug[j]], [pu])
                    stt(Cml[:, h, :], Cml[:, h, :], sdec[:, h:h + 1], pu[:, 0:129], ALU.mult, ALU.add, [Cml, sdec, pu], [Cml])
                    cp("act", Cmlb[:, h, :], Cml[:, h, :], [Cml], [Cmlb])
                    yield
            if sample:
                pt = pm()
                tr(pt[:64, 0:128], nt_s[:, 0:64], identf[:, :], [nt_s, identf], [pt])
                nto = tmp([64, 128], tag="nto", n=1)
                cp("dve", nto[:, :], pt[:64, 0:128], [pt], [nto])
                P.dma("pool", o_sN[l], nto[:, :], reads=[nto], final=True)
                P.dma("pool", o_sM[l:l + 1, :], mnew[0:1, 0:64], reads=[mnew], final=True)
            else:
                cp("dve", mcar[:, :], mnew[:, 0:4], [mnew], [mcar])
                if last_chunk:
                    P.dma("pool", o_pC[l].rearrange("h k v -> k h v"), Cml[:, :, 0:128], reads=[Cml], final=True)
                    nt_p = tmp([128, 4], tag="nt_p", n=1)
                    cp("pool", nt_p[:, :], Cml[:, :, 128], [Cml], [nt_p])
                    pt = pm()
                    tr(pt[:4, 0:128], nt_p[:, 0:4], identf[:, :], [nt_p, identf], [pt])
                    nto = tmp([64, 128], tag="nto", n=1)
                    cp("dve", nto[:4, :], pt[:4, 0:128], [pt], [nto])
                    P.dma("pool", o_pN[l], nto[:4, :], reads=[nto], final=True)
                    P.dma("pool", o_pM[l:l + 1, :], mnew[0:1, 0:4], reads=[mnew], final=True)
            branch_out(hall[:C], False, C, osg[j], oT[2], j * 128, [hall])


        gens = [gen_dn()]
        if DBG_MIX > 2:
            gens.append(gen_ret())
        if DBG_MIX > 3:
            gens.append(gen_ml())
        if sample or os.environ.get("KDBG_NOILV"):
            for g in gens:
                for _ in g:
                    pass
        else:
            live = list(gens)
            dn_g = gens[0]
            while live:
                for g in list(live):
                    try:
                        next(g)
                        if g is dn_g and len(live) > 1:
                            next(g)
                    except StopIteration:
                        live.remove(g)

    for l in range(DEPTH):
        for sc in range(NPASS + 1):
            if npass_done[0] >= DBG_NP:
                break
            if DBG_SAMPLE and sc != NPASS:
                continue
            layer_pass(l, sc)
            npass_done[0] += 1
    if os.environ.get("KDBG_DUMP"):
        import json
        json.dump(P.tags, open(os.environ["KDBG_DUMP"], "w"))
    P.build()
    return nc


_CACHE = {}
_REG = {}


def kernel(x_prompt, x_sample, state_dn_conv, state_dn_S, state_ret_R, state_ml_C, state_ml_n, state_ml_m,
           w_in, dn_conv_w, dn_A_log, dn_dt_bias, dn_norm_w, ml_i_bias, ml_f_bias, ml_norm_w,
           w_br_a, w_br_b, w_br_c, w_out, ln1_g, ln1_b, w_ff1, w_ff2, ln2_g, ln2_b):
    f = lambda a: np.ascontiguousarray(np.asarray(a, dtype=np.float32))
    consts = _host_consts()
    if "nc" not in _CACHE:
        _CACHE["nc"] = build_program(consts)
    nc = _CACHE["nc"]
    shared = dict(w_in=f(w_in), conv_w=f(dn_conv_w), A_log=f(dn_A_log), dt_bias=f(dn_dt_bias), dn_nw=f(dn_norm_w),
                  ibias=f(ml_i_bias), fbias=f(ml_f_bias), ml_nw=f(ml_norm_w), w_br_a=f(w_br_a), w_br_b=f(w_br_b),
                  w_br_c=f(w_br_c), w_out=f(w_out), ln1g=f(ln1_g), ln1b=f(ln1_b), ln2g=f(ln2_g), ln2b=f(ln2_b),
                  w_ff1=f(w_ff1), w_ff2=f(w_ff2))
    for k, v in consts.items():
        shared["c_" + k] = f(v)
    in_maps = []
    for c in range(NCORES):
        b0 = c * NSB
        m = dict(shared)
        m["xp"] = f(x_prompt[c])
        m["xs"] = f(x_sample[b0:b0 + NSB]).reshape(NSB * ST, D)
        m["cconv"] = f(state_dn_conv[:, b0:b0 + NSB]).reshape(DEPTH, NSB * 3, 1536)
        m["sS"] = f(state_dn_S[:, b0:b0 + NSB])
        m["sR"] = f(state_ret_R[:, b0:b0 + NSB])
        m["sC"] = f(state_ml_C[:, b0:b0 + NSB])
        m["sN"] = f(state_ml_n[:, b0:b0 + NSB])
        m["sM"] = f(state_ml_m[:, b0:b0 + NSB])
        in_maps.append(m)
    ncr = int(os.environ.get("KDBG_CORES", str(NCORES)))
    res = run_bass_kernel_spmd(nc, in_maps[:ncr], core_ids=list(range(ncr)))
    R = list(res.results)
    while len(R) < NCORES:
        R.append(R[0])
    cat = lambda key, shp, ax: np.concatenate([np.asarray(R[c][key], dtype=np.float32).reshape(shp) for c in range(NCORES)], axis=ax)
    y_prompt = cat("y_p", (1, SEQ, D), 0)
    y_sample = cat("y_s", (NSB, ST, D), 0)
    p_conv = cat("o_pconv", (DEPTH, 1, 3, 1536), 1)
    p_S = cat("o_pS", (DEPTH, 1, H, 128, 128), 1)
    p_R = cat("o_pR", (DEPTH, 1, H, 128, 128), 1)
    p_C = cat("o_pC", (DEPTH, 1, H, 128, 128), 1)
    p_n = cat("o_pN", (DEPTH, 1, H, 128), 1)
    p_m = cat("o_pM", (DEPTH, 1, H), 1)
    s_conv = cat("o_sconv", (DEPTH, NSB, 3, 1536), 1)
    s_S = cat("o_sS", (DEPTH, NSB, H, 128, 128), 1)
    s_R = cat("o_sR", (DEPTH, NSB, H, 128, 128), 1)
    s_C = cat("o_sC", (DEPTH, NSB, H, 128, 128), 1)
    s_n = cat("o_sN", (DEPTH, NSB, H, 128), 1)
    s_m = cat("o_sM", (DEPTH, NSB, H), 1)
    return (y_prompt, y_sample, p_conv, p_S, p_R, p_C, p_n, p_m, s_conv, s_S, s_R, s_C, s_n, s_m)
```

```python
import os
import numpy as np
import concourse.bass as bass
import concourse.mybir as mybir
from concourse.bass_utils import run_bass_kernel_spmd

F32 = mybir.dt.float32
BF16 = mybir.dt.bfloat16
AF = mybir.ActivationFunctionType
ALU = mybir.AluOpType
AX = mybir.AxisListType

NDMASEM = 48
NCORES = 8
D = 1024
H = 4
DEPTH = 2
SEQ = 2048
PAST = 16384
NSB = 16
ST = 4
ALPHA = (2 * DEPTH) ** 0.25
LN_EPS = 1e-5
RMS_EPS = 1e-6
PW = 9232
OFF = dict(a_qkv=0, a_z=1536, a_beta=2048, a_decay=2052, r_q=2056, r_k=2568, r_v=3080, r_g=3592,
           m_q=4104, m_k=4616, m_v=5128, m_o=5640, m_i=6152, m_f=6156, gates=6160)
NTP = 256
NPASS = SEQ // NTP
BIG = 30000.0


class Dep:
    __slots__ = ("w", "r", "excl")

    def __init__(self):
        self.w = None
        self.r = []
        self.excl = False


class Op:
    __slots__ = ("eng", "fn", "deps", "sig", "cnt", "dma", "dsem", "dval", "reuse")

    def __init__(self, eng, fn, dma):
        self.eng = eng
        self.fn = fn
        self.dma = dma
        self.deps = ()
        self.sig = dma
        self.cnt = 0
        self.dsem = None
        self.dval = 0
        self.reuse = None


def _flat(xs):
    out = []
    for x in xs:
        if isinstance(x, Dep):
            out.append(x)
        elif hasattr(x, "ds"):
            out.extend(x.ds)
        else:
            out.append(x.d)
    return out


class Vw:
    def __init__(self, ap, deps):
        self.t = ap
        self.ds = deps

    def __getitem__(self, k):
        return self.t[k]


class Prog:
    ENG = ("pe", "dve", "act", "pool", "sp")

    def __init__(self, nc):
        self.nc = nc
        self.ops = []
        self.final = []
        self.tag = ""
        self.tags = []

    def op(self, eng, fn, reads=(), writes=(), dma=False):
        reads = _flat(reads)
        writes = _flat(writes)
        for d in reads:
            if d.excl and d not in writes:
                writes.append(d)
        i = len(self.ops)
        o = Op(eng, fn, dma)
        self.tags.append((eng, self.tag))
        deps = set()
        hard = set()
        for d in reads:
            if d.w is not None:
                deps.add(d.w)
                hard.add(d.w)
        for d in writes:
            if d.w is not None:
                deps.add(d.w)
                hard.add(d.w)
            deps.update(d.r)
        keep = []
        latest = {}
        for p in deps:
            po = self.ops[p]
            if po.dma:
                keep.append(p)
                continue
            if (not dma) and po.eng == eng:
                if eng == "pe" or p not in hard:
                    continue
            if latest.get(po.eng, -1) < p:
                latest[po.eng] = p
        keep.extend(latest.values())
        for p in keep:
            self.ops[p].sig = True
        o.deps = tuple(sorted(keep))
        for d in reads:
            d.r.append(i)
        for d in writes:
            d.w = i
            d.r = []
        self.ops.append(o)
        return i

    def dma(self, q, out, in_, reads=(), writes=(), final=False, **kw):
        def fn(e):
            return e.dma_start(out=out, in_=in_, **kw)
        i = self.op(q, fn, reads, writes, dma=True)
        if final:
            self.final.append(i)
        return i

    def build(self):
        nc = self.nc
        esem = {e: nc.alloc_semaphore("sem_" + e) for e in self.ENG}
        dsems = [nc.alloc_semaphore("dsem%d" % i) for i in range(NDMASEM)]
        cnt = {e: 0 for e in self.ENG}
        slot_used = [False] * NDMASEM
        slot_val = [0] * NDMASEM
        NSW = 20
        kq = {"sw": 0, "hw": 0}
        for o in self.ops:
            if o.dma:
                if o.eng == "pool":
                    s = kq["sw"] % NSW
                    kq["sw"] += 1
                else:
                    s = NSW + kq["hw"] % (NDMASEM - NSW)
                    kq["hw"] += 1
                o.dsem = dsems[s]
                if slot_used[s]:
                    o.reuse = (dsems[s], slot_val[s])
                slot_used[s] = True
                slot_val[s] += 16
                o.dval = slot_val[s]
            elif o.sig:
                cnt[o.eng] += 1
                o.cnt = cnt[o.eng]
        ops = self.ops
        finals = [(ops[i].dsem, ops[i].dval) for i in self.final]
        if os.environ.get("KDBG_DUMP"):
            import json
            json.dump([(o.eng, o.cnt, self.tags[i][1]) for i, o in enumerate(ops) if o.sig and not o.dma],
                      open(os.environ["KDBG_DUMP"] + ".cnt", "w"))

        def emit(ename):
            def body(e):
                waited = {}

                def wait(sem, val):
                    key = id(sem)
                    if waited.get(key, 0) >= val:
                        return
                    waited[key] = val
                    e.wait_ge(sem, val)

                for o in ops:
                    if o.eng != ename:
                        continue
                    for p in o.deps:
                        po = ops[p]
                        if po.dma:
                            wait(po.dsem, po.dval)
                        else:
                            wait(esem[po.eng], po.cnt)
                    if o.reuse is not None:
                        wait(*o.reuse)
                    inst = o.fn(e)
                    if o.dma:
                        inst.then_inc(o.dsem, 16)
                    elif o.sig:
                        inst.then_inc(esem[ename], 1)
                if ename == "pool":
                    for (s, v) in finals:
                        wait(s, v)
            return body

        with nc.Block() as block:
            block.tensor(emit("pe"))
            block.vector(emit("dve"))
            block.scalar(emit("act"))
            block.gpsimd(emit("pool"))
            block.sync(emit("sp"))


class Tl:
    def __init__(self, nc, name, shape, dt=F32, psum=False):
        if psum:
            self.t = nc.alloc_psum_tensor(name, list(shape), dt)
        else:
            self.t = nc.alloc_sbuf_tensor(name, list(shape), dt)
        self.d = Dep()
        self.d.excl = psum

    def __getitem__(self, k):
        return self.t[k]


class DT_:
    def __init__(self, ap):
        self.ap = ap
        self.d = Dep()


def _host_consts():
    c = {}
    f = np.float32
    idx = np.arange(128)
    c["identf"] = np.eye(128, dtype=f)
    c["onesf"] = np.ones((128, 128), f)
    for m, C, blk in (("p", 128, 128), ("s", 64, 4)):
        i = np.arange(C)
        b = i // blk
        same = (b[:, None] == b[None, :])
        c["U_" + m] = (same & (i[:, None] <= i[None, :])).astype(f)
        c["L_" + m] = (same & (i[:, None] > i[None, :])).astype(f)
        last = (b + 1) * blk - 1
        c["lastsel_" + m] = (i[:, None] == last[None, :]).astype(f)
        nb = C // blk
        c["lastind_" + m] = (i[:, None] == (np.arange(nb)[None, :] + 1) * blk - 1).astype(f)
        c["blkind_" + m] = (b[:, None] == np.arange(nb)[None, :]).astype(f)
        c["strictT_" + m] = (same & (i[:, None] < i[None, :])).astype(f)
        c["cmT_" + m] = (same & (i[:, None] <= i[None, :])).astype(f)
        c["maskb_" + m] = np.where(same & (i[None, :] <= i[:, None]), 0.0, -BIG).astype(f)
        c["maskbT_" + m] = np.where(same & (i[:, None] <= i[None, :]), 0.0, -BIG).astype(f)
    cm = np.zeros((128, NSB, 64), f)
    for bb in range(NSB):
        cm[:, bb, bb * ST:(bb + 1) * ST] = 1.0
    c["colmask"] = cm.reshape(128, NSB * 64)
    half = 64
    inv_freq = (1.0 / (np.float32(10000.0) ** np.linspace(0.0, 1.0, half, dtype=f))).astype(f)
    lg = np.log(1.0 - 2.0 ** (-5.0 - np.arange(H, dtype=np.float64)))
    tab = np.zeros((17, 128, 4, H, half), f)
    for ci in range(17):
        if ci < 16:
            pos = (ci * 128 + idx).astype(f)
            tb = idx.astype(np.float64)
        else:
            pos = np.zeros(128, f)
            tb = np.zeros(128)
            pos[:64] = (PAST + (np.arange(64) % ST)).astype(f)
            tb[:64] = (np.arange(64) % ST)
        ang = (pos[:, None] * inv_freq[None, :]).astype(f)
        cs, sn = np.cos(ang.astype(np.float64)), np.sin(ang.astype(np.float64))
        for h in range(H):
            qs = np.exp((tb + 1.0) * lg[h])[:, None]
            ks = np.exp(-(tb + 1.0) * lg[h])[:, None] * (128.0 ** -0.5)
            tab[ci, :, 0, h] = cs * qs
            tab[ci, :, 1, h] = sn * qs
            tab[ci, :, 2, h] = cs * ks
            tab[ci, :, 3, h] = sn * ks
    c["rot"] = tab.reshape(17, 128, 4 * H * half)
    return c


_F32C = ["identf", "onesf", "U_p", "L_p", "lastsel_p", "lastind_p", "blkind_p", "strictT_p", "cmT_p",
         "U_s", "L_s", "lastsel_s", "lastind_s", "blkind_s", "strictT_s", "cmT_s"]
_BF16C = ["maskb_p", "maskbT_p", "maskb_s", "maskbT_s", "colmask", "identf", "onesf", "blkind_s"]


def build_program(consts):
    nc = bass.Bass("TRN2", target_bir_lowering=False)
    P = Prog(nc)
    uid = [0]

    def nm(s):
        uid[0] += 1
        return "%s_%d" % (s, uid[0])

    def din(name, shape, dt=F32):
        return nc.dram_tensor(name, list(shape), dt, kind="ExternalInput").ap()

    def dout(name, shape):
        return nc.dram_tensor(name, list(shape), F32, kind="ExternalOutput").ap()

    def dscr(name, shape, dt):
        return nc.dram_tensor(name, list(shape), dt, kind="Internal").ap()

    def T(name, shape, dt=F32):
        n_ = nm(name)
        _REG.setdefault(name, []).append(n_)
        return Tl(nc, n_, shape, dt)

    xp = din("xp", [SEQ, D])
    xs = din("xs", [NSB * ST, D])
    cconv = din("cconv", [DEPTH, NSB * 3, 1536])
    sS = din("sS", [DEPTH, NSB, H, 128, 128])
    sR = din("sR", [DEPTH, NSB, H, 128, 128])
    sC = din("sC", [DEPTH, NSB, H, 128, 128])
    sN = din("sN", [DEPTH, NSB, H, 128])
    sM = din("sM", [DEPTH, NSB, H])
    w_in = din("w_in", [DEPTH, D, PW])
    conv_w = din("conv_w", [DEPTH, 4, 1536])
    A_log = din("A_log", [DEPTH, H])
    dt_bias = din("dt_bias", [DEPTH, H])
    dn_nw = din("dn_nw", [DEPTH, 128])
    ibias = din("ibias", [DEPTH, H])
    fbias = din("fbias", [DEPTH, H])
    ml_nw = din("ml_nw", [DEPTH, 512])
    w_br = [din("w_br_" + s, [DEPTH, 512, D]) for s in "abc"]
    w_out = din("w_out", [DEPTH, D, D])
    ln1g = din("ln1g", [DEPTH, D]); ln1b = din("ln1b", [DEPTH, D])
    ln2g = din("ln2g", [DEPTH, D]); ln2b = din("ln2b", [DEPTH, D])
    w_ff1 = din("w_ff1", [DEPTH, D, 4 * D])
    w_ff2 = din("w_ff2", [DEPTH, 4 * D, D])
    cin = {k: din("c_" + k, list(consts[k].shape)) for k in consts}

    y_p = dout("y_p", [SEQ, D])
    y_s = dout("y_s", [NSB * ST, D])
    o_pconv = dout("o_pconv", [DEPTH, 3, 1536])
    o_pS = dout("o_pS", [DEPTH, H, 128, 128])
    o_pR = dout("o_pR", [DEPTH, H, 128, 128])
    o_pC = dout("o_pC", [DEPTH, H, 128, 128])
    o_pN = dout("o_pN", [DEPTH, H, 128])
    o_pM = dout("o_pM", [DEPTH, H])
    o_sconv = dout("o_sconv", [DEPTH, NSB * 3, 1536])
    o_sS = dout("o_sS", [DEPTH, NSB, H, 128, 128])
    o_sR = dout("o_sR", [DEPTH, NSB, H, 128, 128])
    o_sC = dout("o_sC", [DEPTH, NSB, H, 128, 128])
    o_sN = dout("o_sN", [DEPTH, NSB * H, 128])
    o_sM = dout("o_sM", [DEPTH, NSB * H])

    xmid = DT_(dscr("xmid", [SEQ + NSB * ST, D], F32))
    xmid_d = [Dep() for _ in range(17)]

    wsrc, wscr = {}, {}
    for l in range(DEPTH):
        for key, src in ((("in", l), w_in[l]), (("br0", l), w_br[0][l]), (("br1", l), w_br[1][l]),
                         (("br2", l), w_br[2][l]), (("out", l), w_out[l]), (("ff1", l), w_ff1[l]), (("ff2", l), w_ff2[l])):
            wsrc[key] = src

    def block_plan(l):
        bl = []
        for blk in range(3):
            bl.append((("in", l), 0, 8, OFF["a_qkv"] + blk * 512, 512))
        for nm_, w_ in (("m_q", 512), ("m_k", 512), ("a_z", 512), ("a_beta", 8), ("r_q", 512), ("r_k", 512), ("r_v", 512),
                        ("r_g", 512), ("m_v", 512), ("m_o", 512), ("m_i", 8)):
            bl.append((("in", l), 0, 8, OFF[nm_], w_))
        for br in range(3):
            bl.append((("br%d" % br, l), 0, 4, 0, D))
            for half in range(2):
                bl.append((("in", l), 0, 8, OFF["gates"] + br * D + half * 512, 512))
        for half in range(2):
            bl.append((("out", l), 0, 8, half * 512, 512))
        for hb in range(8):
            bl.append((("ff1", l), 0, 8, hb * 512, 512))
        for half in range(2):
            for q in range(4):
                bl.append((("ff2", l), q * 8, 8, half * 512, 512))
        return bl

    plans = [block_plan(l) for l in range(DEPTH)]
    plan_idx = [{sp: i for i, sp in enumerate(plans[l])} for l in range(DEPTH)]
    cast_dep = {}
    cast_cur = [0] * DEPTH
    LOOKAHEAD = 6

    def emit_cast(l):
        if cast_cur[l] >= len(plans[l]):
            return False
        sp = plans[l][cast_cur[l]]
        cast_cur[l] += 1
        key, k0, kc, c0, ncols = sp
        d = Dep()
        scr = dscr("wsc_%d_%d" % (l, cast_cur[l]), [128, kc * ncols], BF16)
        wscr[sp] = scr
        P.dma("pool", scr.rearrange("p (k e) -> p k e", k=kc),
              wsrc[key].rearrange("(k p) e -> p k e", p=128)[:, k0:k0 + kc, c0:c0 + ncols], writes=[d])
        cast_dep[sp] = d
        return True

    K = {}
    for k in _F32C:
        sh = consts[k].shape
        K[k] = T("k_" + k, list(sh))
        P.dma("sp", K[k][:], cin[k], writes=[K[k]])
    KB = {}
    for k in _BF16C:
        sh = consts[k].shape
        KB[k] = T("kb_" + k, list(sh), BF16)
        P.dma("pool", KB[k][:], cin[k], writes=[KB[k]])
    identf, onesf = K["identf"], K["onesf"]
    identb, onesb = KB["identf"], KB["onesf"]

    PS = [Tl(nc, "psb%d" % i, [128, 512], F32, psum=True) for i in range(8)]
    rr = {"d": 0, "m": 0}

    def pd():
        rr["d"] = (rr["d"] + 1) % 4
        return PS[rr["d"]]

    def pm():
        rr["m"] = (rr["m"] + 1) % 4
        return PS[4 + rr["m"]]

    PACC = [PS[2], PS[3]]

    def mm(out, lhsT, rhs, R, W, start=True, stop=True):
        P.op("pe", lambda e: e.matmul(out, lhsT=lhsT, rhs=rhs, start=start, stop=stop, skip_group_check=True),
             reads=R, writes=W)

    def tr(out, in_, ident, R, W):
        P.op("pe", lambda e: e.transpose(out, in_, ident), reads=R, writes=W)

    def act(out, in_, func, R, W, **kw):
        P.op("act", lambda e: e.activation(out=out, in_=in_, func=func, **kw), reads=R, writes=W)

    def cp(eng, out, in_, R, W):
        if eng == "act":
            P.op("act", lambda e: e.copy(out=out, in_=in_), reads=R, writes=W)
        else:
            P.op(eng, lambda e: e.tensor_copy(out=out, in_=in_), reads=R, writes=W)

    def tt(eng, out, in0, in1, op, R, W):
        P.op(eng, lambda e: e.tensor_tensor(out=out, in0=in0, in1=in1, op=op), reads=R, writes=W)

    def ts(eng, out, in0, s1, op0, R, W, s2=None, op1=None):
        if op1 is None:
            P.op(eng, lambda e: e.tensor_scalar(out=out, in0=in0, scalar1=s1, scalar2=None, op0=op0), reads=R, writes=W)
        else:
            P.op(eng, lambda e: e.tensor_scalar(out=out, in0=in0, scalar1=s1, scalar2=s2, op0=op0, op1=op1),
                 reads=R, writes=W)

    def stt(out, in0, scalar, in1, op0, op1, R, W, accum=None):
        if accum is None:
            P.op("dve", lambda e: e.scalar_tensor_tensor(out=out, in0=in0, scalar=scalar, in1=in1, op0=op0, op1=op1),
                 reads=R, writes=W)
        else:
            P.op("dve", lambda e: e.scalar_tensor_tensor(out=out, in0=in0, scalar=scalar, in1=in1, op0=op0, op1=op1,
                                                         accum_out=accum), reads=R, writes=W)

    def memset(eng, ap, val, W):
        P.op(eng, lambda e: e.memset(ap, val), writes=W)

    NWB = 4
    wring = [T("wring", [128, 4096], BF16) for _ in range(NWB)]
    wri = [0]

    def getw_rows(key, k0, kc, c0, ncols):
        sp = (key, k0, kc, c0, ncols)
        l = key[1]
        want = plan_idx[l][sp] + 1 + LOOKAHEAD
        while cast_cur[l] < min(want, len(plans[l])):
            emit_cast(l)
        if l + 1 < DEPTH and cast_cur[l] >= len(plans[l]):
            emit_cast(l + 1)
        buf = wring[wri[0] % NWB]
        wri[0] += 1
        view = buf.t[:, 0:kc * ncols].rearrange("p (k e) -> p k e", k=kc)
        P.dma("sp", buf.t[:, 0:kc * ncols], wscr[sp], reads=[cast_dep[sp]], writes=[buf])
        return buf, view

    def getw(key, kc, c0, ncols):
        return getw_rows(key, 0, kc, c0, ncols)

    xt = [T("xt", [128, D]) for _ in range(2)]
    xT = T("xT", [128, 8, NTP], BF16)
    big1 = T("big1", [128, 4096])
    dbig = [big1.d, Dep()]
    pre_c = [Dep() for _ in range(12)]
    pre = Vw(big1.t[:, 0:12 * (NTP + 3)].rearrange("p (c n) -> p c n", c=12), dbig + pre_c)
    actT = Vw(big1.t[:, :].bitcast(BF16).rearrange("p (k n) -> p k n", k=32), dbig + pre_c)
    cvt = [T("cvt", [128, NTP]) for _ in range(2)]
    slt = [T("slt", [128, NTP]) for _ in range(2)]
    sqt = [T("sqt", [128, NTP], BF16) for _ in range(2)]
    rnt = [T("rnt", [128, NTP]) for _ in range(2)]
    dqT = T("dqT", [128, H, NTP], BF16)
    dkT = T("dkT", [128, H, NTP], BF16)
    dvT = T("dvT", [128, H, NTP], BF16)
    mqT = T("mqT", [128, H, NTP], BF16)
    mkT = T("mkT", [128, H, NTP], BF16)
    zs = [T("zs", [128, 512], BF16) for _ in range(2)]
    gsl = [T("gsl", [128, 512], BF16) for _ in range(2)]
    osg = [T("osg", [128, 512], BF16) for _ in range(2)]
    rq_tm = [T("rq_tm", [128, H, 128], BF16) for _ in range(2)]
    rk_tm = [T("rk_tm", [128, H, 128], BF16) for _ in range(2)]
    rv_tm = [T("rv_tm", [128, H, 128], BF16) for _ in range(2)]
    vaug = [T("vaug", [128, H, 129], BF16) for _ in range(2)]
    smt = [T("smt", [128, 16]) for _ in range(2)]
    rott = [T("rott", [128, 4, H, 64]) for _ in range(2)]
    oT = [T("oT%d" % i, [128, H, NTP], BF16) for i in range(3)]
    mergedT = T("mergedT", [128, 8, NTP], BF16)
    hp = nc.alloc_sbuf_tensor("hp", [128, 2064], F32)
    hTt = nc.alloc_sbuf_tensor("hTt", [128, 2064], BF16)
    dhp = [Dep(), Dep()]
    dhT = Dep()
    hpre = [Vw(hp[:, jj * D:(jj + 1) * D], [dhp[jj]]) for jj in range(2)]
    macc = Vw(hp[:, 0:8 * NTP].rearrange("p (d n) -> p d n", d=8), dhp)
    htl = hpre
    hT = Vw(hTt[:, 0:8 * NTP].rearrange("p (k n) -> p k n", k=8), [dhT])
    lnrow = [T("lnrow", [128, D]) for _ in range(2)]
    cw = T("cw", [128, 12, 4])
    halo = T("halo", [128, 12, 3])
    rowc = T("rowc", [128, 16])
    alog = T("alog", [128, 4])
    dnnw = T("dnnw", [128, 128])
    mlnw = T("mlnw", [128, 512])
    Sdn = T("Sdn", [128, H, 128]); Sdnb = T("Sdnb", [128, H, 128], BF16)
    Rrt = T("Rrt", [128, H, 128]); Rrtb = T("Rrtb", [128, H, 128], BF16)
    Cml = T("Cml", [128, H, 129]); Cmlb = T("Cmlb", [128, H, 129], BF16)
    mcar = T("mcar", [128, H])
    S0 = [Vw(hp[:, 0:NSB * 129].rearrange("p (b v) -> p b v", b=NSB), dhp)] * 2
    S1 = [Vw(big1.t[:, jj * 2048:(jj + 1) * 2048].rearrange("p (b v) -> p b v", b=NSB),
             [dbig[jj]] + (pre_c[0:8] if jj == 0 else pre_c[7:12])) for jj in range(2)]
    S0b = [Vw(hTt[:, 0:NSB * 129].rearrange("p (b v) -> p b v", b=NSB), [dhT])] * 2
    msp = T("msp", [64, H])

    for v in vaug:
        memset("pool", v[:, :, 128:129], 1.0, [v])

    scr_i = [0]
    scr_pool = {}

    def tmp(shape, dt=F32, n=4, tag=""):
        key = (tuple(shape), str(dt), tag)
        if key not in scr_pool:
            scr_pool[key] = [[T("tmp", list(shape), dt) for _ in range(n)], 0]
        ent = scr_pool[key]
        ent[1] = (ent[1] + 1) % len(ent[0])
        return ent[0][ent[1]]

    def psum_bf(pt):
        return pt.t[:].bitcast(BF16)

    def layer_norm(src, dst, C, grow, brow):
        st = tmp([128, 2, 6], tag="bnst")
        for hh in range(2):
            P.op("dve", (lambda hh: lambda e: e.bn_stats(out=st[:C, hh, :], in_=src[:C, hh * 512:(hh + 1) * 512]))(hh),
                 reads=[src], writes=[st])
        mv = tmp([128, 4], tag="bnmv")
        P.op("dve", lambda e: e.bn_aggr(out=mv[:C, 0:2], in_=st[:C].rearrange("p a b -> p (a b)")), reads=[st], writes=[mv])
        ts("dve", mv[:C, 2:3], mv[:C, 1:2], LN_EPS, ALU.add, [mv], [mv])
        act(mv[:C, 2:3], mv[:C, 2:3], AF.Ln, [mv], [mv])
        act(mv[:C, 2:3], mv[:C, 2:3], AF.Exp, [mv], [mv], scale=-0.5)
        stt(mv[:C, 3:4], mv[:C, 0:1], -1.0, mv[:C, 2:3], ALU.mult, ALU.mult, [mv], [mv])
        act(dst[:C, :], src[:C, :], AF.Identity, [src, mv], [dst], scale=mv[:C, 2:3], bias=mv[:C, 3:4])
        tt("dve", dst[:C, :], dst[:C, :], grow[:C, :], ALU.mult, [dst, grow], [dst])
        tt("dve", dst[:C, :], dst[:C, :], brow[:C, :], ALU.add, [dst, brow], [dst])

    def branch_out(o_src, o_is_psum, C, gate_tile, oTdst, col0, R_extra):
        ss = tmp([128, 8], tag="rms")
        junk = tmp([128, 128], tag="junk", n=2)
        for h in range(H):
            P.op("act", (lambda h: lambda e: e.activation(out=junk[:C, :], in_=o_src[:, h, :], func=AF.Square,
                                                           accum_out=ss[:C, h:h + 1]))(h),
                 reads=R_extra, writes=[ss, junk])
        ts("dve", ss[:C, 4:8], ss[:C, 0:4], 1.0 / 128.0, ALU.mult, [ss], [ss], s2=RMS_EPS, op1=ALU.add)
        act(ss[:C, 4:8], ss[:C, 4:8], AF.Ln, [ss], [ss])
        act(ss[:C, 4:8], ss[:C, 4:8], AF.Exp, [ss], [ss], scale=-0.5)
        ob = tmp([128, H, 128], BF16, tag="ob", n=2)
        for h in range(H):
            stt(ob[:C, h, :], o_src[:, h, :], ss[:C, 4 + h:5 + h], gate_tile[:C, h * 128:(h + 1) * 128],
                ALU.mult, ALU.mult, R_extra + [ss, gate_tile], [ob])
        pt = pm()
        pv = psum_bf(pt)
        for h in range(H):
            tr(pv[:, h * C:(h + 1) * C], ob[:C, h, :], identb[:C, :C], [ob, identb], [pt])
        cp("act", oTdst[:, :, col0:col0 + C], pv[:, 0:H * C].rearrange("p (h c) -> p h c", h=H), [pt], [oTdst])

    DBG_NP = int(os.environ.get("KDBG_PASSES", "999"))
    DBG_ST = int(os.environ.get("KDBG_STAGE", "999"))
    DBG_SAMPLE = int(os.environ.get("KDBG_SAMPLE", "0"))
    DBG_MIX = int(os.environ.get("KDBG_MIX", "999"))
    DBG_ML = int(os.environ.get("KDBG_ML", "999"))
    npass_done = [0]

    def stop(n):
        return npass_done[0] == DBG_NP - 1 and DBG_ST <= n

    def layer_pass(l, sc):
        sample = (sc == NPASS)
        m = "s" if sample else "p"
        NT = NSB * ST if sample else NTP
        C = 64 if sample else 128
        nch = 1 if sample else 2
        nblk = NSB if sample else 1
        first = (sc == 0)
        lastp = (sc == NPASS - 1)
        Uc, Lc = K["U_" + m], K["L_" + m]
        lastsel, lastind, blkind = K["lastsel_" + m], K["lastind_" + m], K["blkind_" + m]
        strictT, cmT = K["strictT_" + m], K["cmT_" + m]
        maskb, maskbT = KB["maskb_" + m], KB["maskbT_" + m]
        colmask = KB["colmask"]
        blkindb = KB["blkind_s"]
        tok0 = SEQ if sample else sc * NTP
        P.tag = "L%d.P%d.xT" % (l, sc)

        if first:
            for c12 in range(12):
                P.dma("sp", cw[:, c12, :], conv_w[l, :, c12 * 128:(c12 + 1) * 128].rearrange("j p -> p j"),
                      writes=[cw], allow_slow_non_contiguous=True)
            P.dma("sp", rowc[:, 0:4], dt_bias[l:l + 1, :].broadcast_to([128, H]), writes=[rowc])
            P.dma("sp", alog[:], A_log[l:l + 1, :].broadcast_to([128, H]), writes=[alog])
            P.dma("sp", rowc[:, 8:12], ibias[l:l + 1, :].broadcast_to([128, H]), writes=[rowc])
            P.dma("sp", rowc[:, 12:16], fbias[l:l + 1, :].broadcast_to([128, H]), writes=[rowc])
            P.dma("sp", dnnw[:], dn_nw[l:l + 1, :].broadcast_to([128, 128]), writes=[dnnw])
            P.dma("sp", mlnw[:], ml_nw[l:l + 1, :].broadcast_to([128, 512]), writes=[mlnw])
            act(alog[:], alog[:], AF.Exp, [alog], [alog])
            ts("dve", rowc[:, 4:8], alog[:], -1.0, ALU.mult, [alog, rowc], [rowc])
            memset("pool", Sdn[:], 0.0, [Sdn]); memset("pool", Sdnb[:], 0.0, [Sdnb])
            memset("pool", Rrt[:], 0.0, [Rrt]); memset("pool", Rrtb[:], 0.0, [Rrtb])
            memset("pool", Cml[:], 0.0, [Cml]); memset("pool", Cmlb[:], 0.0, [Cmlb])
            memset("pool", mcar[:], 0.0, [mcar])

        for j in range(nch):
            r0 = j * 128
            if l == 0:
                src = xs[:, :] if sample else xp[sc * NTP + r0: sc * NTP + r0 + 128, :]
                P.dma("sp", xt[j][:C, :], src, writes=[xt[j]])
            else:
                ci = 16 if sample else sc * 2 + j
                P.dma("sp", xt[j][:C, :], xmid.ap[tok0 + r0: tok0 + r0 + C, :], reads=[xmid_d[ci]], writes=[xt[j]])
            for g in range(2):
                pt = pm()
                for i in range(4):
                    k = g * 4 + i
                    tr(pt[:, i * C:(i + 1) * C], xt[j][:C, k * 128:(k + 1) * 128], identf[:C, :C], [xt[j], identf], [pt])
                cp("act" if g == 0 else "dve", xT[:, g * 4:(g + 1) * 4, r0:r0 + C],
                   pt[:, 0:4 * C].rearrange("p (i c) -> p i c", i=4), [pt], [xT])

        if stop(1):
            return
        if sample:
            pre_v = pre.t[:, :, 0:NSB * 7].rearrange("p c (b j) -> p c b j", j=7)
            for g in range(3):
                cst = tmp([48, 512], tag="cst", n=1)
                P.dma("sp", cst[:, :], cconv[l, :, g * 512:(g + 1) * 512], writes=[cst])
                pt = pm()
                for i in range(4):
                    tr(pt[:, i * 48:(i + 1) * 48], cst[:48, i * 128:(i + 1) * 128], identf[:48, :48], [cst, identf], [pt])
                cp("dve", pre_v[:, g * 4:(g + 1) * 4, :, 0:3],
                   pt[:, 0:4 * 48].rearrange("p (i b j) -> p i b j", i=4, j=3), [pt], [pre])
        else:
            if first:
                memset("pool", pre[:, :, 0:3], 0.0, [pre])
            else:
                cp("pool", pre[:, :, 0:3], halo[:, :, :], [halo], [pre])

        P.tag = "L%d.P%d.fm" % (l, sc)

        def dn_gen(c12, pt):
            i2 = c12 % 2
            pc = pre_c[c12]
            if sample:
                cp("act", pre_v[:, c12, :, 3:7], pt[:, 0:NT].rearrange("p (b t) -> p b t", t=ST), [pt], [pc])
                src = lambda jj: pre_v[:, c12, :, jj:jj + ST]
                cv = cvt[i2].t[:, 0:NT].rearrange("p (b t) -> p b t", t=ST)
            else:
                cp("act", pre[:, c12, 3:3 + NT], pt[:, 0:NT], [pt], [pc])
                src = lambda jj: pre[:, c12, jj:jj + NT]
                cv = cvt[i2][:, 0:NT]
            yield
            ts("dve", cv, src(0), cw[:, c12, 0:1], ALU.mult, [pc, cw], [cvt[i2]])
            yield
            for jj in range(1, 4):
                stt(cv, src(jj), cw[:, c12, jj:jj + 1], cv, ALU.mult, ALU.add, [pc, cw, cvt[i2]], [cvt[i2]])
            yield
            h = c12 % 4
            act(slt[i2][:, 0:NT], cvt[i2][:, 0:NT], AF.Exp, [cvt[i2]], [slt[i2]], scale=-1.0)
            act(slt[i2][:, 0:NT], slt[i2][:, 0:NT], AF.Ln, [slt[i2]], [slt[i2]], bias=1.0)
            act(slt[i2][:, 0:NT], slt[i2][:, 0:NT], AF.Exp, [slt[i2]], [slt[i2]], scale=-1.0)
            yield
            if c12 >= 8:
                tt("dve", dvT[:, h, 0:NT], cvt[i2][:, 0:NT], slt[i2][:, 0:NT], ALU.mult, [cvt[i2], slt[i2]], [dvT])
                return
            tt("dve", slt[i2][:, 0:NT], cvt[i2][:, 0:NT], slt[i2][:, 0:NT], ALU.mult, [cvt[i2], slt[i2]], [slt[i2]])
            yield
            tt("pool", sqt[i2][:, 0:NT], slt[i2][:, 0:NT], slt[i2][:, 0:NT], ALU.mult, [slt[i2]], [sqt[i2]])
            yield
            p2 = pm()
            mm(p2[:, 0:NT], onesb[:, :], sqt[i2][:, 0:NT], [onesb, sqt[i2]], [p2])
            if c12 < 4:
                act(rnt[i2][:, 0:NT], p2[:, 0:NT], AF.Ln, [p2], [rnt[i2]], scale=128.0, bias=128.0 * RMS_EPS)
            else:
                act(rnt[i2][:, 0:NT], p2[:, 0:NT], AF.Ln, [p2], [rnt[i2]], scale=1.0, bias=RMS_EPS)
            act(rnt[i2][:, 0:NT], rnt[i2][:, 0:NT], AF.Exp, [rnt[i2]], [rnt[i2]], scale=-0.5)
            yield
            dst = dqT if c12 < 4 else dkT
            tt("dve", dst[:, h, 0:NT], slt[i2][:, 0:NT], rnt[i2][:, 0:NT], ALU.mult, [slt[i2], rnt[i2]], [dst])

        def mlq_gen(cc, pt):
            cp("act", mqT[:, cc, 0:NT], pt[:, 0:NT], [pt], [mqT])
            return
            yield

        def mlk_gen(cc, pt):
            act(mkT[:, cc, 0:NT], pt[:, 0:NT], AF.Copy, [pt], [mkT], scale=128.0 ** -0.5)
            return
            yield

        if stop(2):
            return
        active = []

        def pump(limit):
            while len(active) > limit:
                for g in list(active):
                    try:
                        next(g)
                    except StopIteration:
                        active.remove(g)

        fm_list = [(OFF["a_qkv"] + blk * 512, "dn", blk) for blk in range(3)] + [(OFF["m_q"], "mq", 0), (OFF["m_k"], "mk", 0)]
        for (c0, kind, blk) in fm_list:
            buf, wv = getw(("in", l), 8, c0, 512)
            for cc in range(4):
                pt = pd()
                for k in range(8):
                    mm(pt[:, 0:NT], wv[:, k, cc * 128:(cc + 1) * 128], xT[:, k, 0:NT], [buf, xT], [pt],
                       start=(k == 0), stop=(k == 7))
                if kind == "dn":
                    active.append(dn_gen(blk * 4 + cc, pt))
                elif kind == "mq":
                    active.append(mlq_gen(cc, pt))
                else:
                    active.append(mlk_gen(cc, pt))
                pump(1)
        pump(0)
        if not sample:
            cp("pool", halo[:, :, :], pre[:, :, NTP:NTP + 3], [pre], [halo])

        if stop(3):
            return
        if sample:
            for g in range(3):
                cso = tmp([48, 512], tag="cst", n=1)
                cso_src = tmp([128, 4, NSB, 3], tag="csos", n=1)
                cp("pool", cso_src[:], pre_v[:, g * 4:(g + 1) * 4, :, 4:7], [pre], [cso_src])
                pt = pm()
                for i in range(4):
                    tr(pt[:48, i * 128:(i + 1) * 128], cso_src[:, i].rearrange("p b j -> p (b j)"), identf[:, :],
                       [cso_src, identf], [pt])
                cp("dve", cso[:48, :], pt[:48, 0:512], [pt], [cso])
                P.dma("pool", o_sconv[l, :, g * 512:(g + 1) * 512], cso[:48, :], reads=[cso], final=True)
        elif lastp:
            for g in range(3):
                cso = tmp([48, 512], tag="cst", n=1)
                cso_src = tmp([128, 4, NSB, 3], tag="csos", n=1)
                cp("pool", cso_src[:, :, 0, :], pre[:, g * 4:(g + 1) * 4, NTP:NTP + 3], [pre], [cso_src])
                pt = pm()
                for i in range(4):
                    tr(pt[:3, i * 128:(i + 1) * 128], cso_src[:, i, 0, :], identf[:, :], [cso_src, identf], [pt])
                cp("dve", cso[:3, :], pt[:3, 0:512], [pt], [cso])
                P.dma("pool", o_pconv[l, :, g * 512:(g + 1) * 512], cso[:3, :], reads=[cso], final=True)

        if stop(4):
            return
        P.tag = "L%d.P%d.tm" % (l, sc)

        def tm_block(c0, ncols, handler):
            buf, wv = getw(("in", l), 8, c0, ncols)
            for j in range(nch):
                pt = pd()
                for k in range(8):
                    mm(pt[:C, 0:ncols], xT[:, k, j * 128:j * 128 + C], wv[:, k, :], [buf, xT], [pt],
                       start=(k == 0), stop=(k == 7))
                handler(j, pt)

        for j in range(nch):
            ci = 16 if sample else sc * 2 + j
            P.dma("sp", rott[j][:].rearrange("p a h f -> p (a h f)"), cin["rot"][ci], writes=[rott[j]])

        def z_h(j, pt):
            act(zs[j][:C, :], pt[:C, 0:512], AF.Silu, [pt], [zs[j]])
            tt("pool", zs[j][:C, :].rearrange("p (h v) -> p h v", h=H), zs[j][:C, :].rearrange("p (h v) -> p h v", h=H),
               dnnw[:C, :].unsqueeze(1).broadcast_to([C, H, 128]), ALU.mult, [zs[j], dnnw], [zs[j]])

        def g_h(j, pt):
            act(gsl[j][:C, :], pt[:C, 0:512], AF.Silu, [pt], [gsl[j]])

        def o_h(j, pt):
            act(osg[j][:C, :], pt[:C, 0:512], AF.Sigmoid, [pt], [osg[j]])
            tt("pool", osg[j][:C, :], osg[j][:C, :], mlnw[:C, :], ALU.mult, [osg[j], mlnw], [osg[j]])

        def rot_h(dst_list, ta, tb_):
            def hnd(j, pt):
                x = pt[:C, 0:512].rearrange("p (h two f) -> p h two f", h=H, two=2)
                x1, x2 = x[:, :, 0, :], x[:, :, 1, :]
                cq, sq = rott[j][:C, ta], rott[j][:C, tb_]
                dst = dst_list[j]
                dv_ = dst[:C].rearrange("p h (two f) -> p h two f", two=2)
                t1 = tmp([128, H, 64], tag="rot", n=4); t2 = tmp([128, H, 64], tag="rot", n=4)
                t3 = tmp([128, H, 64], tag="rot", n=4); t4 = tmp([128, H, 64], tag="rot", n=4)
                tt("dve", t1[:C], x1, cq, ALU.mult, [pt, rott[j]], [t1])
                tt("dve", t2[:C], x2, sq, ALU.mult, [pt, rott[j]], [t2])
                tt("dve", t3[:C], x1, sq, ALU.mult, [pt, rott[j]], [t3])
                tt("dve", t4[:C], x2, cq, ALU.mult, [pt, rott[j]], [t4])
                tt("pool", dv_[:, :, 0, :], t1[:C], t2[:C], ALU.subtract, [t1, t2], [dst])
                tt("pool", dv_[:, :, 1, :], t3[:C], t4[:C], ALU.add, [t3, t4], [dst])
            return hnd

        def rv_h(j, pt):
            cp("act", rv_tm[j][:C].rearrange("p h v -> p (h v)"), pt[:C, 0:512], [pt], [rv_tm[j]])

        def mv_h(j, pt):
            cp("act", vaug[j][:C, :, 0:128], pt[:C, 0:512].rearrange("p (h v) -> p h v", h=H), [pt], [vaug[j]])

        def sm_h(col):
            def hnd(j, pt):
                cp("dve", smt[j][:C, col:col + 8], pt[:C, 0:8], [pt], [smt[j]])
            return hnd

        tm_block(OFF["a_z"], 512, z_h)
        tm_block(OFF["a_beta"], 8, sm_h(0))
        tm_block(OFF["r_q"], 512, rot_h(rq_tm, 0, 1))
        tm_block(OFF["r_k"], 512, rot_h(rk_tm, 2, 3))
        tm_block(OFF["r_v"], 512, rv_h)
        tm_block(OFF["r_g"], 512, g_h)
        tm_block(OFF["m_v"], 512, mv_h)
        tm_block(OFF["m_o"], 512, o_h)
        tm_block(OFF["m_i"], 8, sm_h(8))

        if stop(5):
            return
        for j in range(nch):
            P.tag = "L%d.P%d.mix%d" % (l, sc, j)
            cs = slice(j * 128, j * 128 + C)
            last_chunk = lastp and j == nch - 1
            mixers(l, j, cs, C, nblk, sample, last_chunk, Uc, Lc, lastsel, lastind, blkind, blkindb, strictT, cmT,
                   maskb, maskbT, colmask)

        if stop(6):
            return
        P.tag = "L%d.P%d.merge" % (l, sc)
        for br in range(3):
            bbuf, bwv = getw(("br%d" % br, l), 4, 0, D)
            for half in range(2):
                gbuf, gwv = getw(("in", l), 8, OFF["gates"] + br * D + half * 512, 512)
                for cc in range(4):
                    d = half * 4 + cc
                    pg = pd()
                    for k in range(8):
                        mm(pg[:, 0:NT], gwv[:, k, cc * 128:(cc + 1) * 128], xT[:, k, 0:NT], [gbuf, xT], [pg],
                           start=(k == 0), stop=(k == 7))
                    gs = tmp([128, NTP], tag="gs", n=1)
                    act(gs[:, 0:NT], pg[:, 0:NT], AF.Sigmoid, [pg], [gs])
                    pb = pd()
                    for k in range(4):
                        mm(pb[:, 0:NT], bwv[:, k, d * 128:(d + 1) * 128], oT[br][:, k, 0:NT], [bbuf, oT[br]], [pb],
                           start=(k == 0), stop=(k == 3))
                    if br == 0:
                        tt("dve", macc[:, d, 0:NT], pb[:, 0:NT], gs[:, 0:NT], ALU.mult, [pb, gs], [macc])
                    else:
                        t_ = tmp([128, NTP], tag="mt", n=1)
                        tt("dve", t_[:, 0:NT], pb[:, 0:NT], gs[:, 0:NT], ALU.mult, [pb, gs], [t_])
                        if br == 1:
                            tt("pool", macc[:, d, 0:NT], macc[:, d, 0:NT], t_[:, 0:NT], ALU.add, [macc, t_], [macc])
                        else:
                            tt("pool", mergedT[:, d, 0:NT], macc[:, d, 0:NT], t_[:, 0:NT], ALU.add, [macc, t_], [mergedT])

        if stop(7):
            return
        P.tag = "L%d.P%d.wout" % (l, sc)
        P.dma("sp", lnrow[0][:], ln1g[l:l + 1, :].broadcast_to([128, D]), writes=[lnrow[0]])
        P.dma("sp", lnrow[1][:], ln1b[l:l + 1, :].broadcast_to([128, D]), writes=[lnrow[1]])
        wo = [getw(("out", l), 8, half * 512, 512) for half in range(2)]
        for j in range(nch):
            for half in range(2):
                wbuf, wv = wo[half]
                pt = pd()
                for k in range(8):
                    mm(pt[:C, 0:512], mergedT[:, k, j * 128:j * 128 + C], wv[:, k, :], [wbuf, mergedT], [pt],
                       start=(k == 0), stop=(k == 7))
                stt(hpre[j][:C, half * 512:(half + 1) * 512], xt[j][:C, half * 512:(half + 1) * 512], ALPHA,
                    pt[:C, 0:512], ALU.mult, ALU.add, [xt[j], pt], [hpre[j]])
            if j == 0:
                layer_norm(hpre[0], htl[0], C, lnrow[0], lnrow[1])
        for j in range(nch):
            if j > 0:
                layer_norm(hpre[j], htl[j], C, lnrow[0], lnrow[1])
            for g in range(2):
                pt = pm()
                for i in range(4):
                    k = g * 4 + i
                    tr(pt[:, i * C:(i + 1) * C], htl[j][:C, k * 128:(k + 1) * 128], identf[:C, :C], [htl[j], identf], [pt])
                cp("act" if g == 0 else "dve", hT[:, g * 4:(g + 1) * 4, j * 128:j * 128 + C],
                   pt[:, 0:4 * C].rearrange("p (i c) -> p i c", i=4), [pt], [hT])

        if stop(8):
            return
        P.tag = "L%d.P%d.ff1" % (l, sc)
        for hb in range(8):
            fbuf, fv = getw(("ff1", l), 8, hb * 512, 512)
            for cc in range(4):
                pt = pd()
                for k in range(8):
                    mm(pt[:, 0:NT], fv[:, k, cc * 128:(cc + 1) * 128], hT[:, k, 0:NT], [fbuf, hT], [pt],
                       start=(k == 0), stop=(k == 7))
                r_ = tmp([128, NTP], tag="relu", n=1)
                act(r_[:, 0:NT], pt[:, 0:NT], AF.Relu, [pt], [r_])
                tt("pool" if cc % 2 else "dve", actT[:, hb * 4 + cc, 0:NT], r_[:, 0:NT], r_[:, 0:NT], ALU.mult, [r_], [actT])
        P.tag = "L%d.P%d.ff2" % (l, sc)
        for half in range(2):
            for q in range(4):
                fbuf, fv = getw_rows(("ff2", l), q * 8, 8, half * 512, 512)
                for j in range(nch):
                    for k in range(8):
                        mm(PACC[j][:C, 0:512], actT[:, q * 8 + k, j * 128:j * 128 + C], fv[:, k, :], [fbuf, actT], [PACC[j]],
                           start=(q == 0 and k == 0), stop=(q == 3 and k == 7))
            for j in range(nch):
                stt(hpre[j][:C, half * 512:(half + 1) * 512], htl[j][:C, half * 512:(half + 1) * 512], ALPHA,
                    PACC[j][:C, 0:512], ALU.mult, ALU.add, [htl[j], PACC[j]], [hpre[j]])
        P.dma("sp", lnrow[0][:], ln2g[l:l + 1, :].broadcast_to([128, D]), writes=[lnrow[0]])
        P.dma("sp", lnrow[1][:], ln2b[l:l + 1, :].broadcast_to([128, D]), writes=[lnrow[1]])
        for j in range(nch):
            yt = hpre[j]
            layer_norm(hpre[j], yt, C, lnrow[0], lnrow[1])
            if l == DEPTH - 1:
                dst = y_s[:, :] if sample else y_p[sc * NTP + j * 128: sc * NTP + j * 128 + 128, :]
                P.dma("pool", dst, yt[:C, :], reads=[yt], final=True)
            else:
                ci = 16 if sample else sc * 2 + j
                P.dma("pool", xmid.ap[tok0 + j * 128: tok0 + j * 128 + C, :], yt[:C, :], reads=[yt], writes=[xmid_d[ci]])

    def mixers(l, j, cs, C, nblk, sample, last_chunk, Uc, Lc, lastsel, lastind, blkind, blkindb, strictT, cmT,
               maskb, maskbT, colmask):
        nb4 = nblk * 4
        sm = smt[j]
        gt = tmp([128, 40], tag="gt", n=2)

        def masked_cols(srcT_ap, R):
            o = tmp([128, NSB, 64], BF16, tag="mcol", n=2)
            tt("pool", o[:], srcT_ap.unsqueeze(1).broadcast_to([128, NSB, 64]),
               colmask[:].rearrange("p (b c) -> p b c", b=NSB), ALU.mult, R + [colmask], [o])
            return o

        def masked_rows(src_ap, ncol, R):
            o = tmp([64, NSB, 129], BF16, tag="mrow", n=1)
            tt("pool", o[:, :, 0:ncol], src_ap.unsqueeze(1).broadcast_to([64, NSB, ncol]),
               blkindb[:, :].unsqueeze(2).broadcast_to([64, NSB, ncol]), ALU.mult, R + [blkindb], [o])
            return o

        tg0 = P.tag
        def gen_dn():
            P.tag = tg0 + ".dn_gate"
            act(gt[:C, 0:4], sm[:C, 0:4], AF.Exp, [sm], [gt], scale=-1.0)
            act(gt[:C, 0:4], gt[:C, 0:4], AF.Ln, [gt], [gt], bias=1.0)
            act(gt[:C, 0:4], gt[:C, 0:4], AF.Exp, [gt], [gt], scale=-1.0)
            tt("dve", gt[:C, 4:8], sm[:C, 4:8], rowc[:C, 0:4], ALU.add, [sm, rowc], [gt])
            act(gt[:C, 8:12], gt[:C, 4:8], AF.Exp, [gt], [gt])
            act(gt[:C, 12:16], gt[:C, 8:12], AF.Ln, [gt], [gt], bias=1.0)
            tt("dve", gt[:C, 16:20], gt[:C, 12:16], rowc[:C, 4:8], ALU.mult, [gt, rowc], [gt])
            gbt = tmp([128, NSB * 4], tag="gbt", n=2)
            tt("dve", gbt[:C, 0:nb4].rearrange("p (b h) -> p b h", h=H),
               gt[:C, 16:20].unsqueeze(1).broadcast_to([C, nblk, H]),
               blkind[:C, 0:nblk].unsqueeze(2).broadcast_to([C, nblk, H]), ALU.mult, [gt, blkind], [gbt])
            p1 = pm()
            mm(p1[:C, 0:4], Uc[:C, :C], gt[:C, 16:20], [Uc, gt], [p1])
            mm(p1[:C, 4:8], Lc[:C, :C], gt[:C, 16:20], [Lc, gt], [p1])
            mm(p1[:, 8:8 + nb4], onesf[:C, :], gbt[:C, 0:nb4], [onesf, gbt], [p1])
            act(gt[:C, 20:28], p1[:C, 0:8], AF.Exp, [p1], [gt])
            egt = tmp([128, NSB * 4], tag="egt", n=2)
            act(egt[:, 0:nb4], p1[:, 8:8 + nb4], AF.Exp, [p1], [egt])
            yield
            if DBG_MIX <= 1:
                return
            dk_tm = tmp([128, H, 128], BF16, tag="dk_tm", n=1)
            dv_tm = tmp([128, H, 128], BF16, tag="dv_tm", n=1)
            for (srcT, dst) in ((dkT, dk_tm), (dvT, dv_tm)):
                pt = pm()
                pv = psum_bf(pt)
                for h in range(H):
                    tr(pv[:C, h * 128:(h + 1) * 128], srcT[:, h, cs], identb[:, :], [srcT, identb], [pt])
                cp("act", dst[:C].rearrange("p h v -> p (h v)"), pv[:C, 0:512], [pt], [dst])
                yield
            o_dn = tmp([128, H, 128], tag="o_all", n=2)
            def dn_pre(h):
                P.tag = tg0 + ".dn_pre"
                kg = tmp([128, 128], BF16, tag="kg", n=2)
                kdec = tmp([128, 128], BF16, tag="kdec", n=2)
                act(kg[:C, :], dk_tm[:C, h, :], AF.Copy, [dk_tm, gt], [kg], scale=gt[:C, 20 + h:21 + h])
                act(kdec[:C, :], dk_tm[:C, h, :], AF.Copy, [dk_tm, gt], [kdec], scale=gt[:C, 24 + h:25 + h])
                Lg = tmp([128, 128], tag="Lg", n=2)
                ts("dve", Lg[:C, :C], Lc[:C, :C], gt[:C, 16 + h:17 + h], ALU.mult, [Lc, gt], [Lg])
                pe_ = pm()
                mm(pe_[:C, 0:C], Lg[:C, :C], Uc[:C, :C], [Lg, Uc], [pe_], start=True, stop=False)
                mm(pe_[:C, 0:C], identb[:C, :C], maskbT[:C, :C], [identb, maskbT], [pe_], start=False, stop=True)
                DTt = tmp([128, 128], tag="DTt", n=2)
                act(DTt[:C, :C], pe_[:C, 0:C], AF.Exp, [pe_], [DTt])
                DTs = tmp([128, 128], tag="DTs", n=2)
                tt("dve", DTs[:C, :C], DTt[:C, :C], strictT[:C, :C], ALU.mult, [DTt, strictT], [DTs])
                pg = pm()
                mm(pg[:C, 0:C], dkT[:, h, cs], dkT[:, h, cs], [dkT], [pg])
                mm(pg[:C, 128:128 + C], dkT[:, h, cs], dqT[:, h, cs], [dkT, dqT], [pg])
                NM = tmp([128, 2, 128], F32, tag="NM", n=4)
                stt(NM[:C, 0, :C], pg[:C, 0:C], gt[:C, h:h + 1], DTs[:C, :C], ALU.mult, ALU.mult, [pg, gt, DTs], [NM])
                PTt = tmp([128, 128], BF16, tag="PTt", n=2)
                tt("dve", PTt[:C, :C], pg[:C, 128:128 + C], DTt[:C, :C], ALU.mult, [pg, DTt], [PTt])
                X = tmp([128, 128], F32, tag="X", n=4)
                tt("dve", X[:C, :C], identf[:C, :C], NM[:C, 0, :C], ALU.subtract, [identf, NM], [X])
                pt = pm()
                tr(pt[:C, 0:C], NM[:C, 0, :C], identf[:C, :C], [NM, identf], [pt])
                cp("act", NM[:C, 1, :C], pt[:C, 0:C], [pt], [NM])
                return dict(kg=kg, kdec=kdec, PTt=PTt, X=X, NM=NM)

            def dn_level(st, lastlev):
                P.tag = tg0 + ".dn_lev"
                X, NM = st["X"], st["NM"]
                pa = pm()
                if not lastlev:
                    mm(pa[:C, 0:C], NM[:C, 1, :C], NM[:C, 0, :C], [NM], [pa])
                mm(pa[:C, 128:128 + C], NM[:C, 0, :C], NM[:C, 1, :C], [NM], [pa])
                NM2 = tmp([128, 2, 128], F32, tag="NM", n=4)
                if lastlev:
                    cp("act", NM2[:C, 1, :C], pa[:C, 128:128 + C], [pa], [NM2])
                else:
                    cp("act", NM2[:C, :, :C], pa[:C, 0:256].rearrange("p (a c) -> p a c", a=2)[:, :, 0:C], [pa], [NM2])
                pb = pm()
                mm(pb[:C, 0:C], NM2[:C, 1, :C], X[:C, :C], [NM2, X], [pb])
                if lastlev:
                    X2 = tmp([128, 128], BF16, tag="Xb", n=2)
                else:
                    X2 = tmp([128, 128], F32, tag="X", n=4)
                tt("dve", X2[:C, :C], pb[:C, 0:C], X[:C, :C], ALU.add, [pb, X], [X2])
                X, NM = X2, NM2
                st["X"], st["NM"] = X, NM

            def dn_post(h, kg, kdec, PTt, X):
                P.tag = tg0 + ".dn_post"
                pw = pm()
                mm(pw[:, 0:C], kg[:C, :], X[:C, :C], [kg, X], [pw])
                nwT = tmp([128, 128], BF16, tag="nwT", n=2)
                act(nwT[:, 0:C], pw[:, 0:C], AF.Copy, [pw], [nwT], scale=-1.0)
                yield
                if sample:
                    i2 = h % 2
                    P.dma("sp", S0[i2][:, :, 0:128], sS[l, :, h].rearrange("b k v -> k b v"), writes=[S0[i2]])
                    cp("act", S0b[i2][:, :, 0:128], S0[i2][:, :, 0:128], [S0[i2]], [S0b[i2]])
                    nwTm = masked_cols(nwT[:, 0:C], [nwT])
                    qTm = masked_cols(dqT[:, h, cs], [dqT])
                    kdm = masked_rows(kdec[:C, :], 128, [kdec])
                    st_r = [S0b[i2]]
                    S_b = lambda b: S0b[i2][:, b, 0:128]
                    nw_b = lambda b: nwTm[:, b, :]
                    q_b = lambda b: qTm[:, b, :]
                    kd_b = lambda b: kdm[:C, b, 0:128]
                    st_extra = [nwTm, qTm, kdm]
                else:
                    st_r = [Sdnb]
                    S_b = lambda b: Sdnb[:, h, :]
                    nw_b = lambda b: nwT[:, 0:C]
                    q_b = lambda b: dqT[:, h, cs]
                    kd_b = lambda b: kdec[:C, :]
                    st_extra = [nwT, dqT, kdec]
                pvn = pm()
                mm(pvn[:C, 0:128], X[:C, :C], dv_tm[:C, h, :], [X, dv_tm], [pvn], start=True, stop=False)
                for b in range(nblk):
                    mm(pvn[:C, 0:128], nw_b(b), S_b(b), st_r + st_extra, [pvn], start=False, stop=(b == nblk - 1))
                vnew = tmp([128, 128], BF16, tag="vnew", n=2)
                act(vnew[:C, :], pvn[:C, 0:128], AF.Copy, [pvn, gt], [vnew], scale=gt[:C, h:h + 1])
                yield
                pz = pm()
                for b in range(nblk):
                    mm(pz[:C, 0:128], q_b(b), S_b(b), st_r + st_extra, [pz], start=(b == 0), stop=(b == nblk - 1))
                pi_ = pm()
                mm(pi_[:C, 0:128], PTt[:C, :C], vnew[:C, :], [PTt, vnew], [pi_])
                isb = tmp([128, 128], tag="isb", n=2)
                cp("act", isb[:C, :], pi_[:C, 0:128], [pi_], [isb])
                stt(o_dn[:C, h, :], pz[:C, 0:128], gt[:C, 20 + h:21 + h], isb[:C, :], ALU.mult, ALU.add, [pz, gt, isb], [o_dn])
                yield
                if sample:
                    for b4 in range(0, nblk, 4):
                        pu = pm()
                        for bb in range(4):
                            b = b4 + bb
                            mm(pu[:, bb * 128:(bb + 1) * 128], kd_b(b), vnew[:C, :], st_extra + [vnew], [pu])
                        for bb in range(4):
                            b = b4 + bb
                            stt(S1[i2][:, b, :], S0[i2][:, b, 0:128], egt[:, b * 4 + h:b * 4 + h + 1],
                                pu[:, bb * 128:(bb + 1) * 128], ALU.mult, ALU.add, [S0[i2], egt, pu], [S1[i2]])
                    P.dma("pool", o_sS[l, :, h].rearrange("b k v -> k b v"), S1[i2][:, :, :], reads=[S1[i2]], final=True)
                else:
                    pu = pm()
                    mm(pu[:, 0:128], kdec[:C, :], vnew[:C, :], [kdec, vnew], [pu])
                    stt(Sdn[:, h, :], Sdn[:, h, :], egt[:, h:h + 1], pu[:, 0:128], ALU.mult, ALU.add, [Sdn, egt, pu], [Sdn])
                    cp("act", Sdnb[:, h, :], Sdn[:, h, :], [Sdn], [Sdnb])

            nlev = 1 if sample else 6
            for hp in range(0, H, 2):
                sts = []
                for hh in (hp, hp + 1):
                    sts.append(dn_pre(hh))
                    yield
                for lev in range(nlev):
                    for st in sts:
                        dn_level(st, lev == nlev - 1)
                        yield
                for hh, st in zip((hp, hp + 1), sts):
                    yield from dn_post(hh, st["kg"], st["kdec"], st["PTt"], st["X"])
                    yield
            if last_chunk:
                P.dma("pool", o_pS[l].rearrange("h k v -> k h v"), Sdn[:], reads=[Sdn], final=True)
            P.tag = tg0 + ".dn_out"
            branch_out(o_dn[:C], False, C, zs[j], oT[0], j * 128, [o_dn])


        def gen_ret():
            P.tag = tg0 + ".ret"
            rqT = tmp([128, H, 128], BF16, tag="rqT", n=1)
            rkT = tmp([128, H, 128], BF16, tag="rkT", n=1)
            for (src, dst) in ((rq_tm[j], rqT), (rk_tm[j], rkT)):
                pt = pm()
                pv = psum_bf(pt)
                for h in range(H):
                    tr(pv[:, h * C:(h + 1) * C], src[:C, h, :], identb[:C, :C], [src, identb], [pt])
                cp("act", dst[:, :, 0:C], pv[:, 0:H * C].rearrange("p (h c) -> p h c", h=H), [pt], [dst])
                yield
            pp = pm()
            for h in range(H):
                mm(pp[:C, h * C:(h + 1) * C], rkT[:, h, 0:C], rqT[:, h, 0:C], [rkT, rqT], [pp])
            PTm = tmp([128, H, 128], BF16, tag="PTm", n=1)
            tt("dve", PTm[:C, :, 0:C], pp[:C, 0:H * C].rearrange("p (h c) -> p h c", h=H),
               cmT[:C, :C].unsqueeze(1).broadcast_to([C, H, C]), ALU.mult, [pp, cmT], [PTm])
            yield
            po = PACC[0]
            for h in range(H):
                gam = 1.0 - 2.0 ** (-5.0 - h)
                if sample:
                    i2 = h % 2
                    P.dma("sp", S0[i2][:, :, 0:128], sR[l, :, h].rearrange("b k v -> k b v"), writes=[S0[i2]])
                    cp("act", S0b[i2][:, :, 0:128], S0[i2][:, :, 0:128], [S0[i2]], [S0b[i2]])
                    qTm = masked_cols(rqT[:, h, 0:C], [rqT])
                    kdm = masked_rows(rk_tm[j][:C, h, :], 128, [rk_tm[j]])
                mm(po[:C, h * 128:(h + 1) * 128], PTm[:C, h, 0:C], rv_tm[j][:C, h, :], [PTm, rv_tm[j]], [po], start=True, stop=False)
                for b in range(nblk):
                    if sample:
                        mm(po[:C, h * 128:(h + 1) * 128], qTm[:, b, :], S0b[i2][:, b, 0:128], [qTm, S0b[i2]], [po],
                           start=False, stop=(b == nblk - 1))
                    else:
                        mm(po[:C, h * 128:(h + 1) * 128], rqT[:, h, 0:C], Rrtb[:, h, :], [rqT, Rrtb], [po], start=False, stop=True)
                if sample:
                    gC = gam ** ST
                    for b4 in range(0, nblk, 4):
                        pu = pm()
                        for bb in range(4):
                            mm(pu[:, bb * 128:(bb + 1) * 128], kdm[:C, b4 + bb, 0:128], rv_tm[j][:C, h, :], [kdm, rv_tm[j]], [pu])
                        tq = tmp([128, 512], tag="rtmp", n=1)
                        tt("dve", tq[:, :].rearrange("p (b v) -> p b v", b=4), pu[:, 0:512].rearrange("p (b v) -> p b v", b=4),
                           S0[i2][:, b4:b4 + 4, 0:128], ALU.add, [pu, S0[i2]], [tq])
                        act(S1[i2][:, b4:b4 + 4, :], tq[:, :].rearrange("p (b v) -> p b v", b=4), AF.Copy, [tq], [S1[i2]], scale=gC)
                    P.dma("pool", o_sR[l, :, h].rearrange("b k v -> k b v"), S1[i2][:, :, :], reads=[S1[i2]], final=True)
            branch_out(po[:C, 0:512].rearrange("p (h v) -> p h v", h=H), True, C, gsl[j], oT[1], j * 128, [po])
            yield
            if not sample:
                pu = pm()
                for h in range(H):
                    mm(pu[:, h * 128:(h + 1) * 128], rk_tm[j][:C, h, :], rv_tm[j][:C, h, :], [rk_tm[j], rv_tm[j]], [pu])
                tq = tmp([128, 512], tag="rtmp", n=1)
                tt("dve", tq[:, :], pu[:, 0:512], Rrt[:].rearrange("p h v -> p (h v)"), ALU.add, [pu, Rrt], [tq])
                for h in range(H):
                    gC = (1.0 - 2.0 ** (-5.0 - h)) ** C
                    act(Rrt[:, h, :], tq[:, h * 128:(h + 1) * 128], AF.Copy, [tq], [Rrt], scale=gC)
                cp("dve", Rrtb[:], Rrt[:], [Rrt], [Rrtb])
                if last_chunk:
                    P.dma("pool", o_pR[l].rearrange("h k v -> k h v"), Rrt[:], reads=[Rrt], final=True)


        def gen_ml():
            P.tag = tg0 + ".ml_gate"
            g2 = tmp([128, 64], tag="g2", n=2)
            if sample:
                for t4 in range(ST):
                    P.dma("sp", msp[t4::ST, :], sM[l], writes=[msp])
                mprev = msp
            else:
                mprev = mcar
            tt("dve", g2[:C, 0:4], sm[:C, 8:12], rowc[:C, 8:12], ALU.add, [sm, rowc], [g2])
            tt("dve", g2[:C, 4:8], sm[:C, 12:16], rowc[:C, 12:16], ALU.add, [sm, rowc], [g2])
            act(g2[:C, 8:12], g2[:C, 4:8], AF.Exp, [g2], [g2], scale=-1.0)
            act(g2[:C, 12:16], g2[:C, 8:12], AF.Ln, [g2], [g2], bias=1.0)
            SUB = int(os.environ.get("KDBG_SUB", "99"))
            if SUB <= 1:
                return
            p1 = pm()
            mm(p1[:C, 0:4], Uc[:C, :C], g2[:C, 12:16], [Uc, g2], [p1])
            act(g2[:C, 16:20], p1[:C, 0:4], AF.Copy, [p1], [g2], scale=-1.0)
            if SUB <= 2:
                return
            tt("dve", g2[:C, 20:24], g2[:C, 0:4], g2[:C, 16:20], ALU.subtract, [g2], [g2])
            yield
            pls = []
            for h in range(H):
                dA = tmp([128, 128], tag="dA", n=2)
                ts("dve", dA[:C, :C], identf[:C, :C], g2[:C, 20 + h:21 + h], ALU.mult, [identf, g2], [dA])
                if SUB <= 3:
                    continue
                pl = PS[4 + h] if False else pm()
                mm(pl[:C, 0:C], onesf[:C, :C], dA[:C, :C], [onesf, dA], [pl], start=True, stop=False)
                mm(pl[:C, 0:C], identb[:C, :C], maskb[:C, :C], [identb, maskb], [pl], start=False, stop=True)
                if SUB <= 4:
                    continue
                P.op("dve", (lambda pl, h: lambda e: e.tensor_reduce(out=g2[:C, 24 + h:25 + h], in_=pl[:C, 0:C], axis=AX.X, op=ALU.max))(pl, h),
                     reads=[pl], writes=[g2])
                if SUB <= 5:
                    continue
                Lat = tmp([128, 128], tag="Lat", n=4)
                cp(os.environ.get("KDBG_LATENG", "act"), Lat[:C, :C], pl[:C, 0:C], [pl], [Lat])
                pls.append(Lat)
                yield
            if DBG_ML <= 1:
                return
            tt("dve", g2[:C, 28:32], g2[:C, 24:28], mprev[:C, 0:4], ALU.max, [g2, mprev], [g2])
            ts("dve", g2[:C, 32:36], g2[:C, 28:32], -1.0, ALU.mult, [g2], [g2])
            tt("dve", g2[:C, 36:40], mprev[:C, 0:4], g2[:C, 28:32], ALU.subtract, [g2, mprev], [g2])
            act(g2[:C, 36:40], g2[:C, 36:40], AF.Exp, [g2], [g2])
            tt("dve", g2[:C, 40:44], g2[:C, 16:20], g2[:C, 28:32], ALU.add, [g2], [g2])
            act(g2[:C, 44:48], g2[:C, 40:44], AF.Exp, [g2], [g2], scale=-1.0)
            p2 = pm()
            mm(p2[:C, 0:4], lastsel[:C, :C], g2[:C, 32:36], [lastsel, g2], [p2])
            tt("dve", g2[:C, 48:52], p2[:C, 0:4], g2[:C, 20:24], ALU.add, [p2, g2], [g2])
            act(g2[:C, 48:52], g2[:C, 48:52], AF.Exp, [g2], [g2])
            tt("dve", g2[:C, 52:56], mprev[:C, 0:4], g2[:C, 32:36], ALU.add, [g2, mprev], [g2])
            xb = tmp([128, 2, NSB * 4], tag="xb", n=2)
            tt("dve", xb[:C, 0, 0:nb4].rearrange("p (b h) -> p b h", h=H),
               g2[:C, 52:56].unsqueeze(1).broadcast_to([C, nblk, H]),
               lastind[:C, 0:nblk].unsqueeze(2).broadcast_to([C, nblk, H]), ALU.mult, [g2, lastind], [xb])
            tt("dve", xb[:C, 1, 0:nb4].rearrange("p (b h) -> p b h", h=H),
               g2[:C, 40:44].unsqueeze(1).broadcast_to([C, nblk, H]),
               lastind[:C, 0:nblk].unsqueeze(2).broadcast_to([C, nblk, H]), ALU.mult, [g2, lastind], [xb])
            mm(p2[:, 64:64 + nb4], onesf[:C, :], xb[:C, 0, 0:nb4], [onesf, xb], [p2])
            mm(p2[:, 128:128 + nb4], onesf[:C, :], xb[:C, 1, 0:nb4], [onesf, xb], [p2])
            sdec = tmp([128, NSB * 4], tag="sdec", n=2)
            act(sdec[:, 0:nb4], p2[:, 64:64 + nb4], AF.Exp, [p2], [sdec])
            mnew = tmp([128, NSB * 4], tag="mnew", n=2)
            cp("act", mnew[:, 0:nb4], p2[:, 128:128 + nb4], [p2], [mnew])
            yield
            if DBG_ML <= 2:
                return
            mk_tm = tmp([128, H, 128], BF16, tag="mk_tm", n=1)
            pt = pm()
            pv = psum_bf(pt)
            for h in range(H):
                tr(pv[:C, h * 128:(h + 1) * 128], mkT[:, h, cs], identb[:, :], [mkT, identb], [pt])
            cp("act", mk_tm[:C].rearrange("p h v -> p (h v)"), pv[:C, 0:512], [pt], [mk_tm])
            yield
            P.tag = tg0 + ".ml_head"
            hall = tmp([128, H, 128], tag="o_all", n=2)
            rs = tmp([128, 16], tag="rs", n=2)
            nt_s = tmp([128, NSB * 4], tag="nt_s", n=1)
            for h in range(H):
                E = tmp([128, 128], tag="E", n=2)
                act(E[:C, :C], pls[h][:C, :C], AF.Exp, [pls[h], g2], [E], bias=g2[:C, 32 + h:33 + h])
                pq = pm()
                mm(pq[:C, 0:C], mqT[:, h, cs], mkT[:, h, cs], [mqT, mkT], [pq])
                Dm = tmp([128, 128], BF16, tag="Dm", n=2)
                stt(Dm[:C, :C], pq[:C, 0:C], 1.0, E[:C, :C], ALU.mult, ALU.mult, [pq, E], [Dm, rs], accum=rs[:C, h:h + 1])
                yield
                pt = pm()
                pv = psum_bf(pt)
                tr(pv[:C, 0:C], Dm[:C, :C], identb[:C, :C], [Dm, identb], [pt])
                DmT = tmp([128, 128], BF16, tag="DmT", n=2)
                cp("act", DmT[:C, :C], pv[:C, 0:C], [pt], [DmT])
                yield
                if DBG_ML <= 3:
                    continue
                kw = tmp([128, 128], BF16, tag="kw", n=2)
                act(kw[:C, :], mk_tm[:C, h, :], AF.Copy, [mk_tm, g2], [kw], scale=g2[:C, 48 + h:49 + h])
                if sample:
                    i2 = h % 2
                    P.dma("sp", S0[i2][:, :, 0:128], sC[l, :, h].rearrange("b k v -> k b v"), writes=[S0[i2]])
                    P.dma("sp", S0[i2][:, :, 128], sN[l, :, h, :].rearrange("b k -> k b"), writes=[S0[i2]],
                          allow_slow_non_contiguous=True)
                    cp("act", S0b[i2][:], S0[i2][:], [S0[i2]], [S0b[i2]])
                    qTm = masked_cols(mqT[:, h, cs], [mqT])
                    kwm = masked_rows(kw[:C, :], 128, [kw])
                pi_ = pm()
                mm(pi_[:C, 0:128], DmT[:C, :C], vaug[j][:C, h, 0:128], [DmT, vaug[j]], [pi_])
                pz = pm()
                for b in range(nblk):
                    if sample:
                        mm(pz[:C, 0:129], qTm[:, b, :], S0b[i2][:, b, :], [qTm, S0b[i2]], [pz], start=(b == 0), stop=(b == nblk - 1))
                    else:
                        mm(pz[:C, 0:129], mqT[:, h, cs], Cmlb[:, h, :], [mqT, Cmlb], [pz])
                isb = tmp([128, 128], tag="isb", n=2)
                cp("act", isb[:C, :], pi_[:C, 0:128], [pi_], [isb])
                numt = tmp([128, 128], tag="numt", n=2)
                stt(numt[:C, :], pz[:C, 0:128], g2[:C, 36 + h:37 + h], isb[:C, :], ALU.mult, ALU.add, [pz, g2, isb], [numt])
                stt(rs[:C, 4 + h:5 + h], pz[:C, 128:129], g2[:C, 36 + h:37 + h], rs[:C, h:h + 1], ALU.mult, ALU.add,
                    [pz, g2, rs], [rs])
                act(rs[:C, 8 + h:9 + h], rs[:C, 4 + h:5 + h], AF.Abs, [rs], [rs])
                tt("dve", rs[:C, 8 + h:9 + h], rs[:C, 8 + h:9 + h], g2[:C, 44 + h:45 + h], ALU.max, [rs, g2], [rs])
                P.op("dve", (lambda h: lambda e: e.reciprocal(out=rs[:C, 12 + h:13 + h], in_=rs[:C, 8 + h:9 + h]))(h),
                     reads=[rs], writes=[rs])
                ts("dve", hall[:C, h, :], numt[:C, :], rs[:C, 12 + h:13 + h], ALU.mult, [numt, rs], [hall])
                yield
                if DBG_ML <= 4:
                    continue
                if sample:
                    for b in range(nblk):
                        pu = pm()
                        mm(pu[:, 0:129], kwm[:C, b, 0:128], vaug[j][:C, h, :], [kwm, vaug[j]], [pu])
                        stt(S1[i2][:, b, :], S0[i2][:, b, 0:128], sdec[:, b * 4 + h:b * 4 + h + 1], pu[:, 0:128], ALU.mult, ALU.add,
                            [S0[i2], sdec, pu], [S1[i2]])
                        stt(nt_s[:, b * 4 + h:b * 4 + h + 1], S0[i2][:, b, 128:129], sdec[:, b * 4 + h:b * 4 + h + 1],
                            pu[:, 128:129], ALU.mult, ALU.add, [S0[i2], sdec, pu], [nt_s])
                    P.dma("pool", o_sC[l, :, h].rearrange("b k v -> k b v"), S1[i2][:, :, :], reads=[S1[i2]], final=True)
                else:
                    pu = pm()
                    mm(pu[:, 0:129], kw[:C, :], vaug[j][:C, h, :], [kw, vaug[j]], [pu])
                    stt(Cml[:, h, :], Cml[:, h, :], sdec[:, h:h + 1], pu[:, 0:129], ALU.mult, ALU.add, [Cml, sdec, pu], [Cml])
                    cp("act", Cmlb[:, h, :], Cml[:, h, :], [Cml], [Cmlb])
                    yield
            if sample:
                pt = pm()
                tr(pt[:64, 0:128], nt_s[:, 0:64], identf[:, :], [nt_s, identf], [pt])
                nto = tmp([64, 128], tag="nto", n=1)
                cp("dve", nto[:, :], pt[:64, 0:128], [pt], [nto])
                P.dma("pool", o_sN[l], nto[:, :], reads=[nto], final=True)
                P.dma("pool", o_sM[l:l + 1, :], mnew[0:1, 0:64], reads=[mnew], final=True)
            else:
                cp("dve", mcar[:, :], mnew[:, 0:4], [mnew], [mcar])
                if last_chunk:
                    P.dma("pool", o_pC[l].rearrange("h k v -> k h v"), Cml[:, :, 0:128], reads=[Cml], final=True)
                    nt_p = tmp([128, 4], tag="nt_p", n=1)
                    cp("pool", nt_p[:, :], Cml[:, :, 128], [Cml], [nt_p])
                    pt = pm()
                    tr(pt[:4, 0:128], nt_p[:, 0:4], identf[:, :], [nt_p, identf], [pt])
                    nto = tmp([64, 128], tag="nto", n=1)
                    cp("dve", nto[:4, :], pt[:4, 0:128], [pt], [nto])
                    P.dma("pool", o_pN[l], nto[:4, :], reads=[nto], final=True)
                    P.dma("pool", o_pM[l:l + 1, :], mnew[0:1, 0:4], reads=[mnew], final=True)
            branch_out(hall[:C], False, C, osg[j], oT[2], j * 128, [hall])


        gens = [gen_dn()]
        if DBG_MIX > 2:
            gens.append(gen_ret())
        if DBG_MIX > 3:
            gens.append(gen_ml())
        if sample or os.environ.get("KDBG_NOILV"):
            for g in gens:
                for _ in g:
                    pass
        else:
            live = list(gens)
            dn_g = gens[0]
            while live:
                for g in list(live):
                    try:
                        next(g)
                        if g is dn_g and len(live) > 1:
                            next(g)
                    except StopIteration:
                        live.remove(g)

    for l in range(DEPTH):
        for sc in range(NPASS + 1):
            if npass_done[0] >= DBG_NP:
                break
            if DBG_SAMPLE and sc != NPASS:
                continue
            layer_pass(l, sc)
            npass_done[0] += 1
    if os.environ.get("KDBG_DUMP"):
        import json
        json.dump(P.tags, open(os.environ["KDBG_DUMP"], "w"))
    P.build()
    return nc


_CACHE = {}
_REG = {}


def kernel(x_prompt, x_sample, state_dn_conv, state_dn_S, state_ret_R, state_ml_C, state_ml_n, state_ml_m,
           w_in, dn_conv_w, dn_A_log, dn_dt_bias, dn_norm_w, ml_i_bias, ml_f_bias, ml_norm_w,
           w_br_a, w_br_b, w_br_c, w_out, ln1_g, ln1_b, w_ff1, w_ff2, ln2_g, ln2_b):
    f = lambda a: np.ascontiguousarray(np.asarray(a, dtype=np.float32))
    consts = _host_consts()
    if "nc" not in _CACHE:
        _CACHE["nc"] = build_program(consts)
    nc = _CACHE["nc"]
    shared = dict(w_in=f(w_in), conv_w=f(dn_conv_w), A_log=f(dn_A_log), dt_bias=f(dn_dt_bias), dn_nw=f(dn_norm_w),
                  ibias=f(ml_i_bias), fbias=f(ml_f_bias), ml_nw=f(ml_norm_w), w_br_a=f(w_br_a), w_br_b=f(w_br_b),
                  w_br_c=f(w_br_c), w_out=f(w_out), ln1g=f(ln1_g), ln1b=f(ln1_b), ln2g=f(ln2_g), ln2b=f(ln2_b),
                  w_ff1=f(w_ff1), w_ff2=f(w_ff2))
    for k, v in consts.items():
        shared["c_" + k] = f(v)
    in_maps = []
    for c in range(NCORES):
        b0 = c * NSB
        m = dict(shared)
        m["xp"] = f(x_prompt[c])
        m["xs"] = f(x_sample[b0:b0 + NSB]).reshape(NSB * ST, D)
        m["cconv"] = f(state_dn_conv[:, b0:b0 + NSB]).reshape(DEPTH, NSB * 3, 1536)
        m["sS"] = f(state_dn_S[:, b0:b0 + NSB])
        m["sR"] = f(state_ret_R[:, b0:b0 + NSB])
        m["sC"] = f(state_ml_C[:, b0:b0 + NSB])
        m["sN"] = f(state_ml_n[:, b0:b0 + NSB])
        m["sM"] = f(state_ml_m[:, b0:b0 + NSB])
        in_maps.append(m)
    ncr = int(os.environ.get("KDBG_CORES", str(NCORES)))
    res = run_bass_kernel_spmd(nc, in_maps[:ncr], core_ids=list(range(ncr)))
    R = list(res.results)
    while len(R) < NCORES:
        R.append(R[0])
    cat = lambda key, shp, ax: np.concatenate([np.asarray(R[c][key], dtype=np.float32).reshape(shp) for c in range(NCORES)], axis=ax)
    y_prompt = cat("y_p", (1, SEQ, D), 0)
    y_sample = cat("y_s", (NSB, ST, D), 0)
    p_conv = cat("o_pconv", (DEPTH, 1, 3, 1536), 1)
    p_S = cat("o_pS", (DEPTH, 1, H, 128, 128), 1)
    p_R = cat("o_pR", (DEPTH, 1, H, 128, 128), 1)
    p_C = cat("o_pC", (DEPTH, 1, H, 128, 128), 1)
    p_n = cat("o_pN", (DEPTH, 1, H, 128), 1)
    p_m = cat("o_pM", (DEPTH, 1, H), 1)
    s_conv = cat("o_sconv", (DEPTH, NSB, 3, 1536), 1)
    s_S = cat("o_sS", (DEPTH, NSB, H, 128, 128), 1)
    s_R = cat("o_sR", (DEPTH, NSB, H, 128, 128), 1)
    s_C = cat("o_sC", (DEPTH, NSB, H, 128, 128), 1)
    s_n = cat("o_sN", (DEPTH, NSB, H, 128), 1)
    s_m = cat("o_sM", (DEPTH, NSB, H), 1)
    return (y_prompt, y_sample, p_conv, p_S, p_R, p_C, p_n, p_m, s_conv, s_S, s_R, s_C, s_n, s_m)
```
